# Optimizing a Trainium2 kernel written in Bass

```python
import jax, jax.numpy as jnp
from jax import lax
import numpy as np

D_MODEL = 1024
BATCH = 8
SEQ = 2048
DEPTH = 1
DEC_BATCH = 16
DEC_SEQ = 16
PAST_LEN = 2048

CHUNK = 64
QBLOCK = 128
EPS = 1e-6
NEG_INF = -1e30

GDN_HEADS = 8
GDN_DK = 64
GDN_DV = 64
CONV_W = 4
GDN_QK_DIM = GDN_HEADS * GDN_DK
GDN_V_DIM = GDN_HEADS * GDN_DV
GDN_CONV_DIM = 2 * GDN_QK_DIM + GDN_V_DIM

MLA_HEADS = 8
QK_NOPE = 64
QK_ROPE = 32
QK_HEAD = QK_NOPE + QK_ROPE
V_HEAD = 64
Q_LORA = 384
KV_LORA = 256
ROPE_THETA = 10000.0
MLA_V_DIM = MLA_HEADS * V_HEAD

D_FF = 4 * D_MODEL
ADA_DIM = 6 * D_MODEL

IN_SIZES = (GDN_QK_DIM, GDN_QK_DIM, GDN_V_DIM, GDN_V_DIM, GDN_HEADS, GDN_HEADS,
            Q_LORA, KV_LORA, QK_ROPE, D_MODEL, D_MODEL)
IN_DIM = sum(IN_SIZES)

kernel_name = 'chunk_streaming_gdn_mla_hybrid_step'


def rmsnorm(x, g):
    xf = x.astype(jnp.float32)
    y = xf * lax.rsqrt(jnp.mean(xf * xf, axis=-1, keepdims=True) + EPS)
    return (y * g.astype(jnp.float32)).astype(x.dtype)


def l2norm(x):
    xf = x.astype(jnp.float32)
    return xf * lax.rsqrt(jnp.sum(xf * xf, axis=-1, keepdims=True) + EPS)


def split_cols(t, sizes):
    return jnp.split(t, np.cumsum(sizes)[:-1].tolist(), axis=-1)


def rope_tables(pos):
    inv = ROPE_THETA ** (-jnp.arange(0, QK_ROPE, 2, dtype=jnp.float32) / QK_ROPE)
    ang = pos.astype(jnp.float32)[:, None] * inv[None, :]
    return jnp.cos(ang), jnp.sin(ang)


def rope(x, cos, sin):
    x1, x2 = jnp.split(x.astype(jnp.float32), 2, axis=-1)
    return jnp.concatenate([x1 * cos - x2 * sin, x2 * cos + x1 * sin], axis=-1).astype(x.dtype)


def causal_conv_silu(x, buf, w):
    L = x.shape[1]
    xp = jnp.concatenate([buf.astype(x.dtype), x], axis=1)
    y = sum(xp[:, i:i + L, :] * w[i] for i in range(CONV_W))
    return jax.nn.silu(y), xp[:, -(CONV_W - 1):, :]


def to_blocks(t, n, chunk):
    B, L, H = t.shape[:3]
    t = t.reshape((B, n, chunk, H) + t.shape[3:])
    return jnp.moveaxis(t, (1, 3), (0, 2))


def gated_delta_rule(q, k, v, g, beta, state0, chunk):
    B, L, H, DK = q.shape
    DV = v.shape[-1]
    n = L // chunk
    qc, kc, vc = to_blocks(q, n, chunk), to_blocks(k, n, chunk), to_blocks(v, n, chunk)
    gc = jnp.cumsum(to_blocks(g, n, chunk), axis=-1)
    bc = to_blocks(beta, n, chunk)
    idx = jnp.arange(chunk)
    causal = idx[:, None] >= idx[None, :]
    strict = idx[:, None] > idx[None, :]
    gdiff = gc[..., :, None] - gc[..., None, :]
    decay = jnp.where(causal, jnp.exp(jnp.where(causal, gdiff, 0.0)), 0.0)
    kk = jnp.einsum('nbhid,nbhjd->nbhij', kc, kc)
    a_mat = jnp.where(strict, bc[..., :, None] * kk * decay, 0.0) + jnp.eye(chunk, dtype=kk.dtype)
    u = lax.linalg.triangular_solve(a_mat, vc * bc[..., None], left_side=True, lower=True,
                                    unit_diagonal=True)
    w = lax.linalg.triangular_solve(a_mat, kc * (bc * jnp.exp(gc))[..., None], left_side=True,
                                    lower=True, unit_diagonal=True)
    qk = jnp.einsum('nbhid,nbhjd->nbhij', qc, kc) * decay

    def step(s, inp):
        q_i, k_i, u_i, w_i, g_i, qk_i = inp
        v_new = u_i - jnp.einsum('bhck,bhkv->bhcv', w_i, s)
        o_i = (jnp.einsum('bhck,bhkv->bhcv', q_i * jnp.exp(g_i)[..., None], s)
               + jnp.einsum('bhij,bhjv->bhiv', qk_i, v_new))
        g_last = g_i[..., -1:]
        s = (s * jnp.exp(g_last)[..., None]
             + jnp.einsum('bhck,bhcv->bhkv', k_i * jnp.exp(g_last - g_i)[..., None], v_new))
        return s, o_i

    s_final, o = lax.scan(step, state0, (qc, kc, u, w, gc, qk))
    o = jnp.moveaxis(o, (0, 2), (1, 3)).reshape(B, L, H, DV)
    return o, s_final


def gdn_branch(q, k, v, z, a, b, state0, conv0, lw):
    B, L, _ = q.shape
    dt = q.dtype
    qkv, conv_new = causal_conv_silu(jnp.concatenate([q, k, v], axis=-1), conv0, lw['gdn_conv_w'])
    q, k, v = jnp.split(qkv, [GDN_QK_DIM, 2 * GDN_QK_DIM], axis=-1)
    q = l2norm(q.reshape(B, L, GDN_HEADS, GDN_DK)) * (GDN_DK ** -0.5)
    k = l2norm(k.reshape(B, L, GDN_HEADS, GDN_DK))
    v = v.reshape(B, L, GDN_HEADS, GDN_DV).astype(jnp.float32)
    beta = jax.nn.sigmoid(b.astype(jnp.float32))
    g = -jnp.exp(lw['gdn_a_log'].astype(jnp.float32)) * jax.nn.softplus(
        a.astype(jnp.float32) + lw['gdn_dt_bias'].astype(jnp.float32))
    o, state = gated_delta_rule(q, k, v, g, beta, state0.astype(jnp.float32), min(CHUNK, L))
    o = rmsnorm(o, lw['gdn_norm_g']) * jax.nn.silu(z.reshape(B, L, GDN_HEADS, GDN_DV).astype(jnp.float32))
    y = o.reshape(B, L, GDN_V_DIM).astype(dt) @ lw['w_gdn_out']
    return y, state.astype(state0.dtype), conv_new


def mla_queries_and_latents(c_q, c_kv, k_r, pos, lw):
    B, L, _ = c_q.shape
    cos, sin = rope_tables(pos)
    q = (rmsnorm(c_q, lw['mla_q_norm_g']) @ lw['w_uq']).reshape(B, L, MLA_HEADS, QK_HEAD)
    q = jnp.concatenate([q[..., :QK_NOPE], rope(q[..., QK_NOPE:], cos[:, None, :], sin[:, None, :])], axis=-1)
    q = rmsnorm(q, lw['q_head_norm_g'])
    ckv = rmsnorm(c_kv, lw['mla_kv_norm_g'])
    krope = rope(k_r, cos, sin)
    return q, ckv, krope


def mla_expand_kv(ckv, krope, lw):
    B, L, _ = ckv.shape
    kv = (ckv @ lw['w_ukv']).reshape(B, L, MLA_HEADS, QK_NOPE + V_HEAD)
    k_nope, v = kv[..., :QK_NOPE], kv[..., QK_NOPE:]
    k_pe = jnp.broadcast_to(krope[:, :, None, :], (B, L, MLA_HEADS, QK_ROPE))
    k = rmsnorm(jnp.concatenate([k_nope, k_pe.astype(k_nope.dtype)], axis=-1), lw['k_head_norm_g'])
    return k, v


def chunk_causal_attention(q, k, v):
    B, S, H, Dq = q.shape
    nb = S // QBLOCK
    qb = jnp.moveaxis(q.reshape(B, nb, QBLOCK, H, Dq), 1, 0)
    starts = jnp.arange(nb) * QBLOCK
    key_chunk = jnp.arange(S) // CHUNK

    def one_block(args):
        q_blk, start = args
        q_chunk = (start + jnp.arange(QBLOCK)) // CHUNK
        s = jnp.einsum('bqhd,bkhd->bhqk', q_blk, k).astype(jnp.float32) * (QK_HEAD ** -0.5)
        s = jnp.where(key_chunk[None, :] <= q_chunk[:, None], s, NEG_INF)
        p = jax.nn.softmax(s, axis=-1).astype(v.dtype)
        return jnp.einsum('bhqk,bkhd->bqhd', p, v)

    o = lax.map(one_block, (qb, starts))
    return jnp.moveaxis(o, 0, 1).reshape(B, S, H * v.shape[-1])


def full_attention(q, k, v):
    B, Lq, H, _ = q.shape
    s = jnp.einsum('bqhd,bkhd->bhqk', q, k).astype(jnp.float32) * (QK_HEAD ** -0.5)
    p = jax.nn.softmax(s, axis=-1).astype(v.dtype)
    return jnp.einsum('bhqk,bkhd->bqhd', p, v).reshape(B, Lq, H * v.shape[-1])


def trunk_layer(x, c, pos, gdn_state0, gdn_conv0, ckv_past, krope_past, lw):
    mod = (jax.nn.silu(c) @ lw['ada_w'] + lw['ada_b'])[:, None, :]
    shift1, scale1, gate1, shift2, scale2, gate2 = jnp.split(mod, 6, axis=-1)
    h = rmsnorm(x, lw['norm1_g']) * (1.0 + scale1) + shift1
    (q_a, k_a, v_a, z_a, a_a, b_a, c_q, c_kv, k_r, gl_a, gl_b) = split_cols(h @ lw['w_in'], IN_SIZES)
    y_a, gdn_state, gdn_conv = gdn_branch(q_a, k_a, v_a, z_a, a_a, b_a, gdn_state0, gdn_conv0, lw)
    q, ckv_new, krope_new = mla_queries_and_latents(c_q, c_kv, k_r, pos, lw)
    if ckv_past is None:
        k, v = mla_expand_kv(ckv_new, krope_new, lw)
        o_b = chunk_causal_attention(q, k, v)
    else:
        ckv_all = jnp.concatenate([ckv_past.astype(ckv_new.dtype), ckv_new], axis=1)
        krope_all = jnp.concatenate([krope_past.astype(krope_new.dtype), krope_new], axis=1)
        k, v = mla_expand_kv(ckv_all, krope_all, lw)
        o_b = full_attention(q, k, v)
    y_b = o_b @ lw['w_mla_out']
    merged = jax.nn.sigmoid(gl_a) * y_a + jax.nn.sigmoid(gl_b) * y_b
    x = x + gate1 * (merged @ lw['w_o'])
    h2 = rmsnorm(x, lw['norm2_g']) * (1.0 + scale2) + shift2
    x = x + gate2 * (jnp.square(jax.nn.relu(h2 @ lw['w_ff1'])) @ lw['w_ff2'])
    return x, ckv_new, krope_new, gdn_state, gdn_conv


def setup_inputs(seed: int = 0) -> dict:
    key = jax.random.key(seed)
    ks = jax.random.split(key, 28)
    f32 = jnp.float32

    def nrm(k, shape, scale=1.0):
        return jax.random.normal(k, shape, f32) * scale

    def gain(k, shape):
        return 1.0 + 0.02 * jax.random.normal(k, shape, f32)

    dt = jnp.exp(jax.random.uniform(ks[14], (DEPTH, GDN_HEADS), f32, np.log(1e-3), np.log(1e-1)))
    return {
        'x_prompt': nrm(ks[0], (BATCH, SEQ, D_MODEL)),
        'x_sample': nrm(ks[1], (DEC_BATCH, DEC_SEQ, D_MODEL)),
        'c_prompt': nrm(ks[2], (BATCH, D_MODEL)),
        'c_sample': nrm(ks[3], (DEC_BATCH, D_MODEL)),
        'cache_mla_ckv': nrm(ks[4], (DEPTH, DEC_BATCH, PAST_LEN, KV_LORA)),
        'cache_mla_krope': nrm(ks[5], (DEPTH, DEC_BATCH, PAST_LEN, QK_ROPE)),
        'state_gdn': nrm(ks[6], (DEPTH, DEC_BATCH, GDN_HEADS, GDN_DK, GDN_DV), 0.1),
        'state_gdn_conv': nrm(ks[7], (DEPTH, DEC_BATCH, CONV_W - 1, GDN_CONV_DIM)),
        'ada_w': nrm(ks[8], (DEPTH, D_MODEL, ADA_DIM), 0.5 * D_MODEL ** -0.5),
        'ada_b': nrm(ks[9], (DEPTH, ADA_DIM), 0.01),
        'norm1_g': gain(ks[10], (DEPTH, D_MODEL)),
        'w_in': nrm(ks[11], (DEPTH, D_MODEL, IN_DIM), D_MODEL ** -0.5),
        'gdn_conv_w': nrm(ks[12], (DEPTH, CONV_W, GDN_CONV_DIM), CONV_W ** -0.5),
        'gdn_a_log': jnp.log(jax.random.uniform(ks[13], (DEPTH, GDN_HEADS), f32, 1.0, 16.0)),
        'gdn_dt_bias': dt + jnp.log(-jnp.expm1(-dt)),
        'gdn_norm_g': gain(ks[15], (DEPTH, GDN_DV)),
        'w_gdn_out': nrm(ks[16], (DEPTH, GDN_V_DIM, D_MODEL), GDN_V_DIM ** -0.5),
        'mla_q_norm_g': gain(ks[17], (DEPTH, Q_LORA)),
        'w_uq': nrm(ks[18], (DEPTH, Q_LORA, MLA_HEADS * QK_HEAD), Q_LORA ** -0.5),
        'mla_kv_norm_g': gain(ks[19], (DEPTH, KV_LORA)),
        'w_ukv': nrm(ks[20], (DEPTH, KV_LORA, MLA_HEADS * (QK_NOPE + V_HEAD)), KV_LORA ** -0.5),
        'q_head_norm_g': gain(ks[21], (DEPTH, QK_HEAD)),
        'k_head_norm_g': gain(ks[22], (DEPTH, QK_HEAD)),
        'w_mla_out': nrm(ks[23], (DEPTH, MLA_V_DIM, D_MODEL), MLA_V_DIM ** -0.5),
        'w_o': nrm(ks[24], (DEPTH, D_MODEL, D_MODEL), D_MODEL ** -0.5),
        'norm2_g': gain(ks[25], (DEPTH, D_MODEL)),
        'w_ff1': nrm(ks[26], (DEPTH, D_MODEL, D_FF), D_MODEL ** -0.5),
        'w_ff2': nrm(ks[27], (DEPTH, D_FF, D_MODEL), D_FF ** -0.5),
    }


def reference(x_prompt, x_sample, c_prompt, c_sample, cache_mla_ckv, cache_mla_krope, state_gdn,
              state_gdn_conv, ada_w, ada_b, norm1_g, w_in, gdn_conv_w, gdn_a_log, gdn_dt_bias,
              gdn_norm_g, w_gdn_out, mla_q_norm_g, w_uq, mla_kv_norm_g, w_ukv, q_head_norm_g,
              k_head_norm_g, w_mla_out, w_o, norm2_g, w_ff1, w_ff2):
    b_p, s_p, _ = x_prompt.shape
    s_s = x_sample.shape[1]
    past_len = cache_mla_ckv.shape[2]
    pos_p = jnp.arange(s_p)
    pos_s = past_len + jnp.arange(s_s)
    zero_state = jnp.zeros((b_p, GDN_HEADS, GDN_DK, GDN_DV), x_prompt.dtype)
    zero_conv = jnp.zeros((b_p, CONV_W - 1, GDN_CONV_DIM), x_prompt.dtype)

    xp, xs = x_prompt, x_sample
    ckv_p, kr_p, st_p, cv_p = [], [], [], []
    ckv_s, kr_s, st_s, cv_s = [], [], [], []
    for layer in range(DEPTH):
        lw = {
            'ada_w': ada_w[layer], 'ada_b': ada_b[layer], 'norm1_g': norm1_g[layer],
            'w_in': w_in[layer], 'gdn_conv_w': gdn_conv_w[layer], 'gdn_a_log': gdn_a_log[layer],
            'gdn_dt_bias': gdn_dt_bias[layer], 'gdn_norm_g': gdn_norm_g[layer],
            'w_gdn_out': w_gdn_out[layer], 'mla_q_norm_g': mla_q_norm_g[layer], 'w_uq': w_uq[layer],
            'mla_kv_norm_g': mla_kv_norm_g[layer], 'w_ukv': w_ukv[layer],
            'q_head_norm_g': q_head_norm_g[layer], 'k_head_norm_g': k_head_norm_g[layer],
            'w_mla_out': w_mla_out[layer], 'w_o': w_o[layer], 'norm2_g': norm2_g[layer],
            'w_ff1': w_ff1[layer], 'w_ff2': w_ff2[layer],
        }
        xp, a1, a2, a3, a4 = trunk_layer(xp, c_prompt, pos_p, zero_state, zero_conv, None, None, lw)
        xs, b1, b2, b3, b4 = trunk_layer(xs, c_sample, pos_s, state_gdn[layer], state_gdn_conv[layer],
                                         cache_mla_ckv[layer], cache_mla_krope[layer], lw)
        ckv_p.append(a1); kr_p.append(a2); st_p.append(a3); cv_p.append(a4)
        ckv_s.append(b1); kr_s.append(b2); st_s.append(b3); cv_s.append(b4)

    return (xp, xs, jnp.stack(ckv_p), jnp.stack(kr_p), jnp.stack(st_p), jnp.stack(cv_p),
            jnp.stack(ckv_s), jnp.stack(kr_s), jnp.stack(st_s), jnp.stack(cv_s))
```

```python
import os
import numpy as np
import concourse.bass as bass
import concourse.mybir as mybir
from concourse.bass_utils import run_bass_kernel_spmd

F32 = mybir.dt.float32
BF16 = mybir.dt.bfloat16
AF = mybir.ActivationFunctionType
ALU = mybir.AluOpType
AX = mybir.AxisListType

ENGS = ['pe', 'act', 'dve', 'pool', 'sp']
ENGMAP = {'pe': 'tensor', 'act': 'scalar', 'dve': 'vector', 'pool': 'gpsimd', 'sp': 'sync'}

D = 1024
H = 8
IN_DIM = 4784
Q0, K0, V0, Z0, A0, B0, CQ0, CKV0, KR0, GA0, GB0 = 0, 512, 1024, 1536, 2048, 2056, 2064, 2448, 2704, 2736, 3760
EPS = 1e-6
NEG = -30000.0


class Buf:
    __slots__ = ('name', 'w', 'r')

    def __init__(self, name=''):
        self.name = name
        self.w = {}
        self.r = {}


class Tile:
    def __init__(self, h, name=''):
        self.h = h
        self.b = Buf(name)

    def __getitem__(self, k):
        return self.h[k]


class MK:
    def __init__(self, nc, n_dma_sems=40):
        self.nc = nc
        self.ops = {e: [] for e in ENGS}
        self.cnt = {e: 0 for e in ENGS}
        self.sem = {e: nc.alloc_semaphore('s_' + e) for e in ['pe', 'act', 'dve', 'pool']}
        self.dsem = [nc.alloc_semaphore('d%d' % i) for i in range(n_dma_sems)]
        self.dcnt = [0] * n_dma_sems
        self.dnext = 0
        self.dnext_pool = 0
        self.seen = {e: {} for e in ENGS}
        self.out_tokens = []

    def _semof(self, k):
        if isinstance(k, tuple):
            return self.dsem[k[1]]
        return self.sem[k]

    def _deps(self, eng, reads, writes):
        need = {}
        for t in reads:
            for k, v in t.b.w.items():
                if need.get(k, 0) < v:
                    need[k] = v
        strict = eng != 'pe' and os.environ.get('KSTRICT', '1') == '1'
        for t in writes:
            for k, v in t.b.w.items():
                if (strict or k != eng) and need.get(k, 0) < v:
                    need[k] = v
            for k, v in t.b.r.items():
                if (strict or k != eng) and need.get(k, 0) < v:
                    need[k] = v
        waits = []
        seen = self.seen[eng]
        for k, v in need.items():
            if seen.get(k, 0) >= v:
                continue
            seen[k] = v
            waits.append((k, v))
        return waits

    def _commit(self, tok, reads, writes):
        k, v = tok
        for t in reads:
            if t.b.r.get(k, 0) < v:
                t.b.r[k] = v
        for t in writes:
            if t.b.w.get(k, 0) < v:
                t.b.w[k] = v

    def op(self, eng, fn, R=(), W=()):
        waits = self._deps(eng, R, W)
        self.cnt[eng] += 1
        tok = (eng, self.cnt[eng])
        self._commit(tok, R, W)
        self.ops[eng].append((fn, waits, eng))
        return tok

    def dma(self, queue, out, in_, R=(), W=(), is_output=False, slow=False):
        if queue == 'pool':
            i = self.dnext_pool
            self.dnext_pool = (self.dnext_pool + 1) % 8
        else:
            i = 8 + self.dnext
            self.dnext = (self.dnext + 1) % (len(self.dsem) - 8)
        waits = self._deps(queue, R, W)
        key = ('d', i)
        prev = self.dcnt[i]
        if prev > 0 and self.seen[queue].get(key, 0) < prev:
            self.seen[queue][key] = prev
            waits.append((key, prev))
        self.dcnt[i] += 16
        tok = (key, self.dcnt[i])
        self._commit(tok, R, W)
        if slow:
            fn = lambda e: e.dma_start(out=out, in_=in_, allow_slow_non_contiguous=True)
        else:
            fn = lambda e: e.dma_start(out=out, in_=in_)
        self.ops[queue].append((fn, waits, key))
        if is_output:
            self.out_tokens.append(tok)
        return tok

    def barrier(self):
        allk = [(e, self.cnt[e]) for e in ['pe', 'act', 'dve', 'pool'] if self.cnt[e] > 0]
        allk += [(('d', i), self.dcnt[i]) for i in range(len(self.dsem)) if self.dcnt[i] > 0]
        for e in ENGS:
            waits = []
            for k, v in allk:
                if k == e:
                    continue
                if self.seen[e].get(k, 0) < v:
                    self.seen[e][k] = v
                    waits.append((k, v))
            if waits:
                self.ops[e].append((None, waits, None))

    def mm(self, out, lhsT, rhs, start, stop, R, W):
        return self.op('pe', lambda e: e.matmul(out, lhsT=lhsT, rhs=rhs, start=start, stop=stop), R, W)

    def tr(self, out, in_, ident, R, W):
        return self.op('pe', lambda e: e.transpose(out, in_, ident), R, W)

    def act(self, out, in_, func, R, W, scale=1.0, bias=0.0, accum=None):
        if accum is None:
            return self.op('act', lambda e: e.activation(out=out, in_=in_, func=func, bias=bias, scale=scale), R, W)
        return self.op('act', lambda e: e.activation(out=out, in_=in_, func=func, bias=bias, scale=scale, accum_out=accum), R, W)

    def tt(self, eng, out, in0, in1, op, R, W):
        return self.op(eng, lambda e: e.tensor_tensor(out=out, in0=in0, in1=in1, op=op), R, W)

    def ts(self, eng, out, in0, s1, op0, R, W, s2=None, op1=None):
        if eng == 'pool':
            eng = 'dve'
        if op1 is None:
            return self.op(eng, lambda e: e.tensor_scalar(out=out, in0=in0, scalar1=s1, scalar2=None, op0=op0), R, W)
        return self.op(eng, lambda e: e.tensor_scalar(out=out, in0=in0, scalar1=s1, scalar2=s2, op0=op0, op1=op1), R, W)

    def stt(self, eng, out, in0, scalar, in1, op0, op1, R, W):
        if eng == 'pool':
            eng = 'dve'
        if not hasattr(scalar, 'shape'):
            j = self.cvals.index(float(scalar))
            bp = int(out.base_partition())
            n = int(out.shape[0])
            scalar = self.cst[bp:bp + n, j:j + 1]
            R = list(R) + [self.cst]
        return self.op(eng, lambda e: e.scalar_tensor_tensor(out=out, in0=in0, scalar=scalar, in1=in1, op0=op0, op1=op1), R, W)

    def copy(self, eng, out, in_, R, W):
        if eng == 'act':
            return self.act(out, in_, AF.Copy, R, W)
        return self.op(eng, lambda e: e.tensor_copy(out=out, in_=in_), R, W)

    def memset(self, eng, ap, val, W):
        return self.op(eng, lambda e: e.memset(ap, val), (), W)

    def red(self, eng, out, in_, op, R, W):
        return self.op(eng, lambda e: e.tensor_reduce(out=out, in_=in_, axis=AX.X, op=op), R, W)

    def finish(self):
        need = {}
        for k, v in self.out_tokens:
            if need.get(k, 0) < v:
                need[k] = v
        final_waits = list(need.items())
        nc = self.nc
        with nc.Block() as block:
            for e in ENGS:
                def body(eng, e=e):
                    for fn, waits, inc in self.ops[e]:
                        for (k, v) in waits:
                            eng.wait_ge(self._semof(k), v)
                        if fn is None:
                            continue
                        ins = fn(eng)
                        if isinstance(inc, tuple):
                            ins.then_inc(self.dsem[inc[1]], 16)
                        else:
                            ins.then_inc(self.sem[inc], 1)
                    if e == 'sp':
                        for (k, v) in final_waits:
                            eng.wait_ge(self._semof(k), v)
                getattr(block, ENGMAP[e])(body)


class Arena:
    def __init__(self, nc, base, limit):
        self.nc = nc
        self.base = base
        self.ptr = base
        self.limit = limit
        self.n = 0

    def alloc(self, name, shape, dtype):
        nbytes = int(np.prod(shape[1:])) * (2 if dtype == BF16 else 4)
        nbytes = (nbytes + 31) // 32 * 32
        off = self.ptr
        self.ptr += nbytes
        self.peak = max(getattr(self, 'peak', 0), self.ptr)
        assert self.ptr <= self.limit, "SBUF arena overflow %s: %d > %d" % (name, self.ptr, self.limit)
        self.n += 1
        h = self.nc.alloc_sbuf_tensor_at("%s_%d_%d" % (name, off, self.n), list(shape), dtype, offset=off)
        t = Tile(h, name)
        t.off = off
        return t

    def alias(self, name, shape, dtype, offset, share):
        self.n += 1
        h = self.nc.alloc_sbuf_tensor_at("%s_%d_%d" % (name, offset, self.n), list(shape), dtype, offset=offset)
        t = Tile(h, name)
        t.b = share.b
        return t

    def mark(self):
        return self.ptr

    def reset(self, mark=None):
        if os.environ.get('KARENA'):
            print('arena peak', getattr(self, 'peak', 0) - self.base, 'of', self.limit - self.base)
        if mark is None:
            self.peak = 0
        self.ptr = self.base if mark is None else mark


class Seq:
    pass


def build_program(debug=False):
    STAGE = os.environ.get('KSTAGE', 'all')
    SEQS = os.environ.get('KSEQS', '012')
    nc = bass.Bass("TRN2", target_bir_lowering=False)
    K = MK(nc)

    def din(name, shape, dt=F32):
        return Tile(nc.dram_tensor(name, list(shape), dt, kind="ExternalInput").ap(), name)

    def dout(name, shape, dt=F32):
        return Tile(nc.dram_tensor(name, list(shape), dt, kind="ExternalOutput").ap(), name)

    def dscr(name, shape, dt=F32):
        kind = "ExternalOutput" if debug else "Internal"
        return Tile(nc.dram_tensor(name, list(shape), dt, kind=kind).ap(), name)

    xp = din("xp", [2048, D])
    xs = din("xs", [2, 16, D])
    c3 = din("c3", [3, D])
    ckv_c = din("ckv_c", [2, 2048, 256])
    kr_c = din("kr_c", [2, 2048, 32])
    st_c = din("st_c", [2, H, 64, 64])
    cv_c = din("cv_c", [2, 3, 1536])
    ada_w = din("ada_w", [D, 6144])
    ada_b = din("ada_b", [1, 6144])
    norm1_g = din("norm1_g", [1, D])
    w_in = din("w_in", [D, IN_DIM])
    conv_w = din("conv_w", [4, 1536])
    a_log = din("a_log", [1, H])
    dt_bias = din("dt_bias", [1, H])
    gdn_norm_g = din("gdn_norm_g", [1, 64])
    w_gdn_out = din("w_gdn_out", [512, D])
    q_norm_g = din("q_norm_g", [1, 384])
    w_uq = din("w_uq", [384, 768])
    kv_norm_g = din("kv_norm_g", [1, 256])
    w_ukv = din("w_ukv", [256, 1024])
    qh_g = din("qh_g", [1, 96])
    kh_g = din("kh_g", [1, 96])
    w_mla_out = din("w_mla_out", [512, D])
    w_o = din("w_o", [D, D])
    norm2_g = din("norm2_g", [1, D])
    w_ff1 = din("w_ff1", [D, 4096])
    w_ff2 = din("w_ff2", [4096, D])
    rope_cs = din("rope_cs", [2064, 32])

    y_p = dout("y_p", [2048, D])
    y_s = dout("y_s", [2, 16, D])
    ckv_p = dout("ckv_p", [2048, 256])
    kr_p = dout("kr_p", [2048, 32])
    st_p = dout("st_p", [H, 64, 64])
    cv_p = dout("cv_p", [3, 1536])
    ckv_s = dout("ckv_s", [2, 16, 256])
    kr_s = dout("kr_s", [2, 16, 32])
    st_s = dout("st_s", [2, H, 64, 64])
    cv_s = dout("cv_s", [2, 3, 1536])

    mod_d = dscr("mod_d", [3, 6144])

    seqs = []
    for si in range(3):
        s = Seq()
        s.i = si
        s.L = 2048 if si == 0 else 16
        s.TT = 128 if si == 0 else 16
        s.nsub = 2 if si == 0 else 1
        s.C = s.TT // s.nsub
        s.TB = 512 if si == 0 else 16
        s.past = 0 if si == 0 else 2048
        s.x = xp.h if si == 0 else xs.h[si - 1]
        s.xt = xp if si == 0 else xs
        s.y = y_p.h if si == 0 else y_s.h[si - 1]
        s.yt = y_p if si == 0 else y_s
        s.ckv_o = ckv_p.h if si == 0 else ckv_s.h[si - 1]
        s.ckv_ot = ckv_p if si == 0 else ckv_s
        s.kr_o = kr_p.h if si == 0 else kr_s.h[si - 1]
        s.kr_ot = kr_p if si == 0 else kr_s
        s.st_o = st_p.h if si == 0 else st_s.h[si - 1]
        s.st_ot = st_p if si == 0 else st_s
        s.cv_o = cv_p.h if si == 0 else cv_s.h[si - 1]
        s.cv_ot = cv_p if si == 0 else cv_s
        s.hT = dscr("hT%d" % si, [128, 8, s.L], BF16)
        s.maT = dscr("maT%d" % si, [128, 8, s.L], F32)
        s.gbT = dscr("gbT%d" % si, [128, 8, s.L], F32)
        s.x1 = dscr("x1_%d" % si, [s.L, D], F32)
        s.h2T = dscr("h2T%d" % si, [128, 8, s.L], BF16)
        seqs.append(s)

    sb0 = (int(nc.sbuf_base) + 63) // 64 * 64
    sb1 = int(nc.sbuf_top) // 64 * 64
    CONST = Arena(nc, sb0, sb0 + 12 * 1024)
    AR = Arena(nc, sb0 + 12 * 1024, sb1)

    banks = [Tile(nc.alloc_psum_tensor("ps%d" % i, [128, 512], F32), "ps%d" % i) for i in range(8)]

    class Rot:
        def __init__(self, items):
            self.items = items
            self.i = 0

        def next(self):
            t = self.items[self.i]
            self.i = (self.i + 1) % len(self.items)
            return t

    K.cvals = [1.0, 0.5, -1.0, 0.0, -2.0]
    K.cst = CONST.alloc("cst", [128, 8], F32)
    for j_, v_ in enumerate(K.cvals):
        K.op('pool', lambda e, j_=j_, v_=v_: e.memset(K.cst[:, j_:j_ + 1], v_), (), [K.cst])
    ident_f = CONST.alloc("ident_f", [128, 128], F32)
    ident_b = CONST.alloc("ident_b", [128, 128], BF16)
    ones_f = CONST.alloc("ones_f", [128, 128], F32)
    U_full = CONST.alloc("U_full", [128, 128], F32)
    U_blk = CONST.alloc("U_blk", [128, 128], F32)
    SL_full = CONST.alloc("SL_full", [128, 128], F32)
    nm_p = CONST.alloc("nm_p", [128, 384], F32)
    nm_s = CONST.alloc("nm_s", [128, 384], F32)
    mhalf = CONST.alloc("mhalf", [128, 1], F32)
    negM = CONST.alloc("negM", [128, 1], F32)
    gqk = CONST.alloc("gqk", [128, 96], F32)
    sel3 = CONST.alloc("sel3", [3, 3, 128], F32)

    def aff(out, in_, pattern, cmp, fill, base, cm, R, W):
        return K.op('pool', lambda e: e.affine_select(out=out, in_=in_, pattern=pattern, compare_op=cmp, fill=fill, base=base, channel_multiplier=cm), R, W)

    K.memset('pool', ones_f[:, :], 1.0, [ones_f])
    K.memset('pool', mhalf[:, :], -0.5, [mhalf])
    K.memset('pool', ident_f[:, :], 1.0, [ident_f])
    aff(ident_f[:, :], ident_f[:, :], [[-1, 128]], ALU.is_equal, 0.0, 0, 1, [ident_f], [ident_f])
    K.copy('dve', ident_b[:, :], ident_f[:, :], [ident_f], [ident_b])
    K.memset('pool', U_full[:, :], 1.0, [U_full])
    aff(U_full[:, :], U_full[:, :], [[1, 128]], ALU.is_ge, 0.0, 0, -1, [U_full], [U_full])
    K.copy('pool', U_blk[:, :], U_full[:, :], [U_full], [U_blk])
    K.memset('pool', U_blk[0:64, 64:128], 0.0, [U_blk])
    K.memset('pool', SL_full[:, :], 1.0, [SL_full])
    aff(SL_full[:, :], SL_full[:, :], [[-1, 128]], ALU.is_gt, 0.0, 0, 1, [SL_full], [SL_full])
    for nm, blk in ((nm_p, True), (nm_s, False)):
        K.memset('pool', nm[:, :], 0.0, [nm])
        aff(nm[:, 0:128], nm[:, 0:128], [[-1, 128]], ALU.is_gt, NEG, 0, 1, [nm], [nm])
        aff(nm[:, 128:256], nm[:, 128:256], [[1, 128]], ALU.is_gt, NEG, 0, -1, [nm], [nm])
        aff(nm[:, 256:384], nm[:, 256:384], [[1, 128]], ALU.is_ge, NEG, 0, -1, [nm], [nm])
        if blk:
            K.memset('pool', nm[64:128, 0:64], NEG, [nm])
            K.memset('pool', nm[0:64, 192:256], NEG, [nm])
    K.memset('pool', sel3[:, :, :], 1.0, [sel3])
    for r in range(3):
        aff(sel3[:, r, :], sel3[:, r, :], [[0, 128]], ALU.is_equal, 0.0, -r, 1, [sel3], [sel3])

    def bcast_load(tile_ap, tile, dram_row_ap, dram_tile, queue='sp'):
        K.dma(queue, tile_ap, dram_row_ap, R=[dram_tile], W=[tile])

    def rsqrt_inplace(ap, tile, n_part):
        shp = list(ap.shape)
        K.tt('pool', ap, ap, mhalf[0:n_part, 0:1].to_broadcast(shp) if len(shp) == 2 else mhalf[0:n_part, 0:1].unsqueeze(2).to_broadcast(shp), ALU.pow, [tile, mhalf], [tile])

    AR.reset()
    cT = AR.alloc("cT", [128, 8, 3], F32)
    sT = AR.alloc("sT", [128, 8, 3], F32)
    sTb = AR.alloc("sTb", [128, 8, 3], BF16)
    adab = AR.alloc("adab", [3, 6144], F32)
    modsb = AR.alloc("modsb", [3, 6144], F32)
    adw = [AR.alloc("adw%d" % i, [128, 8, 512], BF16) for i in range(3)]
    for r in range(3):
        K.dma('sp', cT[:, :, r], c3.h[r].rearrange("(k p) -> p k", p=128), R=[c3], W=[cT], slow=True)
    K.dma('sp', adab[:, :], ada_b.h[0].partition_broadcast(3), R=[ada_b], W=[adab])
    K.act(sT[:, :, :], cT[:, :, :], AF.Tanh, [cT], [sT], scale=0.5)
    K.stt('dve', sT[:, :, :], sT[:, :, :], 1.0, cT[:, :, :], ALU.add, ALU.mult, [sT, cT], [sT])
    K.ts('dve', sTb[:, :, :], sT[:, :, :], 0.5, ALU.mult, [sT], [sTb])
    adw_rot = Rot(adw)
    ps_rot = Rot(banks[0:2])
    for cb in range(12):
        wt = adw_rot.next()
        K.dma('pool', wt[:, :, :], ada_w.h[:, cb * 512:(cb + 1) * 512].rearrange("(k p) n -> p k n", p=128), R=[ada_w], W=[wt])
        pb = ps_rot.next()
        for k in range(8):
            K.mm(pb[0:3, :], sTb[:, k, :], wt[:, k, :], k == 0, k == 7, [sTb, wt], [pb])
        K.tt('dve', modsb[:, cb * 512:(cb + 1) * 512], pb[0:3, :], adab[:, cb * 512:(cb + 1) * 512], ALU.add, [pb, adab], [modsb])
    K.dma('sp', mod_d.h[:, :], modsb[:, :], R=[modsb], W=[mod_d])
    K.barrier()

    def load_mod_bc(tile, si, idx, queue='sp'):
        K.dma(queue, tile[:, :], mod_d.h[si, idx * 1024:(idx + 1) * 1024].partition_broadcast(128), R=[mod_d], W=[tile])

    AR.reset()
    w_qkv = AR.alloc("w_qkv", [128, 8, 1536], BF16)
    w_z = AR.alloc("w_z", [128, 8, 512], BF16)
    w_ab = AR.alloc("w_ab", [128, 8, 16], BF16)
    w_ga = AR.alloc("w_ga", [128, 8, 1024], BF16)
    w_go = AR.alloc("w_go", [128, 4, 1024], BF16)
    w_gbA = AR.alloc("w_gbA", [128, 8, 1024], BF16)
    w_in_v = w_in.h.rearrange("(k p) n -> p k n", p=128)
    K.dma('pool', w_qkv[:, :, :], w_in_v[:, :, 0:1536], R=[w_in], W=[w_qkv])
    K.dma('pool', w_ab[:, :, :], w_in_v[:, :, A0:A0 + 16], R=[w_in], W=[w_ab])
    K.dma('pool', w_z[:, :, :], w_in_v[:, :, Z0:Z0 + 512], R=[w_in], W=[w_z])
    K.dma('pool', w_ga[:, :, :], w_in_v[:, :, GA0:GA0 + 1024], R=[w_in], W=[w_ga])
    K.dma('pool', w_go[:, :, :], w_gdn_out.h.rearrange("(k p) n -> p k n", p=128), R=[w_gdn_out], W=[w_go])
    K.dma('pool', w_gbA[:, :, :], w_in_v[:, :, GB0:GB0 + 1024], R=[w_in], W=[w_gbA])
    cw = AR.alloc("cw", [128, 4, 12], F32)
    for t_ in range(4):
        K.dma('sp', cw[:, t_, :], conv_w.h[t_].rearrange("(c p) -> p c", p=128), R=[conv_w], W=[cw], slow=True)
    n1g_bc = AR.alloc("n1g_bc", [128, 1024], F32)
    K.dma('sp', n1g_bc[:, :], norm1_g.h[0].partition_broadcast(128), R=[norm1_g], W=[n1g_bc])
    alog_bc = AR.alloc("alog_bc", [128, 8], F32)
    dtb_bc = AR.alloc("dtb_bc", [128, 8], F32)
    gn_bc = AR.alloc("gn_bc", [128, 64], F32)
    K.dma('sp', alog_bc[:, :], a_log.h[0].partition_broadcast(128), R=[a_log], W=[alog_bc])
    K.dma('sp', dtb_bc[:, :], dt_bias.h[0].partition_broadcast(128), R=[dt_bias], W=[dtb_bc])
    K.dma('sp', gn_bc[:, :], gdn_norm_g.h[0].partition_broadcast(128), R=[gdn_norm_g], W=[gn_bc])
    negA = AR.alloc("negA", [128, 8], F32)
    K.act(negA[:, :], alog_bc[:, :], AF.Exp, [alog_bc], [negA])
    K.ts('dve', negA[:, :], negA[:, :], -1.0, ALU.mult, [negA], [negA])
    selT = AR.alloc("selT", [32, 16, 128], F32)
    K.memset('pool', selT[:, :, :], 1.0, [selT])
    for qn in range(4):
        for pr in range(4):
            for hf in range(2):
                base = qn * 8 + 2 * pr + hf
                aff(selT[:, qn * 4 + pr, hf * 64:(hf + 1) * 64], selT[:, qn * 4 + pr, hf * 64:(hf + 1) * 64], [[0, 64]], ALU.is_equal, 0.0, -base, 1, [selT], [selT])

    gmod_bc = AR.alloc("gmod_bc", [128, 1024], F32)
    shift_bc = AR.alloc("shift_bc", [128, 1024], F32)
    markA = AR.mark()

    def pass_A(s):
        if s.i > 0:
            K.barrier()
        AR.reset(markA)
        L, TT, nsub, C = s.L, s.TT, s.nsub, s.C
        TB = 256 if s.i == 0 else 16
        nT = TB // TT
        nblk = L // TB
        nm = nm_p if nsub == 2 else nm_s
        Ublk = U_blk if nsub == 2 else U_full
        nlev = 6 if C == 64 else 4
        load_mod_bc(shift_bc, s.i, 0)
        load_mod_bc(gmod_bc, s.i, 1)
        K.stt('dve', gmod_bc[:, :], gmod_bc[:, :], 1.0, n1g_bc[:, :], ALU.add, ALU.mult, [gmod_bc, n1g_bc], [gmod_bc])
        S = AR.alloc("S", [64, 8, 64], F32)
        Sb = AR.alloc("Sb", [64, 8, 64], BF16)
        Sb0 = AR.alloc("Sb0", [64, 8, 64], BF16)
        if s.i == 0:
            K.memset('pool', S[:, :, :], 0.0, [S])
        else:
            K.dma('sp', S[:, :, :], st_c.h[s.i - 1].rearrange("h d e -> d h e"), R=[st_c], W=[S])
        K.copy('dve', Sb[:, :, :], S[:, :, :], [S], [Sb])
        Sv = S[:, :, :].rearrange("p (a b) e -> p a b e", b=2)
        prec = [AR.alloc("pre%d" % i, [128, TB + 3], F32) for i in range(2)]
        carry = AR.alloc("carry", [128, 3, 12], F32)
        if s.i == 0:
            K.memset('pool', carry[:, :, :], 0.0, [carry])
        else:
            for t_ in range(3):
                K.dma('sp', carry[:, t_, :], cv_c.h[s.i - 1, t_].rearrange("(c p) -> p c", p=128), R=[cv_c], W=[carry], slow=True)
        xt = [AR.alloc("xt%d" % i, [TT, 1024], F32) for i in range(2)]
        xn = [AR.alloc("xn%d" % i, [TT, 1024], BF16) for i in range(2)]
        st1 = AR.alloc("st1", [128, 8], F32)
        hTb = AR.alloc("hTb", [128, 8, TB], BF16)
        acc = AR.alloc("acc", [128, TB], F32)
        tnh = AR.alloc("tnh", [128, TB], F32)
        qkvf = AR.alloc("qkvf", [128, 12, TB], BF16)
        sq = AR.alloc("sq", [128, TB], F32)
        stat = AR.alloc("stat", [TT, nT, 64], F32)
        stat2 = AR.alloc("stat2", [TT, nT, 32], F32)
        statT = AR.alloc("statT", [32, TB], F32)
        varq = AR.alloc("varq", [64, 8, TB], BF16)
        varqg = AR.alloc("varqg", [64, 8, TB], BF16)
        vark = AR.alloc("vark", [64, 8, TB], BF16)
        varkb = AR.alloc("varkb", [64, 8, TB], BF16)
        vtmps = [AR.alloc("vtmp%d" % i, [128, TB], BF16) for i in range(3 if s.i == 0 else 1)]
        vt_rot = Rot(vtmps)
        nbuf_t = 2 if s.i == 0 else 1
        ktoks = [AR.alloc("ktok%d" % i, [TT, 2, 8, 64], BF16) for i in range(nbuf_t)] * (2 // nbuf_t)
        vtoks = [AR.alloc("vtok%d" % i, [TT, 8, 64], BF16) for i in range(nbuf_t)] * (2 // nbuf_t)
        Gu = AR.alloc("Gu", [TT, 8, TT], F32)
        if s.i == 0:
            accs = [acc, AR.alias("acc2", [128, TB], F32, Gu.off, Gu)]
            tnhs = [tnh, AR.alias("tnh2", [128, TB], F32, Gu.off + TB * 4, Gu)]
        else:
            accs, tnhs = [acc, acc], [tnh, tnh]
        dec = AR.alloc("dec", [TT, 8, 3 * TT], F32)
        Lk = [AR.alloc("Lk%d" % i, [TT, 4, TT], F32) for i in range(2)]
        Mk = [AR.alloc("Mk%d" % i, [TT, 4, TT], F32) for i in range(2)]
        Pm = [AR.alloc("Pm%d" % i, [TT, 4, TT], F32) for i in range(2)]
        Pb = AR.alloc("Pb", [TT, 8, TT], BF16)
        QKbs = [AR.alloc("QKb%d" % i, [TT, 8, TT], BF16) for i in range(nbuf_t)] * (2 // nbuf_t)
        usb = AR.alias("usb", [TT, 8, 64], F32, dec.off, dec) if s.i == 0 else AR.alloc("usb", [TT, 8, 64], F32)
        wTb = AR.alloc("wTb", [64, 8, TT], BF16)
        vnews = [AR.alloc("vnew%d" % i, [TT, 8, 64], BF16) for i in range(nsub)]
        for vn_ in vnews:
            K.memset("pool", vn_[:, :, :], 0.0, [vn_])
        egl = AR.alloc("egl", [64, nsub, 8], F32)
        osb = AR.alias("osb", [TT, 8, 64], F32, dec.off + 2048, dec) if s.i == 0 else AR.alloc("osb", [TT, 8, 64], F32)
        osq = AR.alias("osq", [TT, 8, 64], F32, dec.off + 4096, dec) if s.i == 0 else AR.alloc("osq", [TT, 8, 64], F32)
        ost = AR.alloc("ost", [TT, 8], F32)
        zz = AR.alias("zz", [TT, 512], F32, dec.off + 6144, dec) if s.i == 0 else AR.alloc("zz", [TT, 512], F32)
        ogb = AR.alloc("ogb", [TT, 512], BF16)
        ogT = AR.alloc("ogT", [128, 4, TB], BF16)
        tga = tnh
        mao = [acc, sq]
        gbo = [AR.alloc("gbo%d" % i, [128, TB], F32) for i in range(2)]
        prb = Rot(banks)

        def norm_part(blk_, ti):
            t0_ = blk_ * TB
            xtile = xt[ti % 2]
            xnt = xn[ti % 2]
            c_ = 2 * (ti % 2)
            K.dma('sp', xtile[:, :], s.x[t0_ + ti * TT:t0_ + (ti + 1) * TT, :], R=[s.xt], W=[xtile])
            K.memset('pool', st1[0:TT, c_:c_ + 1], 0.0, [st1])
            K.act(xnt[:, :], xtile[:, :], AF.Square, [xtile], [xnt, st1], accum=st1[0:TT, c_:c_ + 1])
            K.ts('dve', st1[0:TT, c_ + 1:c_ + 2], st1[0:TT, c_:c_ + 1], 1.0 / D, ALU.mult, [st1], [st1], s2=EPS, op1=ALU.add)
            rsqrt_inplace(st1[0:TT, c_ + 1:c_ + 2], st1, TT)
            K.stt('dve', xtile[:, :], xtile[:, :], st1[0:TT, c_ + 1:c_ + 2], gmod_bc[0:TT, :], ALU.mult, ALU.mult, [xtile, st1, gmod_bc], [xtile])
            K.tt('pool', xnt[:, :], xtile[:, :], shift_bc[0:TT, :], ALU.add, [xtile, shift_bc], [xnt])

        for blk in range(nblk):
            t0 = blk * TB
            if blk == 0:
                for ti in range(nT):
                    norm_part(0, ti)
            for ti in range(nT):
                xnt = xn[ti % 2]
                pb = prb.next()
                pbv = pb[:, :].bitcast(BF16)
                for k in range(8):
                    K.tr(pbv[:, k * TT:(k + 1) * TT], xnt[:, k * 128:(k + 1) * 128], ident_b[0:TT, 0:TT], [xnt, ident_b], [pb])
                K.copy('act', hTb[:, :, ti * TT:(ti + 1) * TT], pbv[:, 0:8 * TT].rearrange("p (k t) -> p k t", k=8), [pb], [hTb])
            K.dma('sp', s.hT.h[:, :, t0:t0 + TB], hTb[:, :, :], R=[hTb], W=[s.hT])
            if STAGE == 'a1':
                continue

            for ch in range(12):
                pb = prb.next()
                for k in range(8):
                    K.mm(pb[:, 0:TB], w_qkv[:, k, ch * 128:(ch + 1) * 128], hTb[:, k, :], k == 0, k == 7, [w_qkv, hTb], [pb])
                pre = prec[ch % 2]
                acc_, tnh_ = accs[ch % 2], tnhs[ch % 2]
                K.copy('pool', pre[:, 0:3], carry[:, :, ch], [carry], [pre])
                K.copy('act', pre[:, 3:3 + TB], pb[:, 0:TB], [pb], [pre])
                K.ts('dve', acc_[:, :], pre[:, 3:3 + TB], cw[:, 3, ch:ch + 1], ALU.mult, [pre, cw], [acc_])
                K.stt('pool', acc_[:, :], pre[:, 2:2 + TB], cw[:, 2, ch:ch + 1], acc_[:, :], ALU.mult, ALU.add, [pre, cw, acc_], [acc_])
                K.stt('dve', acc_[:, :], pre[:, 1:1 + TB], cw[:, 1, ch:ch + 1], acc_[:, :], ALU.mult, ALU.add, [pre, cw, acc_], [acc_])
                K.stt('pool', acc_[:, :], pre[:, 0:TB], cw[:, 0, ch:ch + 1], acc_[:, :], ALU.mult, ALU.add, [pre, cw, acc_], [acc_])
                K.copy('pool', carry[:, :, ch], pre[:, TB:TB + 3], [pre], [carry])
                K.act(tnh_[:, :], acc_[:, :], AF.Tanh, [acc_], [tnh_], scale=0.5)
                K.stt('dve', qkvf[:, ch, :], tnh_[:, :], 1.0, acc_[:, :], ALU.add, ALU.mult, [tnh_, acc_], [qkvf])
            if blk == nblk - 1:
                for t_ in range(3):
                    K.dma('sp', s.cv_o[t_].rearrange("(c p) -> p c", p=128), carry[:, t_, :], R=[carry], W=[s.cv_ot], is_output=True, slow=True)

            if STAGE == 'a3':
                continue
            pst = prb.next()
            for ch in range(8):
                K.act(sq[:, :], qkvf[:, ch, :], AF.Square, [qkvf], [sq])
                for ti in range(nT):
                    K.mm(pst[0:TT, ti * 16 + ch * 2:ti * 16 + ch * 2 + 2], sq[:, ti * TT:(ti + 1) * TT], U_blk_ones[:, :], True, True, [sq, U_blk_ones], [pst])
            K.copy('dve', stat[:, :, 0:16], pst[0:TT, 0:nT * 16].rearrange("p (t c) -> p t c", t=nT), [pst], [stat])
            pab = prb.next()
            for ti in range(nT):
                for k in range(8):
                    K.mm(pab[0:TT, ti * 16:(ti + 1) * 16], hTb[:, k, ti * TT:(ti + 1) * TT], w_ab[:, k, :], k == 0, k == 7, [hTb, w_ab], [pab])
            pabv = pab[0:TT, 0:nT * 16].rearrange("p (t c) -> p t c", t=nT)
            K.act(stat[:, :, 40:48], pabv[:, :, 8:16], AF.Tanh, [pab], [stat], scale=0.5)
            K.ts('dve', stat[:, :, 40:48], stat[:, :, 40:48], 0.5, ALU.mult, [stat], [stat], s2=0.5, op1=ALU.add)
            K.tt('dve', stat[:, :, 32:40], pabv[:, :, 0:8], dtb_bc[0:TT, :].unsqueeze(1).to_broadcast([TT, nT, 8]), ALU.add, [pab, dtb_bc], [stat])
            K.ts('dve', stat2[:, :, 24:32], stat[:, :, 32:40], 0.0, ALU.max, [stat], [stat2])
            K.stt('dve', stat[:, :, 32:40], stat2[:, :, 24:32], -2.0, stat[:, :, 32:40], ALU.mult, ALU.add, [stat, stat2], [stat])
            K.act(stat[:, :, 32:40], stat[:, :, 32:40], AF.Exp, [stat], [stat])
            K.act(stat[:, :, 32:40], stat[:, :, 32:40], AF.Ln, [stat], [stat], bias=1.0)
            K.tt('dve', stat[:, :, 32:40], stat[:, :, 32:40], stat2[:, :, 24:32], ALU.add, [stat, stat2], [stat])
            K.tt('dve', stat[:, :, 32:40], stat[:, :, 32:40], negA[0:TT, :].unsqueeze(1).to_broadcast([TT, nT, 8]), ALU.mult, [stat, negA], [stat])
            K.ts('dve', stat[:, :, 0:16], stat[:, :, 0:16], 0.25, ALU.mult, [stat], [stat], s2=EPS, op1=ALU.add)
            rsqrt_inplace(stat[:, :, 0:16], stat, TT)
            K.ts('dve', stat[:, :, 0:8], stat[:, :, 0:8], 0.5 / 8.0, ALU.mult, [stat], [stat])
            K.ts('dve', stat[:, :, 8:16], stat[:, :, 8:16], 0.5, ALU.mult, [stat], [stat])
            pcs = prb.next()
            for ti in range(nT):
                K.mm(pcs[0:TT, ti * 16:ti * 16 + 8], Ublk[0:TT, 0:TT], stat[:, ti, 32:40], True, True, [Ublk, stat], [pcs])
                K.mm(pcs[0:TT, ti * 16 + 8:ti * 16 + 16], U_full[0:TT, 0:TT], stat[:, ti, 32:40], True, True, [U_full, stat], [pcs])
            K.copy('dve', stat[:, :, 48:64], pcs[0:TT, 0:nT * 16].rearrange("p (t c) -> p t c", t=nT), [pcs], [stat])
            K.tt('dve', stat[:, :, 16:24], stat[:, :, 8:16], stat[:, :, 40:48], ALU.mult, [stat], [stat])
            K.act(stat2[:, :, 24:32], stat[:, :, 56:64], AF.Exp, [stat], [stat2])
            K.tt('dve', stat[:, :, 24:32], stat[:, :, 0:8], stat2[:, :, 24:32], ALU.mult, [stat, stat2], [stat])
            K.act(stat2[:, :, 24:32], stat[:, :, 48:56], AF.Exp, [stat], [stat2])
            K.tt('dve', stat2[:, :, 0:8], stat[:, :, 16:24], stat2[:, :, 24:32], ALU.mult, [stat, stat2], [stat2])
            K.ts('dve', stat2[:, :, 16:24], stat[:, :, 40:48], 0.5, ALU.mult, [stat], [stat2])
            pgl = prb.next()
            for ti in range(nT):
                K.mm(pgl[0:TT, ti * 8:ti * 8 + 8], SLTblk[s.i][0:TT, 0:TT], stat[:, ti, 32:40], True, True, [SLTblk[s.i], stat], [pgl])
            K.act(stat2[:, :, 24:32], pgl[0:TT, 0:nT * 8].rearrange("p (t c) -> p t c", t=nT), AF.Exp, [pgl], [stat2])
            K.tt('dve', stat2[:, :, 8:16], stat[:, :, 8:16], stat2[:, :, 24:32], ALU.mult, [stat, stat2], [stat2])

            if STAGE == 'a5':
                continue
            pT = prb.next()
            for ti in range(nT):
                K.tr(pT[0:32, ti * TT:(ti + 1) * TT], stat[:, ti, 0:32], ident_f[0:TT, 0:TT], [stat, ident_f], [pT])
            K.copy('act', statT[:, :], pT[0:32, 0:TB], [pT], [statT])
            vsteps = [(pr, qn, dst, srcch) for pr in range(4)
                      for (qn, dst, srcch) in ((0, varq, pr), (3, varqg, pr), (1, vark, 4 + pr), (2, varkb, 4 + pr))]
            pbcs = {}

            def v_sel(i_):
                pr, qn, dst, srcch = vsteps[i_]
                pbc = prb.next()
                K.mm(pbc[:, 0:TB], selT[:, qn * 4 + pr, :], statT[:, :], True, True, [selT, statT], [pbc])
                pbcs[i_] = pbc

            def v_apply(i_):
                pr, qn, dst, srcch = vsteps[i_]
                pbc = pbcs.pop(i_)
                vtmp = vt_rot.next()
                K.tt('dve', vtmp[:, :], qkvf[:, srcch, :], pbc[:, 0:TB], ALU.mult, [qkvf, pbc], [vtmp])
                for hh in range(2):
                    p2 = prb.next()
                    K.mm(p2[0:64, 0:TB], ident_b[:, hh * 64:(hh + 1) * 64], vtmp[:, :], True, True, [ident_b, vtmp], [p2])
                    K.copy('act', dst[:, 2 * pr + hh, :], p2[0:64, 0:TB], [p2], [dst])

            v_sel(0)
            for i_ in range(len(vsteps)):
                if i_ + 1 < len(vsteps):
                    v_sel(i_ + 1)
                v_apply(i_)
            if STAGE == 'a7':
                continue
            def prep(ti):
                c0 = ti * TT
                ktok, vtok, QKb = ktoks[ti % 2], vtoks[ti % 2], QKbs[ti % 2]
                pk = prb.next()
                pkv = pk[:, :].bitcast(BF16)
                for ch in range(4):
                    K.tr(pkv[0:TT, ch * 128:(ch + 1) * 128], qkvf[:, 4 + ch, c0:c0 + TT], ident_b[:, :], [qkvf, ident_b], [pk])
                for ch in range(4):
                    K.tr(pkv[0:TT, 512 + ch * 128:512 + (ch + 1) * 128], qkvf[:, 8 + ch, c0:c0 + TT], ident_b[:, :], [qkvf, ident_b], [pk])
                kview = pkv[0:TT, 0:512].rearrange("p (h d) -> p h d", h=8)
                vview = pkv[0:TT, 512:1024].rearrange("p (h d) -> p h d", h=8)
                K.tt('dve', ktok[:, 0, :, :], kview, stat2[:, ti, 0:8].unsqueeze(2).to_broadcast([TT, 8, 64]), ALU.mult, [pk, stat2], [ktok])
                K.tt('dve', ktok[:, 1, :, :], kview, stat2[:, ti, 8:16].unsqueeze(2).to_broadcast([TT, 8, 64]), ALU.mult, [pk, stat2], [ktok])
                K.tt('dve', vtok[:, :, :], vview, stat2[:, ti, 16:24].unsqueeze(2).to_broadcast([TT, 8, 64]), ALU.mult, [pk, stat2], [vtok])
                for h in range(8):
                    K.ts('dve' if h % 2 == 0 else 'pool', Gu[:, h, :], U_full[0:TT, 0:TT], stat[:, ti, 32 + h:33 + h], ALU.mult, [U_full, stat], [Gu])
                for hp in range(4):
                    pg = prb.next()
                    for hh in range(2):
                        h = hp * 2 + hh
                        o0 = hh * 2 * TT
                        K.mm(pg[0:TT, o0:o0 + TT], Gu[:, h, :], SL_full[0:TT, 0:TT], True, True, [Gu, SL_full], [pg])
                        K.mm(pg[0:TT, o0 + TT:o0 + 2 * TT], SL_full[0:TT, 0:TT], Gu[:, h, :], True, True, [Gu, SL_full], [pg])
                    pgv = pg[0:TT, 0:4 * TT].rearrange("p (h x) -> p h x", h=2)
                    nmLU = nm[0:TT, 0:256] if TT == 128 else nmS2[0:TT, 0:32]
                    K.stt('dve', dec[:, hp * 2:hp * 2 + 2, 0:2 * TT], pgv, 0.0, nmLU.unsqueeze(1).to_broadcast([TT, 2, 2 * TT]), ALU.min, ALU.add, [pg, nm, nmS2], [dec])
                    K.stt('dve', dec[:, hp * 2:hp * 2 + 2, 2 * TT:3 * TT], pgv[:, :, TT:2 * TT], 0.0, nm[0:TT, 256:256 + TT].unsqueeze(1).to_broadcast([TT, 2, TT]), ALU.min, ALU.add, [pg, nm], [dec])
                K.act(dec[:, :, :], dec[:, :, :], AF.Exp, [dec], [dec])
                for g in range(2):
                    pL = prb.next()
                    pM = prb.next()
                    pQ = prb.next()
                    for hh in range(4):
                        h = g * 4 + hh
                        kT_h = vark[:, h, c0:c0 + TT]
                        kbT_h = varkb[:, h, c0:c0 + TT]
                        qT_h = varq[:, h, c0:c0 + TT]
                        K.mm(pL[0:TT, hh * TT:(hh + 1) * TT], kbT_h, kT_h, True, True, [vark, varkb], [pL])
                        K.mm(pM[0:TT, hh * TT:(hh + 1) * TT], kT_h, kbT_h, True, True, [vark, varkb], [pM])
                        K.mm(pQ[0:TT, hh * TT:(hh + 1) * TT], kT_h, qT_h, True, True, [vark, varq], [pQ])
                    gs = slice(g * 4, g * 4 + 4)
                    K.tt('dve', Lk[g][:, :, :], pL[0:TT, 0:4 * TT].rearrange("p (h x) -> p h x", h=4), dec[:, gs, 0:TT], ALU.mult, [pL, dec], [Lk[g]])
                    K.tt('dve', Mk[g][:, :, :], pM[0:TT, 0:4 * TT].rearrange("p (h x) -> p h x", h=4), dec[:, gs, TT:2 * TT], ALU.mult, [pM, dec], [Mk[g]])
                    K.tt('dve', QKb[:, gs, :], pQ[0:TT, 0:4 * TT].rearrange("p (h x) -> p h x", h=4), dec[:, gs, 2 * TT:3 * TT], ALU.mult, [pQ, dec], [QKb])
                    K.stt('pool', Pm[g][:, :, :], Mk[g][:, :, :], -1.0, ident_f[0:TT, 0:TT].unsqueeze(1).to_broadcast([TT, 4, TT]), ALU.mult, ALU.add, [Mk[g], ident_f], [Pm[g]])

            def doubling(ti):
                for lev in range(1, nlev):
                    last = (lev == nlev - 1)
                    for g in range(2):
                        pa = prb.next()
                        pbb = prb.next()
                        for hh in range(4):
                            K.mm(pa[0:TT, hh * TT:(hh + 1) * TT], Mk[g][:, hh, :], Lk[g][:, hh, :], True, True, [Mk[g], Lk[g]], [pa])
                            if not last:
                                K.mm(pbb[0:TT, hh * TT:(hh + 1) * TT], Lk[g][:, hh, :], Mk[g][:, hh, :], True, True, [Mk[g], Lk[g]], [pbb])
                        K.copy('act', Lk[g][:, :, :], pa[0:TT, 0:4 * TT].rearrange("p (h x) -> p h x", h=4), [pa], [Lk[g]])
                        if not last:
                            K.copy('dve', Mk[g][:, :, :], pbb[0:TT, 0:4 * TT].rearrange("p (h x) -> p h x", h=4), [pbb], [Mk[g]])
                    for g in range(2):
                        pc = prb.next()
                        for hh in range(4):
                            K.mm(pc[0:TT, hh * TT:(hh + 1) * TT], Lk[g][:, hh, :], Pm[g][:, hh, :], True, True, [Lk[g], Pm[g]], [pc])
                        K.tt('dve', Pm[g][:, :, :], Pm[g][:, :, :], pc[0:TT, 0:4 * TT].rearrange("p (h x) -> p h x", h=4), ALU.add, [Pm[g], pc], [Pm[g]])
                for g in range(2):
                    K.copy('act', Pb[:, g * 4:g * 4 + 4, :], Pm[g][:, :, :], [Pm[g]], [Pb])

            def scan_out(ti):
                c0 = ti * TT
                ktok, vtok, QKb = ktoks[ti % 2], vtoks[ti % 2], QKbs[ti % 2]
                pu = prb.next()
                for h in range(8):
                    K.mm(pu[0:TT, h * 64:(h + 1) * 64], Pb[:, h, :], vtok[:, h, :], True, True, [Pb, vtok], [pu])
                K.copy('act', usb[:, :, :], pu[0:TT, 0:512].rearrange("p (h e) -> p h e", h=8), [pu], [usb])
                for g in range(2):
                    pw = prb.next()
                    for hh in range(4):
                        h = g * 4 + hh
                        K.mm(pw[0:64, hh * TT:(hh + 1) * TT], ktok[:, 0, h, :], Pb[:, h, :], True, True, [ktok, Pb], [pw])
                    K.copy('act', wTb[:, g * 4:g * 4 + 4, :], pw[0:64, 0:4 * TT].rearrange("p (h x) -> p h x", h=4), [pw], [wTb])
                pe_ = prb.next()
                for sc in range(nsub):
                    K.mm(pe_[0:64, sc * 8:(sc + 1) * 8], (oblk[sc][:, 0:64] if nsub == 2 else ones_f[0:TT, 0:64]), stat[:, ti, 32:40], True, True, [ones_f, oblk[0], oblk[1], stat], [pe_])
                K.act(egl[:, :, :], pe_[0:64, 0:nsub * 8].rearrange("p (s h) -> p s h", s=nsub), AF.Exp, [pe_], [egl])
                K.copy('act', Sb0[:, :, :], Sb[:, :, :], [Sb], [Sb0])
                for sc in range(nsub):
                    rs = slice(sc * C, (sc + 1) * C)
                    pws = prb.next()
                    for h in range(8):
                        K.mm(pws[0:TT, h * 64:(h + 1) * 64], wTb[:, h, :], Sb[:, h, :], True, True, [wTb, Sb], [pws])
                    vnew = vnews[sc]
                    K.tt('dve', vnew[rs, :, :], usb[rs, :, :], pws[rs, 0:512].rearrange("p (h e) -> p h e", h=8), ALU.subtract, [usb, pws], [vnew])
                    psu = prb.next()
                    for h in range(8):
                        K.mm(psu[0:64, h * 64:(h + 1) * 64], ktok[:, 1, h, :], vnew[:, h, :], True, True, [ktok, vnew], [psu])
                    K.tt('dve', S[:, :, :], S[:, :, :], egl[:, sc, :].unsqueeze(2).to_broadcast([64, 8, 64]), ALU.mult, [S, egl], [S])
                    K.tt('dve', S[:, :, :], S[:, :, :], psu[0:64, 0:512].rearrange("p (h e) -> p h e", h=8), ALU.add, [S, psu], [S])
                    K.copy('act', Sb[:, :, :], S[:, :, :], [S], [Sb])
                po_ = prb.next()
                for h in range(8):
                    K.mm(po_[0:TT, h * 64:(h + 1) * 64], varqg[:, h, c0:c0 + TT], Sb0[:, h, :], True, False, [varqg, Sb0], [po_])
                    for sc in range(nsub):
                        K.mm(po_[0:TT, h * 64:(h + 1) * 64], QKb[:, h, :], vnews[sc][:, h, :], False, sc == nsub - 1, [QKb, vnews[sc]], [po_])
                K.copy('act', osb[:, :, :], po_[0:TT, 0:512].rearrange("p (h e) -> p h e", h=8), [po_], [osb])
                K.tt('dve', osq[:, :, :], osb[:, :, :], osb[:, :, :], ALU.mult, [osb], [osq])
                K.red('dve', ost[:, :], osq[:, :, :], ALU.add, [osq], [ost])
                K.ts('dve', ost[:, :], ost[:, :], 1.0 / 64.0, ALU.mult, [ost], [ost], s2=EPS, op1=ALU.add)
                rsqrt_inplace(ost[:, :], ost, TT)
                K.tt('dve', osb[:, :, :], osb[:, :, :], ost[:, :].unsqueeze(2).to_broadcast([TT, 8, 64]), ALU.mult, [osb, ost], [osb])
                K.tt('dve', osb[:, :, :], osb[:, :, :], gn_bc[0:TT, :].unsqueeze(1).to_broadcast([TT, 8, 64]), ALU.mult, [osb, gn_bc], [osb])
                pz = prb.next()
                for k in range(8):
                    K.mm(pz[0:TT, 0:512], hTb[:, k, c0:c0 + TT], w_z[:, k, :], k == 0, k == 7, [hTb, w_z], [pz])
                K.act(zz[:, :], pz[0:TT, 0:512], AF.Tanh, [pz], [zz], scale=0.5)
                K.stt('dve', zz[:, :], zz[:, :], 1.0, pz[0:TT, 0:512], ALU.add, ALU.mult, [zz, pz], [zz])
                K.stt('dve', ogb[:, :], zz[:, :], 0.5, osb[:, :, :].rearrange("p h e -> p (h e)"), ALU.mult, ALU.mult, [zz, osb], [ogb])
                pt = prb.next()
                ptv = pt[:, :].bitcast(BF16)
                for ch in range(4):
                    K.tr(ptv[:, ch * TT:(ch + 1) * TT], ogb[:, ch * 128:(ch + 1) * 128], ident_b[0:TT, 0:TT], [ogb, ident_b], [pt])
                K.copy('act', ogT[:, :, c0:c0 + TT], ptv[:, 0:4 * TT].rearrange("p (c t) -> p c t", c=4), [pt], [ogT])


            prep(0)
            doubling(0)
            for ti in range(nT):
                if ti + 1 < nT:
                    prep(ti + 1)
                scan_out(ti)
                if ti + 1 < nT:
                    doubling(ti + 1)

            if blk + 1 < nblk:
                for ti in range(nT):
                    norm_part(blk + 1, ti)
            for f in range(8):
                py = prb.next()
                for k in range(4):
                    K.mm(py[:, 0:TB], w_go[:, k, f * 128:(f + 1) * 128], ogT[:, k, :], k == 0, k == 3, [w_go, ogT], [py])
                pgt = prb.next()
                for k in range(8):
                    K.mm(pgt[:, 0:TB], w_ga[:, k, f * 128:(f + 1) * 128], hTb[:, k, :], k == 0, k == 7, [w_ga, hTb], [pgt])
                K.act(tga[:, :], pgt[:, 0:TB], AF.Tanh, [pgt], [tga], scale=0.5)
                mo = mao[f % 2]
                K.stt('dve', mo[:, :], tga[:, :], 1.0, py[:, 0:TB], ALU.add, ALU.mult, [tga, py], [mo])
                K.dma('sp', s.maT.h[:, f, t0:t0 + TB], mo[:, :], R=[mo], W=[s.maT])
                pgb_ = prb.next()
                for k in range(8):
                    K.mm(pgb_[:, 0:TB], w_gbA[:, k, f * 128:(f + 1) * 128], hTb[:, k, :], k == 0, k == 7, [w_gbA, hTb], [pgb_])
                go_ = gbo[f % 2]
                K.act(go_[:, :], pgb_[:, 0:TB], AF.Tanh, [pgb_], [go_], scale=0.5)
                K.ts('dve', go_[:, :], go_[:, :], 1.0, ALU.add, [go_], [go_])
                K.dma('sp', s.gbT.h[:, f, t0:t0 + TB], go_[:, :], R=[go_], W=[s.gbT])
        K.dma('sp', s.st_o.rearrange("h d e -> d h e"), S[:, :, :], R=[S], W=[s.st_ot], is_output=True)

    U_blk_ones = CONST.alloc("blk2", [128, 2], F32)
    K.memset('pool', U_blk_ones[:, :], 0.0, [U_blk_ones])
    K.memset('pool', U_blk_ones[0:64, 0:1], 1.0, [U_blk_ones])
    K.memset('pool', U_blk_ones[64:128, 1:2], 1.0, [U_blk_ones])
    SLTp = CONST.alloc("SLTp", [128, 128], F32)
    K.memset('pool', SLTp[:, :], 1.0, [SLTp])
    aff(SLTp[:, :], SLTp[:, :], [[-1, 128]], ALU.is_gt, 0.0, 0, 1, [SLTp], [SLTp])
    SLTb = CONST.alloc("SLTb", [128, 128], F32)
    K.copy('pool', SLTb[:, :], SLTp[:, :], [SLTp], [SLTb])
    K.memset('pool', SLTb[64:128, 0:64], 0.0, [SLTb])
    SLTblk = [SLTb, SLTp, SLTp]
    oblk = [CONST.alloc("oblk%d" % i, [128, 64], F32) for i in range(2)]
    for i_ in range(2):
        K.memset('pool', oblk[i_][:, :], 0.0, [oblk[i_]])
        K.memset('pool', oblk[i_][i_ * 64:(i_ + 1) * 64, :], 1.0, [oblk[i_]])
    sel65 = CONST.alloc("sel65", [65, 64], F32)
    K.memset('pool', sel65[:, :], 0.0, [sel65])
    K.memset('pool', sel65[64:65, :], 1.0, [sel65])
    nmS2 = CONST.alloc("nmS2", [16, 32], F32)
    K.copy('pool', nmS2[:, 0:16], nm_s[0:16, 0:16], [nm_s], [nmS2])
    K.copy('pool', nmS2[:, 16:32], nm_s[0:16, 128:144], [nm_s], [nmS2])

    for s in seqs:
        if STAGE != 'p0' and str(s.i) in SEQS:
            pass_A(s)
    K.barrier()

    AR.reset()
    w_ml = AR.alloc("w_ml", [128, 8, 672], BF16)
    w_uqb = AR.alloc("w_uqb", [128, 3, 768], BF16)
    w_kvb = AR.alloc("w_kvb", [128, 2, 1024], BF16)
    w_mo = AR.alloc("w_mo", [64, 8, 1024], BF16)
    w_ob = AR.alloc("w_ob", [128, 8, 1024], BF16)
    K.dma('pool', w_ml[:, :, :], w_in_v[:, :, CQ0:CQ0 + 672], R=[w_in], W=[w_ml])
    K.dma('pool', w_uqb[:, :, :], w_uq.h.rearrange("(k p) n -> p k n", p=128), R=[w_uq], W=[w_uqb])
    K.dma('pool', w_kvb[:, :, :], w_ukv.h.rearrange("(k p) n -> p k n", p=128), R=[w_ukv], W=[w_kvb])
    K.dma('pool', w_mo[:, :, :], w_mla_out.h.rearrange("(h p) n -> p h n", p=64), R=[w_mla_out], W=[w_mo])
    K.dma('pool', w_ob[:, :, :], w_o.h.rearrange("(k p) n -> p k n", p=128), R=[w_o], W=[w_ob])
    qg_bc = AR.alloc("qg_bc", [128, 384], F32)
    kvg_bc = AR.alloc("kvg_bc", [128, 256], F32)
    gk_bc = AR.alloc("gk_bc", [128, 96], F32)
    K.dma('sp', qg_bc[:, :], q_norm_g.h[0].partition_broadcast(128), R=[q_norm_g], W=[qg_bc])
    K.dma('sp', kvg_bc[:, :], kv_norm_g.h[0].partition_broadcast(128), R=[kv_norm_g], W=[kvg_bc])
    K.dma('sp', gqk[:, :], qh_g.h[0].partition_broadcast(128), R=[qh_g], W=[gqk])
    K.dma('sp', gk_bc[:, :], kh_g.h[0].partition_broadcast(128), R=[kh_g], W=[gk_bc])
    K.tt('dve', gqk[:, :], gqk[:, :], gk_bc[:, :], ALU.mult, [gqk, gk_bc], [gqk])
    K.tt('dve', gk_bc[:, :], gqk[:, :], gqk[:, :], ALU.mult, [gqk], [gk_bc])
    K.op('dve', lambda e: e.tensor_reduce(out=negM[:, 0:1], in_=gk_bc[:, :], axis=AX.X, op=ALU.max), [gk_bc], [negM])
    phalf = AR.alloc("phalf", [128, 1], F32)
    K.memset('pool', phalf[:, :], 0.5, [phalf])
    K.tt('pool', negM[:, :], negM[:, :], phalf[:, :], ALU.pow, [negM, phalf], [negM])
    K.ts('dve', negM[:, :], negM[:, :], -float(np.sqrt(96.0)), ALU.mult, [negM], [negM])
    K.ts('dve', gqk[:, :], gqk[:, :], float(96.0 ** -0.5), ALU.mult, [gqk], [gqk])
    g1h_bc = AR.alloc("g1h_bc", [128, 1024], F32)
    gmod2_bc = AR.alloc("gmod2_bc", [128, 1024], F32)
    shift2_bc = AR.alloc("shift2_bc", [128, 1024], F32)
    markB = AR.mark()

    def pass_B(s):
        if s.i > 0:
            K.barrier()
        AR.reset(markB)
        L, TT = s.L, s.TT
        TB = 256 if s.i == 0 else 16
        nT = TB // TT
        nblk = L // TB
        P = s.past
        nPT = P // 128
        NT = nPT + (L + 127) // 128
        load_mod_bc(g1h_bc, s.i, 2)
        load_mod_bc(shift2_bc, s.i, 3)
        load_mod_bc(gmod2_bc, s.i, 4)
        K.ts('dve', g1h_bc[:, :], g1h_bc[:, :], 0.5, ALU.mult, [g1h_bc], [g1h_bc])
        n2g_tmp = AR.alloc("n2g_tmp", [128, 1024], F32)
        K.dma('sp', n2g_tmp[:, :], norm2_g.h[0].partition_broadcast(128), R=[norm2_g], W=[n2g_tmp])
        K.stt('dve', gmod2_bc[:, :], gmod2_bc[:, :], 1.0, n2g_tmp[:, :], ALU.add, ALU.mult, [gmod2_bc, n2g_tmp], [gmod2_bc])
        K.barrier()
        AR.reset(markB)
        kT = AR.alloc("kT", [96, 8, P + L], BF16)
        vA = AR.alloc("vA", [128, NT, 8, 66], BF16)
        rkS = AR.alloc("rkS", [128, NT, 8], F32)
        kTs = [Tile(kT.h, "kT%d" % i) for i in range(NT)]
        vAs = [Tile(vA.h, "vA%d" % i) for i in range(NT)]
        rkSs = [Tile(rkS.h, "rkS%d" % i) for i in range(NT)]
        K.memset('pool', vA[:, :, :, 64:65], 1.0, vAs)
        hTbs = [AR.alloc("hTbB%d" % i, [128, 8, TB], BF16) for i in range(2)]
        cs = [AR.alloc("cs%d" % i, [128, 32], F32) for i in range(2)]
        st = AR.alloc("stB", [128, 16], F32)
        NW = 2
        ckvn2 = [AR.alloc("ckvn%d" % i, [128, 256], F32) for i in range(NW)]
        kro2 = [AR.alloc("kro%d" % i, [128, 32], F32) for i in range(NW)]
        krr = AR.alloc("krr", [128, 32], F32)
        rt = AR.alloc("rt", [128, 4, 8, 16], F32)
        cqn = AR.alloc("cqn", [128, 384], BF16)
        cqnTs = [AR.alloc("cqnT%d" % i, [128, 3, TB], BF16) for i in range(2)]
        qtok = AR.alloc("qtok", [128, 8, 96], F32)
        qsq = AR.alloc("qsq", [128, 8, 96], F32)
        qnb = AR.alloc("qnb", [128, 8, 96], BF16)
        qTs = [AR.alloc("qT%d" % i, [96, 8, TB], BF16) for i in range(2)]
        pTs = [AR.alloc("pT%d" % i, [128, TB], BF16) for i in range(4)]
        oas = [AR.alloc("oa%d" % i, [65, TB], F32) for i in range(2)]
        rds = [AR.alloc("rd%d" % i, [65, TB], F32) for i in range(2)]
        for rd_ in rds:
            K.memset("pool", rd_[:, :], 0.0, [rd_])
        obTs = [AR.alloc("obT%d" % i, [64, 8, TB], BF16) for i in range(2)]
        tgb = AR.alloc("tgb", [128, TB], F32)
        mbt = AR.alloc("mbt", [128, TB], F32)
        mal = [AR.alloc("mal%d" % i, [128, TB], F32) for i in range(2)]
        gbl = [AR.alloc("gbl%d" % i, [128, TB], F32) for i in range(2)]
        mergedT = AR.alloc("mergedT", [128, 8, TB], BF16)
        xt = [AR.alloc("xtB0", [TT, 1024], F32)] * 2
        x1t = AR.alloc("x1t", [TT, 1024], F32)
        h2b = AR.alloc("h2b", [TT, 1024], BF16)
        h2Tb = AR.alloc("h2Tb", [128, 8, TB], BF16)
        if s.i > 0:
            scs = [AR.alloc("scs%d" % i, [128, 8, 16], F32) for i in range(2)]
            pTq = [AR.alloc("pTq%d" % i, [128, 128], BF16) for i in range(3)]
            oaS = AR.alloc("oaS", [65, 128], F32)
            rdS = AR.alloc("rdS", [65, 128], F32)
            K.memset('pool', rdS[:, :], 0.0, [rdS])
        prb = Rot(banks[3:6])
        prb_tiles = prb
        pra = Rot(banks[0:3])
        po_rot = Rot(banks[6:8])
        pT_rot = Rot(pTs)

        class EB:
            pass
        ebs = []
        for i_ in range(NW):
            e_ = EB()
            e_.ckb = AR.alloc("ckb%d" % i_, [128, 256], BF16)
            e_.krb = AR.alloc("krb%d" % i_, [128, 32], BF16)
            e_.ckT = AR.alloc("ckT%d" % i_, [128, 2, 128], BF16)
            e_.kfull = AR.alloc("kfull%d" % i_, [128, 8, 96], BF16)
            e_.sqk = AR.alloc("sqk%d" % i_, [128, 4, 64], F32)
            e_.ssq = AR.alloc("ssqB%d" % i_, [128, 8], F32)
            e_.krs = AR.alloc("krs%d" % i_, [128, 32], F32)
            e_.st = AR.alloc("stE%d" % i_, [128, 2], F32)
            ebs.append(e_)
        sqk = ebs[0].sqk
        eb_rot = Rot(ebs)

        def expand_kv(kt, ck_ap, ck_t, kr_ap, kr_t, n, col0, prb=None, e_=None):
            if prb is None:
                prb = prb_tiles
            if e_ is None:
                e_ = eb_rot.next()
            ckb, krb, ckT, kfull, sqk_, ssq, krs, ste = e_.ckb, e_.krb, e_.ckT, e_.kfull, e_.sqk, e_.ssq, e_.krs, e_.st
            K.copy('dve', ckb[0:n, :], ck_ap, [ck_t], [ckb])
            K.copy('dve', krb[0:n, :], kr_ap, [kr_t], [krb])
            pt_ = prb.next()
            ptv = pt_[:, :].bitcast(BF16)
            for c in range(2):
                K.tr(ptv[:, c * 128:c * 128 + n], ckb[0:n, c * 128:(c + 1) * 128], ident_b[0:n, 0:n], [ckb, ident_b], [pt_])
            K.copy('act', ckT[:, :, 0:n], ptv[:, 0:256].rearrange("p (c t) -> p c t", c=2)[:, :, 0:n], [pt_], [ckT])
            yield
            for g in range(2):
                pv_ = prb.next()
                for c in range(2):
                    K.mm(pv_[0:n, 0:512], ckT[:, c, 0:n], w_kvb[:, c, g * 512:(g + 1) * 512], c == 0, c == 1, [ckT, w_kvb], [pv_])
                pvv = pv_[0:n, 0:512].rearrange("p (h x) -> p h x", h=4)
                K.copy('act', vA[0:n, kt, g * 4:g * 4 + 4, 0:64], pvv[:, :, 64:128], [pv_], [vAs[kt]])
                K.copy('act', kfull[0:n, g * 4:g * 4 + 4, 0:64], pvv[:, :, 0:64], [pv_], [kfull])
                K.act(sqk_[0:n, :, :], pvv[:, :, 0:64], AF.Square, [pv_], [sqk_])
                K.red('dve', ssq[0:n, g * 4:g * 4 + 4], sqk_[0:n, :, :], ALU.add, [sqk_], [ssq])
                yield
            for h in range(8):
                K.copy('dve', kfull[0:n, h, 64:96], krb[0:n, :], [krb], [kfull])
            yield
            pkt = prb.next()
            pktv = pkt[:, :].bitcast(BF16)
            for h in range(8):
                K.tr(pktv[0:96, h * 128:h * 128 + n], kfull[0:n, h, :], ident_b[0:n, 0:n], [kfull, ident_b], [pkt])
            K.copy('act', kT[:, :, col0:col0 + n], pktv[0:96, :].rearrange("p (h t) -> p h t", h=8)[:, :, 0:n], [pkt], [kTs[kt]])
            K.tt('pool', krs[0:n, :], kr_ap, kr_ap, ALU.mult, [kr_t], [krs])
            K.red('dve', ste[0:n, 0:1], krs[0:n, :], ALU.add, [krs], [ste])
            K.ts('dve', ssq[0:n, :], ssq[0:n, :], ste[0:n, 0:1], ALU.add, [ssq, ste], [ssq])
            K.ts('dve', rkS[0:n, kt, :], ssq[0:n, :], 1.0 / 96.0, ALU.mult, [ssq], [rkSs[kt]], s2=EPS, op1=ALU.add)
            rsqrt_inplace(rkS[0:n, kt, :], rkSs[kt], n)
            yield

        def lockstep(gs):
            alive = list(gs)
            while alive:
                for g in list(alive):
                    try:
                        next(g)
                    except StopIteration:
                        alive.remove(g)

        poolsN = [Rot(banks[0:3]), Rot(banks[3:6])]
        for pt_i in range(0, nPT, NW):
            gens = []
            for j_ in range(NW):
                p_ = pt_i + j_
                ckl = ckvn2[j_]
                krl = kro2[j_]
                K.dma('sp', ckl[:, :], ckv_c.h[s.i - 1, p_ * 128:(p_ + 1) * 128, :], R=[ckv_c], W=[ckl])
                K.dma('sp', krl[:, :], kr_c.h[s.i - 1, p_ * 128:(p_ + 1) * 128, :], R=[kr_c], W=[krl])
                gens.append(expand_kv(p_, ckl[:, :], ckl, krl[:, :], krl, 128, p_ * 128, prb=poolsN[j_], e_=ebs[j_]))
            lockstep(gens)

        def gen_tiles(blk):
            t0 = blk * TB
            hTb, cqnT, qT = hTbs[blk % 2], cqnTs[blk % 2], qTs[blk % 2]
            for ti in range(nT):
                c0 = ti * TT
                pos = P + t0 + c0
                kt = nPT + (t0 + c0) // 128
                cs_ = cs[ti % 2]
                ckvn = ckvn2[ti % 2]
                kro = kro2[ti % 2]
                K.dma('sp', cs_[0:TT, :], rope_cs.h[pos:pos + TT, :], R=[rope_cs], W=[cs_])
                pcq = prb.next()
                pck = prb.next()
                for k in range(8):
                    K.mm(pcq[0:TT, 0:384], hTb[:, k, c0:c0 + TT], w_ml[:, k, 0:384], k == 0, k == 7, [hTb, w_ml], [pcq])
                for k in range(8):
                    K.mm(pck[0:TT, 0:288], hTb[:, k, c0:c0 + TT], w_ml[:, k, 384:672], k == 0, k == 7, [hTb, w_ml], [pck])
                K.memset('pool', st[0:TT, 0:2], 0.0, [st])
                K.act(qsq[0:TT, 0:4, :].rearrange("p a b -> p (a b)"), pcq[0:TT, 0:384], AF.Square, [pcq], [qsq, st], accum=st[0:TT, 0:1])
                K.act(sqk[0:TT, :, :].rearrange("p a b -> p (a b)"), pck[0:TT, 0:256], AF.Square, [pck], [sqk, st], accum=st[0:TT, 1:2])
                K.ts('dve', st[0:TT, 2:3], st[0:TT, 0:1], 1.0 / 384.0, ALU.mult, [st], [st], s2=EPS, op1=ALU.add)
                K.ts('dve', st[0:TT, 3:4], st[0:TT, 1:2], 1.0 / 256.0, ALU.mult, [st], [st], s2=EPS, op1=ALU.add)
                rsqrt_inplace(st[0:TT, 2:4], st, TT)
                K.stt('dve', qsq[0:TT, 0:4, :].rearrange("p a b -> p (a b)"), pcq[0:TT, 0:384], st[0:TT, 2:3], qg_bc[0:TT, :], ALU.mult, ALU.mult, [pcq, st, qg_bc], [qsq])
                K.copy('act', cqn[0:TT, :], qsq[0:TT, 0:4, :].rearrange("p a b -> p (a b)"), [qsq], [cqn])
                ptq = prb.next()
                ptqv = ptq[:, :].bitcast(BF16)
                for c in range(3):
                    K.tr(ptqv[:, c * TT:(c + 1) * TT], cqn[0:TT, c * 128:(c + 1) * 128], ident_b[0:TT, 0:TT], [cqn, ident_b], [ptq])
                K.copy('act', cqnT[:, :, c0:c0 + TT], ptqv[:, 0:3 * TT].rearrange("p (c t) -> p c t", c=3), [ptq], [cqnT])
                yield
                K.stt('dve', ckvn[0:TT, :], pck[0:TT, 0:256], st[0:TT, 3:4], kvg_bc[0:TT, :], ALU.mult, ALU.mult, [pck, st, kvg_bc], [ckvn])
                K.dma('sp', s.ckv_o[t0 + c0:t0 + c0 + TT, :], ckvn[0:TT, :], R=[ckvn], W=[s.ckv_ot], is_output=True)
                K.copy('act', krr[0:TT, :], pck[0:TT, 256:288], [pck], [krr])
                cosv, sinv = cs_[0:TT, 0:16], cs_[0:TT, 16:32]
                K.tt('dve', rt[0:TT, 0, 0, :], krr[0:TT, 0:16], cosv, ALU.mult, [krr, cs_], [rt])
                K.tt('dve', rt[0:TT, 1, 0, :], krr[0:TT, 16:32], sinv, ALU.mult, [krr, cs_], [rt])
                K.tt('dve', rt[0:TT, 2, 0, :], krr[0:TT, 16:32], cosv, ALU.mult, [krr, cs_], [rt])
                K.tt('dve', rt[0:TT, 3, 0, :], krr[0:TT, 0:16], sinv, ALU.mult, [krr, cs_], [rt])
                K.tt('dve', kro[0:TT, 0:16], rt[0:TT, 0, 0, :], rt[0:TT, 1, 0, :], ALU.subtract, [rt], [kro])
                K.tt('dve', kro[0:TT, 16:32], rt[0:TT, 2, 0, :], rt[0:TT, 3, 0, :], ALU.add, [rt], [kro])
                K.dma('sp', s.kr_o[t0 + c0:t0 + c0 + TT, :], kro[0:TT, :], R=[kro], W=[s.kr_ot], is_output=True)
                yield
                yield from expand_kv(kt, ckvn[0:TT, :], ckvn, kro[0:TT, :], kro, TT, P + t0 + c0)
                yield
                yield
                pq0 = prb.next()
                pq1 = prb.next()
                for c in range(3):
                    K.mm(pq0[0:TT, 0:480], cqnT[:, c, c0:c0 + TT], w_uqb[:, c, 0:480], c == 0, c == 2, [cqnT, w_uqb], [pq0])
                for c in range(3):
                    K.mm(pq1[0:TT, 0:288], cqnT[:, c, c0:c0 + TT], w_uqb[:, c, 480:768], c == 0, c == 2, [cqnT, w_uqb], [pq1])
                K.copy('act', qtok[0:TT, 0:5, :], pq0[0:TT, 0:480].rearrange("p (h d) -> p h d", h=5), [pq0], [qtok])
                K.copy('act', qtok[0:TT, 5:8, :], pq1[0:TT, 0:288].rearrange("p (h d) -> p h d", h=3), [pq1], [qtok])
                yield
                cos8 = cs_[0:TT, 0:16].unsqueeze(1).to_broadcast([TT, 8, 16])
                sin8 = cs_[0:TT, 16:32].unsqueeze(1).to_broadcast([TT, 8, 16])
                K.tt('dve', rt[0:TT, 0, :, :], qtok[0:TT, :, 64:80], cos8, ALU.mult, [qtok, cs_], [rt])
                K.tt('dve', rt[0:TT, 1, :, :], qtok[0:TT, :, 80:96], sin8, ALU.mult, [qtok, cs_], [rt])
                K.tt('dve', rt[0:TT, 2, :, :], qtok[0:TT, :, 80:96], cos8, ALU.mult, [qtok, cs_], [rt])
                K.tt('dve', rt[0:TT, 3, :, :], qtok[0:TT, :, 64:80], sin8, ALU.mult, [qtok, cs_], [rt])
                K.tt('dve', qtok[0:TT, :, 64:80], rt[0:TT, 0, :, :], rt[0:TT, 1, :, :], ALU.subtract, [rt], [qtok])
                K.tt('dve', qtok[0:TT, :, 80:96], rt[0:TT, 2, :, :], rt[0:TT, 3, :, :], ALU.add, [rt], [qtok])
                K.tt('dve', qsq[0:TT, :, :], qtok[0:TT, :, :], qtok[0:TT, :, :], ALU.mult, [qtok], [qsq])
                K.red('dve', st[0:TT, 8:16], qsq[0:TT, :, :], ALU.add, [qsq], [st])
                K.ts('dve', st[0:TT, 8:16], st[0:TT, 8:16], 1.0 / 96.0, ALU.mult, [st], [st], s2=EPS, op1=ALU.add)
                rsqrt_inplace(st[0:TT, 8:16], st, TT)
                K.tt('dve', qtok[0:TT, :, :], qtok[0:TT, :, :], st[0:TT, 8:16].unsqueeze(2).to_broadcast([TT, 8, 96]), ALU.mult, [qtok, st], [qtok])
                K.tt('dve', qnb[0:TT, :, :], qtok[0:TT, :, :], gqk[0:TT, :].unsqueeze(1).to_broadcast([TT, 8, 96]), ALU.mult, [qtok, gqk], [qnb])
                yield
                ptt = prb.next()
                pttv = ptt[:, :].bitcast(BF16)
                for h in range(8):
                    K.tr(pttv[0:96, h * TT:(h + 1) * TT], qnb[0:TT, h, :], ident_b[0:TT, 0:TT], [qnb, ident_b], [ptt])
                K.copy('act', qT[:, :, c0:c0 + TT], pttv[0:96, 0:8 * TT].rearrange("p (h t) -> p h t", h=8), [ptt], [qT])

            yield

        def gen_attn(blk):
            t0 = blk * TB
            qT, obT = qTs[blk % 2], obTs[blk % 2]
            if s.i == 0:
                vis = list(range(0, (t0 + TB) // 128))
            else:
                vis = list(range(NT))
            items = [(h, vi, kt) for h in range(8) for vi, kt in enumerate(vis)]
            DPIPE = 2
            pos_ = {}
            inflight = {}

            def geom(kt):
                if s.i == 0:
                    r_ = kt - t0 // 128
                    diag = r_ >= 0
                    return 128, diag, (128 * r_ if diag else 0)
                return (128 if kt < nPT else L), False, 0

            def emit_qk(it):
                h, vi, kt = it
                nk, diag, q0 = geom(kt)
                ps_ = pra.next()
                K.mm(ps_[0:nk, 0:TB - q0], kT[:, h, kt * 128:kt * 128 + nk], qT[:, h, q0:TB], True, True, [kTs[kt], qT], [ps_])
                inflight[it] = ps_

            def emit_pv(it):
                h, vi, kt = it
                nk, diag, q0 = geom(kt)
                nq = TB - q0
                ps_ = inflight.pop(it)
                if vi == 0:
                    pos_[h] = po_rot.next()
                po_ = pos_[h]
                pt_ = pT_rot.next()
                K.act(pt_[0:nk, 0:nq], ps_[0:nk, 0:nq], AF.Exp, [ps_, rkSs[kt], negM], [pt_], scale=rkS[0:nk, kt, h:h + 1], bias=negM[0:nk, 0:1])
                if diag:
                    K.memset('pool', pt_[64:128, 0:64], 0.0, [pt_])
                K.mm(po_[0:65, q0:TB], vA[0:nk, kt, h, 0:65], pt_[0:nk, 0:nq], vi == 0, vi == len(vis) - 1, [vAs[kt], pt_], [po_])
                if vi == len(vis) - 1:
                    oa, rd = oas[h % 2], rds[h % 2]
                    K.copy('act', oa[:, :], po_[0:65, 0:TB], [po_], [oa])
                    K.op('dve', lambda e: e.reciprocal(out=rd[64:65, :], in_=oa[64:65, :]), [oa], [rd])
                    K.mm(po_[0:64, 256:256 + TB], sel65[0:65, :], rd[0:65, :], True, True, [sel65, rd], [po_])
                    K.tt('dve', obT[:, h, :], oa[0:64, :], po_[0:64, 256:256 + TB], ALU.mult, [oa, po_], [obT])

            for i_ in range(len(items) + DPIPE):
                if i_ < len(items):
                    emit_qk(items[i_])
                if i_ >= DPIPE:
                    emit_pv(items[i_ - DPIPE])
                yield


        def gen_attn_sample(blk):
            qT, obT = qTs[0], obTs[0]
            vis = list(range(NT))
            po_ = po_rot.next()
            pq_rot = Rot(pTq)
            for vi, kt in enumerate(vis):
                nk = 128 if kt < nPT else L
                ps_ = pra.next()
                for h in range(8):
                    K.mm(ps_[0:nk, h * 16:(h + 1) * 16], kT[:, h, kt * 128:kt * 128 + nk], qT[:, h, 0:16], True, True, [kTs[kt], qT], [ps_])
                sc_ = scs[vi % 2]
                K.tt('dve', sc_[0:nk, :, :], ps_[0:nk, 0:128].rearrange("p (h q) -> p h q", h=8), rkS[0:nk, kt, :].unsqueeze(2).to_broadcast([nk, 8, 16]), ALU.mult, [ps_, rkSs[kt]], [sc_])
                pt_ = pq_rot.next()
                K.act(pt_[0:nk, :], sc_[0:nk, :, :].rearrange("p h q -> p (h q)"), AF.Exp, [sc_, negM], [pt_], bias=negM[0:nk, 0:1])
                for h in range(8):
                    K.mm(po_[0:65, h * 16:(h + 1) * 16], vA[0:nk, kt, h, 0:65], pt_[0:nk, h * 16:(h + 1) * 16], vi == 0 and h == 0, vi == len(vis) - 1, [vAs[kt], pt_], [po_])
                yield
            K.copy('act', oaS[:, :], po_[0:65, 0:128], [po_], [oaS])
            K.op('dve', lambda e: e.reciprocal(out=rdS[64:65, :], in_=oaS[64:65, :]), [oaS], [rdS])
            K.mm(po_[0:64, 256:384], sel65[0:65, :], rdS[0:65, :], True, True, [sel65, rdS], [po_])
            K.tt('dve', obT[:, :, :].rearrange("p h q -> p (h q)"), oaS[0:64, :], po_[0:64, 256:384], ALU.mult, [oaS, po_], [obT])
            yield

        def gen_merge(blk):
            t0 = blk * TB
            obT = obTs[blk % 2]
            def ld_f(f_):
                K.dma('sp', mal[f_ % 2][:, :], s.maT.h[:, f_, t0:t0 + TB], R=[s.maT], W=[mal[f_ % 2]])
                K.dma('sp', gbl[f_ % 2][:, :], s.gbT.h[:, f_, t0:t0 + TB], R=[s.gbT], W=[gbl[f_ % 2]])
            ld_f(0)
            ld_f(1)
            K.dma('sp', xt[0][:, :], s.x[t0:t0 + TT, :], R=[s.xt], W=[xt[0]])
            for f in range(8):
                ml = mal[f % 2]
                py = prb.next()
                for h in range(8):
                    K.mm(py[:, 0:TB], w_mo[:, h, f * 128:(f + 1) * 128], obT[:, h, :], h == 0, h == 7, [w_mo, obT], [py])
                gl_ = gbl[f % 2]
                K.tt('dve', mbt[:, :], gl_[:, :], py[:, 0:TB], ALU.mult, [gl_, py], [mbt])
                K.tt('pool', mergedT[:, f, :], mbt[:, :], ml[:, :], ALU.add, [mbt, ml], [mergedT])
                if f + 2 < 8:
                    ld_f(f + 2)
                yield
            for ti in range(nT):
                c0 = ti * TT
                xtile = xt[ti % 2]
                if ti > 0:
                    K.dma('sp', xtile[:, :], s.x[t0 + c0:t0 + c0 + TT, :], R=[s.xt], W=[xtile])
                for cb in range(2):
                    pw = prb.next()
                    for k in range(8):
                        K.mm(pw[0:TT, 0:512], mergedT[:, k, c0:c0 + TT], w_ob[:, k, cb * 512:(cb + 1) * 512], k == 0, k == 7, [mergedT, w_ob], [pw])
                    K.tt('dve', x1t[:, cb * 512:(cb + 1) * 512], pw[0:TT, 0:512], g1h_bc[0:TT, cb * 512:(cb + 1) * 512], ALU.mult, [pw, g1h_bc], [x1t])
                K.tt('dve', x1t[:, :], x1t[:, :], xtile[:, :], ALU.add, [x1t, xtile], [x1t])
                K.dma('sp', s.x1.h[t0 + c0:t0 + c0 + TT, :], x1t[:, :], R=[x1t], W=[s.x1])
                yield
                K.memset('pool', st[0:TT, 4:5], 0.0, [st])
                K.act(h2b[:, :], x1t[:, :], AF.Square, [x1t], [h2b, st], accum=st[0:TT, 4:5])
                K.ts('dve', st[0:TT, 5:6], st[0:TT, 4:5], 1.0 / D, ALU.mult, [st], [st], s2=EPS, op1=ALU.add)
                rsqrt_inplace(st[0:TT, 5:6], st, TT)
                K.stt('dve', xtile[:, :], x1t[:, :], st[0:TT, 5:6], gmod2_bc[0:TT, :], ALU.mult, ALU.mult, [x1t, st, gmod2_bc], [xtile])
                K.tt('pool', h2b[:, :], xtile[:, :], shift2_bc[0:TT, :], ALU.add, [xtile, shift2_bc], [h2b])
                yield
                ph = prb.next()
                phv = ph[:, :].bitcast(BF16)
                for k in range(8):
                    K.tr(phv[:, k * TT:(k + 1) * TT], h2b[:, k * 128:(k + 1) * 128], ident_b[0:TT, 0:TT], [h2b, ident_b], [ph])
                K.copy('act', h2Tb[:, :, c0:c0 + TT], phv[:, 0:8 * TT].rearrange("p (k t) -> p k t", k=8), [ph], [h2Tb])
            K.dma('sp', s.h2T.h[:, :, t0:t0 + TB], h2Tb[:, :, :], R=[h2Tb], W=[s.h2T])

            yield

        def run_all(g):
            for _ in g:
                pass

        def chain(*gs):
            for g in gs:
                if g is not None:
                    yield from g

        def interleave(ga, gb_, ra=1):
            a_alive, b_alive = True, True
            while a_alive or b_alive:
                if a_alive:
                    for _ in range(ra):
                        try:
                            next(ga)
                        except StopIteration:
                            a_alive = False
                            break
                if b_alive:
                    try:
                        next(gb_)
                    except StopIteration:
                        b_alive = False

        def load_hT(blk_):
            K.dma('sp', hTbs[blk_ % 2][:, :, :], s.hT.h[:, :, blk_ * TB:(blk_ + 1) * TB], R=[s.hT], W=[hTbs[blk_ % 2]])

        load_hT(0)
        run_all(gen_tiles(0))
        for blk in range(nblk):
            if blk + 1 < nblk:
                load_hT(blk + 1)
            others = chain(gen_merge(blk - 1) if blk > 0 else None, gen_tiles(blk + 1) if blk + 1 < nblk else None)
            n_items = 8 * ((blk * TB + TB) // 128 if s.i == 0 else NT)
            interleave(gen_attn(blk) if s.i == 0 else gen_attn_sample(blk), others, ra=max(1, n_items // 28))
        run_all(gen_merge(nblk - 1))

    for s in seqs:
        if STAGE in ('all', 'b') and str(s.i) in SEQS:
            pass_B(s)
    K.barrier()

    AR.reset()
    wf1 = AR.alloc("wf1", [128, 8, 4096], BF16)
    wf2 = AR.alloc("wf2", [128, 32, 1024], BF16)
    wf1q = [Tile(wf1.h, "wf1q%d" % i) for i in range(4)]
    wf2q = [Tile(wf2.h, "wf2q%d" % i) for i in range(4)]
    if STAGE == 'all':
        for q4 in range(4):
            K.dma('pool', wf1[:, :, q4 * 1024:(q4 + 1) * 1024], w_ff1.h[:, q4 * 1024:(q4 + 1) * 1024].rearrange("(k p) n -> p k n", p=128), R=[w_ff1], W=[wf1q[q4]])
        for q4 in range(4):
            K.dma('pool', wf2[:, q4 * 8:(q4 + 1) * 8, :], w_ff2.h[q4 * 1024:(q4 + 1) * 1024, :].rearrange("(j p) n -> p j n", p=128), R=[w_ff2], W=[wf2q[q4]])
    g2_bc = AR.alloc("g2_bc", [128, 1024], F32)
    markC = AR.mark()

    def pass_C(s):
        if s.i > 0:
            K.barrier()
        AR.reset(markC)
        L, TT = s.L, s.TT
        TB = 512 if s.i == 0 else 16
        nT = TB // TT
        nblk = L // TB
        load_mod_bc(g2_bc, s.i, 5)
        h2l = [AR.alloc("h2l%d" % i, [128, 8, TB], BF16) for i in range(2)]
        hid = AR.alloc("hid", [128, 32, TB], BF16)
        rl = [AR.alloc("rl%d" % i, [128, TB], BF16) for i in range(2)]
        x1l = [AR.alloc("x1l%d" % i, [TT, 1024], F32) for i in range(2)]
        yt = [AR.alloc("yt0", [TT, 1024], F32)] * 2
        pf_rot = Rot(banks[0:4])
        pw_rot = Rot(banks[4:8])
        K.dma('sp', h2l[0][:, :, :], s.h2T.h[:, :, 0:TB], R=[s.h2T], W=[h2l[0]])
        for blk in range(nblk):
            t0 = blk * TB
            hb = h2l[blk % 2]
            if blk + 1 < nblk:
                K.dma('sp', h2l[(blk + 1) % 2][:, :, :], s.h2T.h[:, :, t0 + TB:t0 + 2 * TB], R=[s.h2T], W=[h2l[(blk + 1) % 2]])
            for j in range(32):
                pf = pf_rot.next()
                for k in range(8):
                    K.mm(pf[:, 0:TB], wf1[:, k, j * 128:(j + 1) * 128], hb[:, k, :], k == 0, k == 7, [wf1q[j // 8], hb], [pf])
                r_ = rl[j % 2]
                K.act(r_[:, :], pf[:, 0:TB], AF.Relu, [pf], [r_])
                K.tt('pool' if j % 2 == 0 else 'dve', hid[:, j, :], r_[:, :], r_[:, :], ALU.mult, [r_], [hid])
            for ti in range(nT):
                c0 = ti * TT
                xl = x1l[ti % 2]
                yo = yt[ti % 2]
                K.dma('sp', xl[:, :], s.x1.h[t0 + c0:t0 + c0 + TT, :], R=[s.x1], W=[xl])
                for cb in range(2):
                    pw = pw_rot.next()
                    for j in range(32):
                        K.mm(pw[0:TT, 0:512], hid[:, j, c0:c0 + TT], wf2[:, j, cb * 512:(cb + 1) * 512], j == 0, j == 31, [hid, wf2q[j // 8]], [pw])
                    K.tt('dve', yo[:, cb * 512:(cb + 1) * 512], pw[0:TT, 0:512], g2_bc[0:TT, cb * 512:(cb + 1) * 512], ALU.mult, [pw, g2_bc], [yo])
                K.tt('pool', yo[:, :], yo[:, :], xl[:, :], ALU.add, [yo, xl], [yo])
                K.dma('sp', s.y[t0 + c0:t0 + c0 + TT, :], yo[:, :], R=[yo], W=[s.yt], is_output=True)

    for s in seqs:
        if STAGE == 'all' and str(s.i) in SEQS:
            pass_C(s)

    return nc, K, seqs, locals()


def _rope_table():
    inv = (10000.0 ** (-np.arange(0, 32, 2, dtype=np.float32) / np.float32(32))).astype(np.float32)
    pos = np.arange(2064, dtype=np.float32)
    ang = (pos[:, None] * inv[None, :]).astype(np.float32)
    return np.concatenate([np.cos(ang), np.sin(ang)], axis=1).astype(np.float32)


def make_in_maps(inp):
    f = lambda a: np.ascontiguousarray(np.asarray(a, dtype=np.float32))
    shared = {
        "ada_w": f(inp["ada_w"][0]), "ada_b": f(inp["ada_b"]), "norm1_g": f(inp["norm1_g"]),
        "w_in": f(inp["w_in"][0]), "conv_w": f(inp["gdn_conv_w"][0]), "a_log": f(inp["gdn_a_log"]),
        "dt_bias": f(inp["gdn_dt_bias"]), "gdn_norm_g": f(inp["gdn_norm_g"]),
        "w_gdn_out": f(inp["w_gdn_out"][0]), "q_norm_g": f(inp["mla_q_norm_g"]), "w_uq": f(inp["w_uq"][0]),
        "kv_norm_g": f(inp["mla_kv_norm_g"]), "w_ukv": f(inp["w_ukv"][0]), "qh_g": f(inp["q_head_norm_g"]),
        "kh_g": f(inp["k_head_norm_g"]), "w_mla_out": f(inp["w_mla_out"][0]), "w_o": f(inp["w_o"][0]),
        "norm2_g": f(inp["norm2_g"]), "w_ff1": f(inp["w_ff1"][0]), "w_ff2": f(inp["w_ff2"][0]),
        "rope_cs": _rope_table(),
    }
    maps = []
    for c in range(8):
        m = dict(shared)
        m["xp"] = f(inp["x_prompt"][c])
        m["xs"] = f(inp["x_sample"][2 * c:2 * c + 2])
        m["c3"] = f(np.concatenate([np.asarray(inp["c_prompt"])[c:c + 1], np.asarray(inp["c_sample"])[2 * c:2 * c + 2]], axis=0))
        m["ckv_c"] = f(inp["cache_mla_ckv"][0, 2 * c:2 * c + 2])
        m["kr_c"] = f(inp["cache_mla_krope"][0, 2 * c:2 * c + 2])
        m["st_c"] = f(inp["state_gdn"][0, 2 * c:2 * c + 2])
        m["cv_c"] = f(inp["state_gdn_conv"][0, 2 * c:2 * c + 2])
        maps.append(m)
    return maps


def kernel(**inputs):
    nc, K, seqs, _ = build_program(debug=False)
    K.finish()
    maps = make_in_maps(inputs)
    res = run_bass_kernel_spmd(nc, maps, core_ids=list(range(8)))
    R = res.results
    g = lambda n: [np.asarray(R[c][n], dtype=np.float32) for c in range(8)]
    y_prompt = np.stack(g("y_p"), axis=0)
    y_sample = np.concatenate(g("y_s"), axis=0)
    ckv_p = np.stack(g("ckv_p"), axis=0)[None]
    kr_p = np.stack(g("kr_p"), axis=0)[None]
    st_p = np.stack(g("st_p"), axis=0)[None]
    cv_p = np.stack(g("cv_p"), axis=0)[None]
    ckv_s = np.concatenate(g("ckv_s"), axis=0)[None]
    kr_s = np.concatenate(g("kr_s"), axis=0)[None]
    st_s = np.concatenate(g("st_s"), axis=0)[None]
    cv_s = np.concatenate(g("cv_s"), axis=0)[None]
    return (y_prompt, y_sample, ckv_p, kr_p, st_p, cv_p, ckv_s, kr_s, st_s, cv_s)
```

```python
import os
import numpy as np
import concourse.bass as bass
import concourse.mybir as mybir
from concourse.bass_utils import run_bass_kernel_spmd

F32 = mybir.dt.float32
BF16 = mybir.dt.bfloat16
AF = mybir.ActivationFunctionType
ALU = mybir.AluOpType
AX = mybir.AxisListType

ENGS = ['pe', 'act', 'dve', 'pool', 'sp']
ENGMAP = {'pe': 'tensor', 'act': 'scalar', 'dve': 'vector', 'pool': 'gpsimd', 'sp': 'sync'}

D = 1024
H = 8
IN_DIM = 4784
Q0, K0, V0, Z0, A0, B0, CQ0, CKV0, KR0, GA0, GB0 = 0, 512, 1024, 1536, 2048, 2056, 2064, 2448, 2704, 2736, 3760
EPS = 1e-6
NEG = -30000.0


class Buf:
    __slots__ = ('name', 'w', 'r')

    def __init__(self, name=''):
        self.name = name
        self.w = {}
        self.r = {}


class Tile:
    def __init__(self, h, name=''):
        self.h = h
        self.b = Buf(name)

    def __getitem__(self, k):
        return self.h[k]


class MK:
    def __init__(self, nc, n_dma_sems=40):
        self.nc = nc
        self.ops = {e: [] for e in ENGS}
        self.cnt = {e: 0 for e in ENGS}
        self.sem = {e: nc.alloc_semaphore('s_' + e) for e in ['pe', 'act', 'dve', 'pool']}
        self.dsem = [nc.alloc_semaphore('d%d' % i) for i in range(n_dma_sems)]
        self.dcnt = [0] * n_dma_sems
        self.dnext = 0
        self.dnext_pool = 0
        self.seen = {e: {} for e in ENGS}
        self.out_tokens = []

    def _semof(self, k):
        if isinstance(k, tuple):
            return self.dsem[k[1]]
        return self.sem[k]

    def _deps(self, eng, reads, writes):
        need = {}
        for t in reads:
            for k, v in t.b.w.items():
                if need.get(k, 0) < v:
                    need[k] = v
        strict = eng != 'pe' and os.environ.get('KSTRICT', '1') == '1'
        for t in writes:
            for k, v in t.b.w.items():
                if (strict or k != eng) and need.get(k, 0) < v:
                    need[k] = v
            for k, v in t.b.r.items():
                if (strict or k != eng) and need.get(k, 0) < v:
                    need[k] = v
        waits = []
        seen = self.seen[eng]
        for k, v in need.items():
            if seen.get(k, 0) >= v:
                continue
            seen[k] = v
            waits.append((k, v))
        return waits

    def _commit(self, tok, reads, writes):
        k, v = tok
        for t in reads:
            if t.b.r.get(k, 0) < v:
                t.b.r[k] = v
        for t in writes:
            if t.b.w.get(k, 0) < v:
                t.b.w[k] = v

    def op(self, eng, fn, R=(), W=()):
        waits = self._deps(eng, R, W)
        self.cnt[eng] += 1
        tok = (eng, self.cnt[eng])
        self._commit(tok, R, W)
        self.ops[eng].append((fn, waits, eng))
        return tok

    def dma(self, queue, out, in_, R=(), W=(), is_output=False, slow=False):
        if queue == 'pool':
            i = self.dnext_pool
            self.dnext_pool = (self.dnext_pool + 1) % 8
        else:
            i = 8 + self.dnext
            self.dnext = (self.dnext + 1) % (len(self.dsem) - 8)
        waits = self._deps(queue, R, W)
        key = ('d', i)
        prev = self.dcnt[i]
        if prev > 0 and self.seen[queue].get(key, 0) < prev:
            self.seen[queue][key] = prev
            waits.append((key, prev))
        self.dcnt[i] += 16
        tok = (key, self.dcnt[i])
        self._commit(tok, R, W)
        if slow:
            fn = lambda e: e.dma_start(out=out, in_=in_, allow_slow_non_contiguous=True)
        else:
            fn = lambda e: e.dma_start(out=out, in_=in_)
        self.ops[queue].append((fn, waits, key))
        if is_output:
            self.out_tokens.append(tok)
        return tok

    def barrier(self):
        allk = [(e, self.cnt[e]) for e in ['pe', 'act', 'dve', 'pool'] if self.cnt[e] > 0]
        allk += [(('d', i), self.dcnt[i]) for i in range(len(self.dsem)) if self.dcnt[i] > 0]
        for e in ENGS:
            waits = []
            for k, v in allk:
                if k == e:
                    continue
                if self.seen[e].get(k, 0) < v:
                    self.seen[e][k] = v
                    waits.append((k, v))
            if waits:
                self.ops[e].append((None, waits, None))

    def mm(self, out, lhsT, rhs, start, stop, R, W):
        return self.op('pe', lambda e: e.matmul(out, lhsT=lhsT, rhs=rhs, start=start, stop=stop), R, W)

    def tr(self, out, in_, ident, R, W):
        return self.op('pe', lambda e: e.transpose(out, in_, ident), R, W)

    def act(self, out, in_, func, R, W, scale=1.0, bias=0.0, accum=None):
        if accum is None:
            return self.op('act', lambda e: e.activation(out=out, in_=in_, func=func, bias=bias, scale=scale), R, W)
        return self.op('act', lambda e: e.activation(out=out, in_=in_, func=func, bias=bias, scale=scale, accum_out=accum), R, W)

    def tt(self, eng, out, in0, in1, op, R, W):
        return self.op(eng, lambda e: e.tensor_tensor(out=out, in0=in0, in1=in1, op=op), R, W)

    def ts(self, eng, out, in0, s1, op0, R, W, s2=None, op1=None):
        if eng == 'pool':
            eng = 'dve'
        if op1 is None:
            return self.op(eng, lambda e: e.tensor_scalar(out=out, in0=in0, scalar1=s1, scalar2=None, op0=op0), R, W)
        return self.op(eng, lambda e: e.tensor_scalar(out=out, in0=in0, scalar1=s1, scalar2=s2, op0=op0, op1=op1), R, W)

    def stt(self, eng, out, in0, scalar, in1, op0, op1, R, W):
        if eng == 'pool':
            eng = 'dve'
        if not hasattr(scalar, 'shape'):
            j = self.cvals.index(float(scalar))
            bp = int(out.base_partition())
            n = int(out.shape[0])
            scalar = self.cst[bp:bp + n, j:j + 1]
            R = list(R) + [self.cst]
        return self.op(eng, lambda e: e.scalar_tensor_tensor(out=out, in0=in0, scalar=scalar, in1=in1, op0=op0, op1=op1), R, W)

    def copy(self, eng, out, in_, R, W):
        if eng == 'act':
            return self.act(out, in_, AF.Copy, R, W)
        return self.op(eng, lambda e: e.tensor_copy(out=out, in_=in_), R, W)

    def memset(self, eng, ap, val, W):
        return self.op(eng, lambda e: e.memset(ap, val), (), W)

    def red(self, eng, out, in_, op, R, W):
        return self.op(eng, lambda e: e.tensor_reduce(out=out, in_=in_, axis=AX.X, op=op), R, W)

    def finish(self):
        need = {}
        for k, v in self.out_tokens:
            if need.get(k, 0) < v:
                need[k] = v
        final_waits = list(need.items())
        nc = self.nc
        with nc.Block() as block:
            for e in ENGS:
                def body(eng, e=e):
                    for fn, waits, inc in self.ops[e]:
                        for (k, v) in waits:
                            eng.wait_ge(self._semof(k), v)
                        if fn is None:
                            continue
                        ins = fn(eng)
                        if isinstance(inc, tuple):
                            ins.then_inc(self.dsem[inc[1]], 16)
                        else:
                            ins.then_inc(self.sem[inc], 1)
                    if e == 'sp':
                        for (k, v) in final_waits:
                            eng.wait_ge(self._semof(k), v)
                getattr(block, ENGMAP[e])(body)


class Arena:
    def __init__(self, nc, base, limit):
        self.nc = nc
        self.base = base
        self.ptr = base
        self.limit = limit
        self.n = 0

    def alloc(self, name, shape, dtype):
        nbytes = int(np.prod(shape[1:])) * (2 if dtype == BF16 else 4)
        nbytes = (nbytes + 31) // 32 * 32
        off = self.ptr
        self.ptr += nbytes
        self.peak = max(getattr(self, 'peak', 0), self.ptr)
        assert self.ptr <= self.limit, "SBUF arena overflow %s: %d > %d" % (name, self.ptr, self.limit)
        self.n += 1
        h = self.nc.alloc_sbuf_tensor_at("%s_%d_%d" % (name, off, self.n), list(shape), dtype, offset=off)
        t = Tile(h, name)
        t.off = off
        return t

    def alias(self, name, shape, dtype, offset, share):
        self.n += 1
        h = self.nc.alloc_sbuf_tensor_at("%s_%d_%d" % (name, offset, self.n), list(shape), dtype, offset=offset)
        t = Tile(h, name)
        t.b = share.b
        return t

    def mark(self):
        return self.ptr

    def reset(self, mark=None):
        if os.environ.get('KARENA'):
            print('arena peak', getattr(self, 'peak', 0) - self.base, 'of', self.limit - self.base)
        if mark is None:
            self.peak = 0
        self.ptr = self.base if mark is None else mark


class Seq:
    pass


def build_program(debug=False):
    STAGE = os.environ.get('KSTAGE', 'all')
    SEQS = os.environ.get('KSEQS', '012')
    nc = bass.Bass("TRN2", target_bir_lowering=False)
    K = MK(nc)

    def din(name, shape, dt=F32):
        return Tile(nc.dram_tensor(name, list(shape), dt, kind="ExternalInput").ap(), name)

    def dout(name, shape, dt=F32):
        return Tile(nc.dram_tensor(name, list(shape), dt, kind="ExternalOutput").ap(), name)

    def dscr(name, shape, dt=F32):
        kind = "ExternalOutput" if debug else "Internal"
        return Tile(nc.dram_tensor(name, list(shape), dt, kind=kind).ap(), name)

    xp = din("xp", [2048, D])
    xs = din("xs", [2, 16, D])
    c3 = din("c3", [3, D])
    ckv_c = din("ckv_c", [2, 2048, 256])
    kr_c = din("kr_c", [2, 2048, 32])
    st_c = din("st_c", [2, H, 64, 64])
    cv_c = din("cv_c", [2, 3, 1536])
    ada_w = din("ada_w", [D, 6144])
    ada_b = din("ada_b", [1, 6144])
    norm1_g = din("norm1_g", [1, D])
    w_in = din("w_in", [D, IN_DIM])
    conv_w = din("conv_w", [4, 1536])
    a_log = din("a_log", [1, H])
    dt_bias = din("dt_bias", [1, H])
    gdn_norm_g = din("gdn_norm_g", [1, 64])
    w_gdn_out = din("w_gdn_out", [512, D])
    q_norm_g = din("q_norm_g", [1, 384])
    w_uq = din("w_uq", [384, 768])
    kv_norm_g = din("kv_norm_g", [1, 256])
    w_ukv = din("w_ukv", [256, 1024])
    qh_g = din("qh_g", [1, 96])
    kh_g = din("kh_g", [1, 96])
    w_mla_out = din("w_mla_out", [512, D])
    w_o = din("w_o", [D, D])
    norm2_g = din("norm2_g", [1, D])
    w_ff1 = din("w_ff1", [D, 4096])
    w_ff2 = din("w_ff2", [4096, D])
    rope_cs = din("rope_cs", [2064, 32])

    y_p = dout("y_p", [2048, D])
    y_s = dout("y_s", [2, 16, D])
    ckv_p = dout("ckv_p", [2048, 256])
    kr_p = dout("kr_p", [2048, 32])
    st_p = dout("st_p", [H, 64, 64])
    cv_p = dout("cv_p", [3, 1536])
    ckv_s = dout("ckv_s", [2, 16, 256])
    kr_s = dout("kr_s", [2, 16, 32])
    st_s = dout("st_s", [2, H, 64, 64])
    cv_s = dout("cv_s", [2, 3, 1536])

    mod_d = dscr("mod_d", [3, 6144])

    seqs = []
    for si in range(3):
        s = Seq()
        s.i = si
        s.L = 2048 if si == 0 else 16
        s.TT = 128 if si == 0 else 16
        s.nsub = 2 if si == 0 else 1
        s.C = s.TT // s.nsub
        s.TB = 512 if si == 0 else 16
        s.past = 0 if si == 0 else 2048
        s.x = xp.h if si == 0 else xs.h[si - 1]
        s.xt = xp if si == 0 else xs
        s.y = y_p.h if si == 0 else y_s.h[si - 1]
        s.yt = y_p if si == 0 else y_s
        s.ckv_o = ckv_p.h if si == 0 else ckv_s.h[si - 1]
        s.ckv_ot = ckv_p if si == 0 else ckv_s
        s.kr_o = kr_p.h if si == 0 else kr_s.h[si - 1]
        s.kr_ot = kr_p if si == 0 else kr_s
        s.st_o = st_p.h if si == 0 else st_s.h[si - 1]
        s.st_ot = st_p if si == 0 else st_s
        s.cv_o = cv_p.h if si == 0 else cv_s.h[si - 1]
        s.cv_ot = cv_p if si == 0 else cv_s
        s.hT = dscr("hT%d" % si, [128, 8, s.L], BF16)
        s.maT = dscr("maT%d" % si, [128, 8, s.L], F32)
        s.gbT = dscr("gbT%d" % si, [128, 8, s.L], F32)
        s.x1 = dscr("x1_%d" % si, [s.L, D], F32)
        s.h2T = dscr("h2T%d" % si, [128, 8, s.L], BF16)
        seqs.append(s)

    sb0 = (int(nc.sbuf_base) + 63) // 64 * 64
    sb1 = int(nc.sbuf_top) // 64 * 64
    CONST = Arena(nc, sb0, sb0 + 12 * 1024)
    AR = Arena(nc, sb0 + 12 * 1024, sb1)

    banks = [Tile(nc.alloc_psum_tensor("ps%d" % i, [128, 512], F32), "ps%d" % i) for i in range(8)]

    class Rot:
        def __init__(self, items):
            self.items = items
            self.i = 0

        def next(self):
            t = self.items[self.i]
            self.i = (self.i + 1) % len(self.items)
            return t

    K.cvals = [1.0, 0.5, -1.0, 0.0, -2.0]
    K.cst = CONST.alloc("cst", [128, 8], F32)
    for j_, v_ in enumerate(K.cvals):
        K.op('pool', lambda e, j_=j_, v_=v_: e.memset(K.cst[:, j_:j_ + 1], v_), (), [K.cst])
    ident_f = CONST.alloc("ident_f", [128, 128], F32)
    ident_b = CONST.alloc("ident_b", [128, 128], BF16)
    ones_f = CONST.alloc("ones_f", [128, 128], F32)
    U_full = CONST.alloc("U_full", [128, 128], F32)
    U_blk = CONST.alloc("U_blk", [128, 128], F32)
    SL_full = CONST.alloc("SL_full", [128, 128], F32)
    nm_p = CONST.alloc("nm_p", [128, 384], F32)
    nm_s = CONST.alloc("nm_s", [128, 384], F32)
    mhalf = CONST.alloc("mhalf", [128, 1], F32)
    negM = CONST.alloc("negM", [128, 1], F32)
    gqk = CONST.alloc("gqk", [128, 96], F32)
    sel3 = CONST.alloc("sel3", [3, 3, 128], F32)

    def aff(out, in_, pattern, cmp, fill, base, cm, R, W):
        return K.op('pool', lambda e: e.affine_select(out=out, in_=in_, pattern=pattern, compare_op=cmp, fill=fill, base=base, channel_multiplier=cm), R, W)

    K.memset('pool', ones_f[:, :], 1.0, [ones_f])
    K.memset('pool', mhalf[:, :], -0.5, [mhalf])
    K.memset('pool', ident_f[:, :], 1.0, [ident_f])
    aff(ident_f[:, :], ident_f[:, :], [[-1, 128]], ALU.is_equal, 0.0, 0, 1, [ident_f], [ident_f])
    K.copy('dve', ident_b[:, :], ident_f[:, :], [ident_f], [ident_b])
    K.memset('pool', U_full[:, :], 1.0, [U_full])
    aff(U_full[:, :], U_full[:, :], [[1, 128]], ALU.is_ge, 0.0, 0, -1, [U_full], [U_full])
    K.copy('pool', U_blk[:, :], U_full[:, :], [U_full], [U_blk])
    K.memset('pool', U_blk[0:64, 64:128], 0.0, [U_blk])
    K.memset('pool', SL_full[:, :], 1.0, [SL_full])
    aff(SL_full[:, :], SL_full[:, :], [[-1, 128]], ALU.is_gt, 0.0, 0, 1, [SL_full], [SL_full])
    for nm, blk in ((nm_p, True), (nm_s, False)):
        K.memset('pool', nm[:, :], 0.0, [nm])
        aff(nm[:, 0:128], nm[:, 0:128], [[-1, 128]], ALU.is_gt, NEG, 0, 1, [nm], [nm])
        aff(nm[:, 128:256], nm[:, 128:256], [[1, 128]], ALU.is_gt, NEG, 0, -1, [nm], [nm])
        aff(nm[:, 256:384], nm[:, 256:384], [[1, 128]], ALU.is_ge, NEG, 0, -1, [nm], [nm])
        if blk:
            K.memset('pool', nm[64:128, 0:64], NEG, [nm])
            K.memset('pool', nm[0:64, 192:256], NEG, [nm])
    K.memset('pool', sel3[:, :, :], 1.0, [sel3])
    for r in range(3):
        aff(sel3[:, r, :], sel3[:, r, :], [[0, 128]], ALU.is_equal, 0.0, -r, 1, [sel3], [sel3])

    def bcast_load(tile_ap, tile, dram_row_ap, dram_tile, queue='sp'):
        K.dma(queue, tile_ap, dram_row_ap, R=[dram_tile], W=[tile])

    def rsqrt_inplace(ap, tile, n_part):
        shp = list(ap.shape)
        K.tt('pool', ap, ap, mhalf[0:n_part, 0:1].to_broadcast(shp) if len(shp) == 2 else mhalf[0:n_part, 0:1].unsqueeze(2).to_broadcast(shp), ALU.pow, [tile, mhalf], [tile])

    AR.reset()
    cT = AR.alloc("cT", [128, 8, 3], F32)
    sT = AR.alloc("sT", [128, 8, 3], F32)
    sTb = AR.alloc("sTb", [128, 8, 3], BF16)
    adab = AR.alloc("adab", [3, 6144], F32)
    modsb = AR.alloc("modsb", [3, 6144], F32)
    adw = [AR.alloc("adw%d" % i, [128, 8, 512], BF16) for i in range(3)]
    for r in range(3):
        K.dma('sp', cT[:, :, r], c3.h[r].rearrange("(k p) -> p k", p=128), R=[c3], W=[cT], slow=True)
    K.dma('sp', adab[:, :], ada_b.h[0].partition_broadcast(3), R=[ada_b], W=[adab])
    K.act(sT[:, :, :], cT[:, :, :], AF.Tanh, [cT], [sT], scale=0.5)
    K.stt('dve', sT[:, :, :], sT[:, :, :], 1.0, cT[:, :, :], ALU.add, ALU.mult, [sT, cT], [sT])
    K.ts('dve', sTb[:, :, :], sT[:, :, :], 0.5, ALU.mult, [sT], [sTb])
    adw_rot = Rot(adw)
    ps_rot = Rot(banks[0:2])
    for cb in range(12):
        wt = adw_rot.next()
        K.dma('pool', wt[:, :, :], ada_w.h[:, cb * 512:(cb + 1) * 512].rearrange("(k p) n -> p k n", p=128), R=[ada_w], W=[wt])
        pb = ps_rot.next()
        for k in range(8):
            K.mm(pb[0:3, :], sTb[:, k, :], wt[:, k, :], k == 0, k == 7, [sTb, wt], [pb])
        K.tt('dve', modsb[:, cb * 512:(cb + 1) * 512], pb[0:3, :], adab[:, cb * 512:(cb + 1) * 512], ALU.add, [pb, adab], [modsb])
    K.dma('sp', mod_d.h[:, :], modsb[:, :], R=[modsb], W=[mod_d])
    K.barrier()

    def load_mod_bc(tile, si, idx, queue='sp'):
        K.dma(queue, tile[:, :], mod_d.h[si, idx * 1024:(idx + 1) * 1024].partition_broadcast(128), R=[mod_d], W=[tile])

    AR.reset()
    w_qkv = AR.alloc("w_qkv", [128, 8, 1536], BF16)
    w_z = AR.alloc("w_z", [128, 8, 512], BF16)
    w_ab = AR.alloc("w_ab", [128, 8, 16], BF16)
    w_ga = AR.alloc("w_ga", [128, 8, 1024], BF16)
    w_go = AR.alloc("w_go", [128, 4, 1024], BF16)
    w_gbA = AR.alloc("w_gbA", [128, 8, 1024], BF16)
    w_in_v = w_in.h.rearrange("(k p) n -> p k n", p=128)
    K.dma('pool', w_qkv[:, :, :], w_in_v[:, :, 0:1536], R=[w_in], W=[w_qkv])
    K.dma('pool', w_ab[:, :, :], w_in_v[:, :, A0:A0 + 16], R=[w_in], W=[w_ab])
    K.dma('pool', w_z[:, :, :], w_in_v[:, :, Z0:Z0 + 512], R=[w_in], W=[w_z])
    K.dma('pool', w_ga[:, :, :], w_in_v[:, :, GA0:GA0 + 1024], R=[w_in], W=[w_ga])
    K.dma('pool', w_go[:, :, :], w_gdn_out.h.rearrange("(k p) n -> p k n", p=128), R=[w_gdn_out], W=[w_go])
    K.dma('pool', w_gbA[:, :, :], w_in_v[:, :, GB0:GB0 + 1024], R=[w_in], W=[w_gbA])
    cw = AR.alloc("cw", [128, 4, 12], F32)
    for t_ in range(4):
        K.dma('sp', cw[:, t_, :], conv_w.h[t_].rearrange("(c p) -> p c", p=128), R=[conv_w], W=[cw], slow=True)
    n1g_bc = AR.alloc("n1g_bc", [128, 1024], F32)
    K.dma('sp', n1g_bc[:, :], norm1_g.h[0].partition_broadcast(128), R=[norm1_g], W=[n1g_bc])
    alog_bc = AR.alloc("alog_bc", [128, 8], F32)
    dtb_bc = AR.alloc("dtb_bc", [128, 8], F32)
    gn_bc = AR.alloc("gn_bc", [128, 64], F32)
    K.dma('sp', alog_bc[:, :], a_log.h[0].partition_broadcast(128), R=[a_log], W=[alog_bc])
    K.dma('sp', dtb_bc[:, :], dt_bias.h[0].partition_broadcast(128), R=[dt_bias], W=[dtb_bc])
    K.dma('sp', gn_bc[:, :], gdn_norm_g.h[0].partition_broadcast(128), R=[gdn_norm_g], W=[gn_bc])
    negA = AR.alloc("negA", [128, 8], F32)
    K.act(negA[:, :], alog_bc[:, :], AF.Exp, [alog_bc], [negA])
    K.ts('dve', negA[:, :], negA[:, :], -1.0, ALU.mult, [negA], [negA])
    selT = AR.alloc("selT", [32, 16, 128], F32)
    K.memset('pool', selT[:, :, :], 1.0, [selT])
    for qn in range(4):
        for pr in range(4):
            for hf in range(2):
                base = qn * 8 + 2 * pr + hf
                aff(selT[:, qn * 4 + pr, hf * 64:(hf + 1) * 64], selT[:, qn * 4 + pr, hf * 64:(hf + 1) * 64], [[0, 64]], ALU.is_equal, 0.0, -base, 1, [selT], [selT])

    gmod_bc = AR.alloc("gmod_bc", [128, 1024], F32)
    shift_bc = AR.alloc("shift_bc", [128, 1024], F32)
    markA = AR.mark()

    def pass_A(s):
        if s.i > 0:
            K.barrier()
        AR.reset(markA)
        L, TT, nsub, C = s.L, s.TT, s.nsub, s.C
        TB = 256 if s.i == 0 else 16
        nT = TB // TT
        nblk = L // TB
        nm = nm_p if nsub == 2 else nm_s
        Ublk = U_blk if nsub == 2 else U_full
        nlev = 6 if C == 64 else 4
        load_mod_bc(shift_bc, s.i, 0)
        load_mod_bc(gmod_bc, s.i, 1)
        K.stt('dve', gmod_bc[:, :], gmod_bc[:, :], 1.0, n1g_bc[:, :], ALU.add, ALU.mult, [gmod_bc, n1g_bc], [gmod_bc])
        S = AR.alloc("S", [64, 8, 64], F32)
        Sb = AR.alloc("Sb", [64, 8, 64], BF16)
        Sb0 = AR.alloc("Sb0", [64, 8, 64], BF16)
        if s.i == 0:
            K.memset('pool', S[:, :, :], 0.0, [S])
        else:
            K.dma('sp', S[:, :, :], st_c.h[s.i - 1].rearrange("h d e -> d h e"), R=[st_c], W=[S])
        K.copy('dve', Sb[:, :, :], S[:, :, :], [S], [Sb])
        Sv = S[:, :, :].rearrange("p (a b) e -> p a b e", b=2)
        prec = [AR.alloc("pre%d" % i, [128, TB + 3], F32) for i in range(2)]
        carry = AR.alloc("carry", [128, 3, 12], F32)
        if s.i == 0:
            K.memset('pool', carry[:, :, :], 0.0, [carry])
        else:
            for t_ in range(3):
                K.dma('sp', carry[:, t_, :], cv_c.h[s.i - 1, t_].rearrange("(c p) -> p c", p=128), R=[cv_c], W=[carry], slow=True)
        xt = [AR.alloc("xt%d" % i, [TT, 1024], F32) for i in range(2)]
        xn = [AR.alloc("xn%d" % i, [TT, 1024], BF16) for i in range(2)]
        st1 = AR.alloc("st1", [128, 8], F32)
        hTb = AR.alloc("hTb", [128, 8, TB], BF16)
        acc = AR.alloc("acc", [128, TB], F32)
        tnh = AR.alloc("tnh", [128, TB], F32)
        qkvf = AR.alloc("qkvf", [128, 12, TB], BF16)
        sq = AR.alloc("sq", [128, TB], F32)
        stat = AR.alloc("stat", [TT, nT, 64], F32)
        stat2 = AR.alloc("stat2", [TT, nT, 32], F32)
        statT = AR.alloc("statT", [32, TB], F32)
        varq = AR.alloc("varq", [64, 8, TB], BF16)
        varqg = AR.alloc("varqg", [64, 8, TB], BF16)
        vark = AR.alloc("vark", [64, 8, TB], BF16)
        varkb = AR.alloc("varkb", [64, 8, TB], BF16)
        vtmps = [AR.alloc("vtmp%d" % i, [128, TB], BF16) for i in range(3 if s.i == 0 else 1)]
        vt_rot = Rot(vtmps)
        ktok = AR.alloc("ktok", [TT, 2, 8, 64], BF16)
        vtok = AR.alloc("vtok", [TT, 8, 64], BF16)
        Gu = AR.alloc("Gu", [TT, 8, TT], F32)
        if s.i == 0:
            accs = [acc, AR.alias("acc2", [128, TB], F32, Gu.off, Gu)]
            tnhs = [tnh, AR.alias("tnh2", [128, TB], F32, Gu.off + TB * 4, Gu)]
        else:
            accs, tnhs = [acc, acc], [tnh, tnh]
        dec = AR.alloc("dec", [TT, 8, 3 * TT], F32)
        Lk = [AR.alloc("Lk%d" % i, [TT, 4, TT], F32) for i in range(2)]
        Mk = [AR.alloc("Mk%d" % i, [TT, 4, TT], F32) for i in range(2)]
        Pm = [AR.alloc("Pm%d" % i, [TT, 4, TT], F32) for i in range(2)]
        Pb = AR.alloc("Pb", [TT, 8, TT], BF16)
        QKb = AR.alloc("QKb", [TT, 8, TT], BF16)
        usb = AR.alloc("usb", [TT, 8, 64], F32)
        wTb = AR.alloc("wTb", [64, 8, TT], BF16)
        vnews = [AR.alloc("vnew%d" % i, [TT, 8, 64], BF16) for i in range(nsub)]
        for vn_ in vnews:
            K.memset("pool", vn_[:, :, :], 0.0, [vn_])
        egl = AR.alloc("egl", [64, nsub, 8], F32)
        osb = AR.alloc("osb", [TT, 8, 64], F32)
        osq = AR.alloc("osq", [TT, 8, 64], F32)
        ost = AR.alloc("ost", [TT, 8], F32)
        zz = AR.alloc("zz", [TT, 512], F32)
        ogb = AR.alloc("ogb", [TT, 512], BF16)
        ogT = AR.alloc("ogT", [128, 4, TB], BF16)
        tga = tnh
        mao = [acc, sq]
        gbo = [AR.alloc("gbo%d" % i, [128, TB], F32) for i in range(2)]
        prb = Rot(banks)

        def norm_part(blk_, ti):
            t0_ = blk_ * TB
            xtile = xt[ti % 2]
            xnt = xn[ti % 2]
            c_ = 2 * (ti % 2)
            K.dma('sp', xtile[:, :], s.x[t0_ + ti * TT:t0_ + (ti + 1) * TT, :], R=[s.xt], W=[xtile])
            K.memset('pool', st1[0:TT, c_:c_ + 1], 0.0, [st1])
            K.act(xnt[:, :], xtile[:, :], AF.Square, [xtile], [xnt, st1], accum=st1[0:TT, c_:c_ + 1])
            K.ts('dve', st1[0:TT, c_ + 1:c_ + 2], st1[0:TT, c_:c_ + 1], 1.0 / D, ALU.mult, [st1], [st1], s2=EPS, op1=ALU.add)
            rsqrt_inplace(st1[0:TT, c_ + 1:c_ + 2], st1, TT)
            K.stt('dve', xtile[:, :], xtile[:, :], st1[0:TT, c_ + 1:c_ + 2], gmod_bc[0:TT, :], ALU.mult, ALU.mult, [xtile, st1, gmod_bc], [xtile])
            K.tt('pool', xnt[:, :], xtile[:, :], shift_bc[0:TT, :], ALU.add, [xtile, shift_bc], [xnt])

        for blk in range(nblk):
            t0 = blk * TB
            if blk == 0:
                for ti in range(nT):
                    norm_part(0, ti)
            for ti in range(nT):
                xnt = xn[ti % 2]
                pb = prb.next()
                pbv = pb[:, :].bitcast(BF16)
                for k in range(8):
                    K.tr(pbv[:, k * TT:(k + 1) * TT], xnt[:, k * 128:(k + 1) * 128], ident_b[0:TT, 0:TT], [xnt, ident_b], [pb])
                K.copy('act', hTb[:, :, ti * TT:(ti + 1) * TT], pbv[:, 0:8 * TT].rearrange("p (k t) -> p k t", k=8), [pb], [hTb])
            K.dma('sp', s.hT.h[:, :, t0:t0 + TB], hTb[:, :, :], R=[hTb], W=[s.hT])
            if STAGE == 'a1':
                continue

            for ch in range(12):
                pb = prb.next()
                for k in range(8):
                    K.mm(pb[:, 0:TB], w_qkv[:, k, ch * 128:(ch + 1) * 128], hTb[:, k, :], k == 0, k == 7, [w_qkv, hTb], [pb])
                pre = prec[ch % 2]
                acc_, tnh_ = accs[ch % 2], tnhs[ch % 2]
                K.copy('pool', pre[:, 0:3], carry[:, :, ch], [carry], [pre])
                K.copy('act', pre[:, 3:3 + TB], pb[:, 0:TB], [pb], [pre])
                K.ts('dve', acc_[:, :], pre[:, 3:3 + TB], cw[:, 3, ch:ch + 1], ALU.mult, [pre, cw], [acc_])
                K.stt('pool', acc_[:, :], pre[:, 2:2 + TB], cw[:, 2, ch:ch + 1], acc_[:, :], ALU.mult, ALU.add, [pre, cw, acc_], [acc_])
                K.stt('dve', acc_[:, :], pre[:, 1:1 + TB], cw[:, 1, ch:ch + 1], acc_[:, :], ALU.mult, ALU.add, [pre, cw, acc_], [acc_])
                K.stt('pool', acc_[:, :], pre[:, 0:TB], cw[:, 0, ch:ch + 1], acc_[:, :], ALU.mult, ALU.add, [pre, cw, acc_], [acc_])
                K.copy('pool', carry[:, :, ch], pre[:, TB:TB + 3], [pre], [carry])
                K.act(tnh_[:, :], acc_[:, :], AF.Tanh, [acc_], [tnh_], scale=0.5)
                K.stt('dve', qkvf[:, ch, :], tnh_[:, :], 1.0, acc_[:, :], ALU.add, ALU.mult, [tnh_, acc_], [qkvf])
            if blk == nblk - 1:
                for t_ in range(3):
                    K.dma('sp', s.cv_o[t_].rearrange("(c p) -> p c", p=128), carry[:, t_, :], R=[carry], W=[s.cv_ot], is_output=True, slow=True)

            if STAGE == 'a3':
                continue
            pst = prb.next()
            for ch in range(8):
                K.act(sq[:, :], qkvf[:, ch, :], AF.Square, [qkvf], [sq])
                for ti in range(nT):
                    K.mm(pst[0:TT, ti * 16 + ch * 2:ti * 16 + ch * 2 + 2], sq[:, ti * TT:(ti + 1) * TT], U_blk_ones[:, :], True, True, [sq, U_blk_ones], [pst])
            K.copy('dve', stat[:, :, 0:16], pst[0:TT, 0:nT * 16].rearrange("p (t c) -> p t c", t=nT), [pst], [stat])
            pab = prb.next()
            for ti in range(nT):
                for k in range(8):
                    K.mm(pab[0:TT, ti * 16:(ti + 1) * 16], hTb[:, k, ti * TT:(ti + 1) * TT], w_ab[:, k, :], k == 0, k == 7, [hTb, w_ab], [pab])
            pabv = pab[0:TT, 0:nT * 16].rearrange("p (t c) -> p t c", t=nT)
            K.act(stat[:, :, 40:48], pabv[:, :, 8:16], AF.Tanh, [pab], [stat], scale=0.5)
            K.ts('dve', stat[:, :, 40:48], stat[:, :, 40:48], 0.5, ALU.mult, [stat], [stat], s2=0.5, op1=ALU.add)
            K.tt('dve', stat[:, :, 32:40], pabv[:, :, 0:8], dtb_bc[0:TT, :].unsqueeze(1).to_broadcast([TT, nT, 8]), ALU.add, [pab, dtb_bc], [stat])
            K.ts('dve', stat2[:, :, 24:32], stat[:, :, 32:40], 0.0, ALU.max, [stat], [stat2])
            K.stt('dve', stat[:, :, 32:40], stat2[:, :, 24:32], -2.0, stat[:, :, 32:40], ALU.mult, ALU.add, [stat, stat2], [stat])
            K.act(stat[:, :, 32:40], stat[:, :, 32:40], AF.Exp, [stat], [stat])
            K.act(stat[:, :, 32:40], stat[:, :, 32:40], AF.Ln, [stat], [stat], bias=1.0)
            K.tt('dve', stat[:, :, 32:40], stat[:, :, 32:40], stat2[:, :, 24:32], ALU.add, [stat, stat2], [stat])
            K.tt('dve', stat[:, :, 32:40], stat[:, :, 32:40], negA[0:TT, :].unsqueeze(1).to_broadcast([TT, nT, 8]), ALU.mult, [stat, negA], [stat])
            K.ts('dve', stat[:, :, 0:16], stat[:, :, 0:16], 0.25, ALU.mult, [stat], [stat], s2=EPS, op1=ALU.add)
            rsqrt_inplace(stat[:, :, 0:16], stat, TT)
            K.ts('dve', stat[:, :, 0:8], stat[:, :, 0:8], 0.5 / 8.0, ALU.mult, [stat], [stat])
            K.ts('dve', stat[:, :, 8:16], stat[:, :, 8:16], 0.5, ALU.mult, [stat], [stat])
            pcs = prb.next()
            for ti in range(nT):
                K.mm(pcs[0:TT, ti * 16:ti * 16 + 8], Ublk[0:TT, 0:TT], stat[:, ti, 32:40], True, True, [Ublk, stat], [pcs])
                K.mm(pcs[0:TT, ti * 16 + 8:ti * 16 + 16], U_full[0:TT, 0:TT], stat[:, ti, 32:40], True, True, [U_full, stat], [pcs])
            K.copy('dve', stat[:, :, 48:64], pcs[0:TT, 0:nT * 16].rearrange("p (t c) -> p t c", t=nT), [pcs], [stat])
            K.tt('dve', stat[:, :, 16:24], stat[:, :, 8:16], stat[:, :, 40:48], ALU.mult, [stat], [stat])
            K.act(stat2[:, :, 24:32], stat[:, :, 56:64], AF.Exp, [stat], [stat2])
            K.tt('dve', stat[:, :, 24:32], stat[:, :, 0:8], stat2[:, :, 24:32], ALU.mult, [stat, stat2], [stat])
            K.act(stat2[:, :, 24:32], stat[:, :, 48:56], AF.Exp, [stat], [stat2])
            K.tt('dve', stat2[:, :, 0:8], stat[:, :, 16:24], stat2[:, :, 24:32], ALU.mult, [stat, stat2], [stat2])
            K.ts('dve', stat2[:, :, 16:24], stat[:, :, 40:48], 0.5, ALU.mult, [stat], [stat2])
            pgl = prb.next()
            for ti in range(nT):
                K.mm(pgl[0:TT, ti * 8:ti * 8 + 8], SLTblk[s.i][0:TT, 0:TT], stat[:, ti, 32:40], True, True, [SLTblk[s.i], stat], [pgl])
            K.act(stat2[:, :, 24:32], pgl[0:TT, 0:nT * 8].rearrange("p (t c) -> p t c", t=nT), AF.Exp, [pgl], [stat2])
            K.tt('dve', stat2[:, :, 8:16], stat[:, :, 8:16], stat2[:, :, 24:32], ALU.mult, [stat, stat2], [stat2])

            if STAGE == 'a5':
                continue
            pT = prb.next()
            for ti in range(nT):
                K.tr(pT[0:32, ti * TT:(ti + 1) * TT], stat[:, ti, 0:32], ident_f[0:TT, 0:TT], [stat, ident_f], [pT])
            K.copy('act', statT[:, :], pT[0:32, 0:TB], [pT], [statT])
            vsteps = [(pr, qn, dst, srcch) for pr in range(4)
                      for (qn, dst, srcch) in ((0, varq, pr), (3, varqg, pr), (1, vark, 4 + pr), (2, varkb, 4 + pr))]
            pbcs = {}

            def v_sel(i_):
                pr, qn, dst, srcch = vsteps[i_]
                pbc = prb.next()
                K.mm(pbc[:, 0:TB], selT[:, qn * 4 + pr, :], statT[:, :], True, True, [selT, statT], [pbc])
                pbcs[i_] = pbc

            def v_apply(i_):
                pr, qn, dst, srcch = vsteps[i_]
                pbc = pbcs.pop(i_)
                vtmp = vt_rot.next()
                K.tt('dve', vtmp[:, :], qkvf[:, srcch, :], pbc[:, 0:TB], ALU.mult, [qkvf, pbc], [vtmp])
                for hh in range(2):
                    p2 = prb.next()
                    K.mm(p2[0:64, 0:TB], ident_b[:, hh * 64:(hh + 1) * 64], vtmp[:, :], True, True, [ident_b, vtmp], [p2])
                    K.copy('act', dst[:, 2 * pr + hh, :], p2[0:64, 0:TB], [p2], [dst])

            v_sel(0)
            for i_ in range(len(vsteps)):
                if i_ + 1 < len(vsteps):
                    v_sel(i_ + 1)
                v_apply(i_)
            if STAGE == 'a7':
                continue
            for ti in range(nT):
                c0 = ti * TT
                pk = prb.next()
                pkv = pk[:, :].bitcast(BF16)
                for ch in range(4):
                    K.tr(pkv[0:TT, ch * 128:(ch + 1) * 128], qkvf[:, 4 + ch, c0:c0 + TT], ident_b[:, :], [qkvf, ident_b], [pk])
                for ch in range(4):
                    K.tr(pkv[0:TT, 512 + ch * 128:512 + (ch + 1) * 128], qkvf[:, 8 + ch, c0:c0 + TT], ident_b[:, :], [qkvf, ident_b], [pk])
                kview = pkv[0:TT, 0:512].rearrange("p (h d) -> p h d", h=8)
                vview = pkv[0:TT, 512:1024].rearrange("p (h d) -> p h d", h=8)
                K.tt('dve', ktok[:, 0, :, :], kview, stat2[:, ti, 0:8].unsqueeze(2).to_broadcast([TT, 8, 64]), ALU.mult, [pk, stat2], [ktok])
                K.tt('dve', ktok[:, 1, :, :], kview, stat2[:, ti, 8:16].unsqueeze(2).to_broadcast([TT, 8, 64]), ALU.mult, [pk, stat2], [ktok])
                K.tt('dve', vtok[:, :, :], vview, stat2[:, ti, 16:24].unsqueeze(2).to_broadcast([TT, 8, 64]), ALU.mult, [pk, stat2], [vtok])
                if STAGE == 't1':
                    continue
                for h in range(8):
                    K.ts('dve' if h % 2 == 0 else 'pool', Gu[:, h, :], U_full[0:TT, 0:TT], stat[:, ti, 32 + h:33 + h], ALU.mult, [U_full, stat], [Gu])
                for hp in range(4):
                    pg = prb.next()
                    for hh in range(2):
                        h = hp * 2 + hh
                        o0 = hh * 2 * TT
                        K.mm(pg[0:TT, o0:o0 + TT], Gu[:, h, :], SL_full[0:TT, 0:TT], True, True, [Gu, SL_full], [pg])
                        K.mm(pg[0:TT, o0 + TT:o0 + 2 * TT], SL_full[0:TT, 0:TT], Gu[:, h, :], True, True, [Gu, SL_full], [pg])
                    pgv = pg[0:TT, 0:4 * TT].rearrange("p (h x) -> p h x", h=2)
                    nmLU = nm[0:TT, 0:256] if TT == 128 else nmS2[0:TT, 0:32]
                    K.stt('dve', dec[:, hp * 2:hp * 2 + 2, 0:2 * TT], pgv, 0.0, nmLU.unsqueeze(1).to_broadcast([TT, 2, 2 * TT]), ALU.min, ALU.add, [pg, nm, nmS2], [dec])
                    K.stt('dve', dec[:, hp * 2:hp * 2 + 2, 2 * TT:3 * TT], pgv[:, :, TT:2 * TT], 0.0, nm[0:TT, 256:256 + TT].unsqueeze(1).to_broadcast([TT, 2, TT]), ALU.min, ALU.add, [pg, nm], [dec])
                K.act(dec[:, :, :], dec[:, :, :], AF.Exp, [dec], [dec])
                if STAGE == 't2':
                    continue
                for g in range(2):
                    pL = prb.next()
                    pM = prb.next()
                    pQ = prb.next()
                    for hh in range(4):
                        h = g * 4 + hh
                        kT_h = vark[:, h, c0:c0 + TT]
                        kbT_h = varkb[:, h, c0:c0 + TT]
                        qT_h = varq[:, h, c0:c0 + TT]
                        K.mm(pL[0:TT, hh * TT:(hh + 1) * TT], kbT_h, kT_h, True, True, [vark, varkb], [pL])
                        K.mm(pM[0:TT, hh * TT:(hh + 1) * TT], kT_h, kbT_h, True, True, [vark, varkb], [pM])
                        K.mm(pQ[0:TT, hh * TT:(hh + 1) * TT], kT_h, qT_h, True, True, [vark, varq], [pQ])
                    gs = slice(g * 4, g * 4 + 4)
                    K.tt('dve', Lk[g][:, :, :], pL[0:TT, 0:4 * TT].rearrange("p (h x) -> p h x", h=4), dec[:, gs, 0:TT], ALU.mult, [pL, dec], [Lk[g]])
                    K.tt('dve', Mk[g][:, :, :], pM[0:TT, 0:4 * TT].rearrange("p (h x) -> p h x", h=4), dec[:, gs, TT:2 * TT], ALU.mult, [pM, dec], [Mk[g]])
                    K.tt('dve', QKb[:, gs, :], pQ[0:TT, 0:4 * TT].rearrange("p (h x) -> p h x", h=4), dec[:, gs, 2 * TT:3 * TT], ALU.mult, [pQ, dec], [QKb])
                    K.stt('pool', Pm[g][:, :, :], Mk[g][:, :, :], -1.0, ident_f[0:TT, 0:TT].unsqueeze(1).to_broadcast([TT, 4, TT]), ALU.mult, ALU.add, [Mk[g], ident_f], [Pm[g]])
                if STAGE == 't3':
                    continue
                for lev in range(1, nlev):
                    last = (lev == nlev - 1)
                    for g in range(2):
                        pa = prb.next()
                        pbb = prb.next()
                        for hh in range(4):
                            K.mm(pa[0:TT, hh * TT:(hh + 1) * TT], Mk[g][:, hh, :], Lk[g][:, hh, :], True, True, [Mk[g], Lk[g]], [pa])
                            if not last:
                                K.mm(pbb[0:TT, hh * TT:(hh + 1) * TT], Lk[g][:, hh, :], Mk[g][:, hh, :], True, True, [Mk[g], Lk[g]], [pbb])
                        K.copy('act', Lk[g][:, :, :], pa[0:TT, 0:4 * TT].rearrange("p (h x) -> p h x", h=4), [pa], [Lk[g]])
                        if not last:
                            K.copy('dve', Mk[g][:, :, :], pbb[0:TT, 0:4 * TT].rearrange("p (h x) -> p h x", h=4), [pbb], [Mk[g]])
                    for g in range(2):
                        pc = prb.next()
                        for hh in range(4):
                            K.mm(pc[0:TT, hh * TT:(hh + 1) * TT], Lk[g][:, hh, :], Pm[g][:, hh, :], True, True, [Lk[g], Pm[g]], [pc])
                        K.tt('dve', Pm[g][:, :, :], Pm[g][:, :, :], pc[0:TT, 0:4 * TT].rearrange("p (h x) -> p h x", h=4), ALU.add, [Pm[g], pc], [Pm[g]])
                for g in range(2):
                    K.copy('act', Pb[:, g * 4:g * 4 + 4, :], Pm[g][:, :, :], [Pm[g]], [Pb])
                if STAGE == 't4':
                    continue
                pu = prb.next()
                for h in range(8):
                    K.mm(pu[0:TT, h * 64:(h + 1) * 64], Pb[:, h, :], vtok[:, h, :], True, True, [Pb, vtok], [pu])
                K.copy('act', usb[:, :, :], pu[0:TT, 0:512].rearrange("p (h e) -> p h e", h=8), [pu], [usb])
                for g in range(2):
                    pw = prb.next()
                    for hh in range(4):
                        h = g * 4 + hh
                        K.mm(pw[0:64, hh * TT:(hh + 1) * TT], ktok[:, 0, h, :], Pb[:, h, :], True, True, [ktok, Pb], [pw])
                    K.copy('act', wTb[:, g * 4:g * 4 + 4, :], pw[0:64, 0:4 * TT].rearrange("p (h x) -> p h x", h=4), [pw], [wTb])
                if STAGE == 't5':
                    continue
                pe_ = prb.next()
                for sc in range(nsub):
                    K.mm(pe_[0:64, sc * 8:(sc + 1) * 8], (oblk[sc][:, 0:64] if nsub == 2 else ones_f[0:TT, 0:64]), stat[:, ti, 32:40], True, True, [ones_f, oblk[0], oblk[1], stat], [pe_])
                K.act(egl[:, :, :], pe_[0:64, 0:nsub * 8].rearrange("p (s h) -> p s h", s=nsub), AF.Exp, [pe_], [egl])
                K.copy('act', Sb0[:, :, :], Sb[:, :, :], [Sb], [Sb0])
                for sc in range(nsub):
                    rs = slice(sc * C, (sc + 1) * C)
                    pws = prb.next()
                    for h in range(8):
                        K.mm(pws[0:TT, h * 64:(h + 1) * 64], wTb[:, h, :], Sb[:, h, :], True, True, [wTb, Sb], [pws])
                    vnew = vnews[sc]
                    K.tt('dve', vnew[rs, :, :], usb[rs, :, :], pws[rs, 0:512].rearrange("p (h e) -> p h e", h=8), ALU.subtract, [usb, pws], [vnew])
                    psu = prb.next()
                    for h in range(8):
                        K.mm(psu[0:64, h * 64:(h + 1) * 64], ktok[:, 1, h, :], vnew[:, h, :], True, True, [ktok, vnew], [psu])
                    K.tt('dve', S[:, :, :], S[:, :, :], egl[:, sc, :].unsqueeze(2).to_broadcast([64, 8, 64]), ALU.mult, [S, egl], [S])
                    K.tt('dve', S[:, :, :], S[:, :, :], psu[0:64, 0:512].rearrange("p (h e) -> p h e", h=8), ALU.add, [S, psu], [S])
                    K.copy('act', Sb[:, :, :], S[:, :, :], [S], [Sb])
                po_ = prb.next()
                for h in range(8):
                    K.mm(po_[0:TT, h * 64:(h + 1) * 64], varqg[:, h, c0:c0 + TT], Sb0[:, h, :], True, False, [varqg, Sb0], [po_])
                    for sc in range(nsub):
                        K.mm(po_[0:TT, h * 64:(h + 1) * 64], QKb[:, h, :], vnews[sc][:, h, :], False, sc == nsub - 1, [QKb, vnews[sc]], [po_])
                if STAGE == 't6':
                    continue
                K.copy('act', osb[:, :, :], po_[0:TT, 0:512].rearrange("p (h e) -> p h e", h=8), [po_], [osb])
                K.tt('dve', osq[:, :, :], osb[:, :, :], osb[:, :, :], ALU.mult, [osb], [osq])
                K.red('dve', ost[:, :], osq[:, :, :], ALU.add, [osq], [ost])
                K.ts('dve', ost[:, :], ost[:, :], 1.0 / 64.0, ALU.mult, [ost], [ost], s2=EPS, op1=ALU.add)
                rsqrt_inplace(ost[:, :], ost, TT)
                K.tt('dve', osb[:, :, :], osb[:, :, :], ost[:, :].unsqueeze(2).to_broadcast([TT, 8, 64]), ALU.mult, [osb, ost], [osb])
                K.tt('dve', osb[:, :, :], osb[:, :, :], gn_bc[0:TT, :].unsqueeze(1).to_broadcast([TT, 8, 64]), ALU.mult, [osb, gn_bc], [osb])
                pz = prb.next()
                for k in range(8):
                    K.mm(pz[0:TT, 0:512], hTb[:, k, c0:c0 + TT], w_z[:, k, :], k == 0, k == 7, [hTb, w_z], [pz])
                K.act(zz[:, :], pz[0:TT, 0:512], AF.Tanh, [pz], [zz], scale=0.5)
                K.stt('dve', zz[:, :], zz[:, :], 1.0, pz[0:TT, 0:512], ALU.add, ALU.mult, [zz, pz], [zz])
                K.stt('dve', ogb[:, :], zz[:, :], 0.5, osb[:, :, :].rearrange("p h e -> p (h e)"), ALU.mult, ALU.mult, [zz, osb], [ogb])
                pt = prb.next()
                ptv = pt[:, :].bitcast(BF16)
                for ch in range(4):
                    K.tr(ptv[:, ch * TT:(ch + 1) * TT], ogb[:, ch * 128:(ch + 1) * 128], ident_b[0:TT, 0:TT], [ogb, ident_b], [pt])
                K.copy('act', ogT[:, :, c0:c0 + TT], ptv[:, 0:4 * TT].rearrange("p (c t) -> p c t", c=4), [pt], [ogT])

            if STAGE in ('t1','t2','t3','t4','t5','t6','t7'):
                continue
            if blk + 1 < nblk:
                for ti in range(nT):
                    norm_part(blk + 1, ti)
            for f in range(8):
                py = prb.next()
                for k in range(4):
                    K.mm(py[:, 0:TB], w_go[:, k, f * 128:(f + 1) * 128], ogT[:, k, :], k == 0, k == 3, [w_go, ogT], [py])
                pgt = prb.next()
                for k in range(8):
                    K.mm(pgt[:, 0:TB], w_ga[:, k, f * 128:(f + 1) * 128], hTb[:, k, :], k == 0, k == 7, [w_ga, hTb], [pgt])
                K.act(tga[:, :], pgt[:, 0:TB], AF.Tanh, [pgt], [tga], scale=0.5)
                mo = mao[f % 2]
                K.stt('dve', mo[:, :], tga[:, :], 1.0, py[:, 0:TB], ALU.add, ALU.mult, [tga, py], [mo])
                K.dma('sp', s.maT.h[:, f, t0:t0 + TB], mo[:, :], R=[mo], W=[s.maT])
                pgb_ = prb.next()
                for k in range(8):
                    K.mm(pgb_[:, 0:TB], w_gbA[:, k, f * 128:(f + 1) * 128], hTb[:, k, :], k == 0, k == 7, [w_gbA, hTb], [pgb_])
                go_ = gbo[f % 2]
                K.act(go_[:, :], pgb_[:, 0:TB], AF.Tanh, [pgb_], [go_], scale=0.5)
                K.ts('dve', go_[:, :], go_[:, :], 1.0, ALU.add, [go_], [go_])
                K.dma('sp', s.gbT.h[:, f, t0:t0 + TB], go_[:, :], R=[go_], W=[s.gbT])
        K.dma('sp', s.st_o.rearrange("h d e -> d h e"), S[:, :, :], R=[S], W=[s.st_ot], is_output=True)

    U_blk_ones = CONST.alloc("blk2", [128, 2], F32)
    K.memset('pool', U_blk_ones[:, :], 0.0, [U_blk_ones])
    K.memset('pool', U_blk_ones[0:64, 0:1], 1.0, [U_blk_ones])
    K.memset('pool', U_blk_ones[64:128, 1:2], 1.0, [U_blk_ones])
    SLTp = CONST.alloc("SLTp", [128, 128], F32)
    K.memset('pool', SLTp[:, :], 1.0, [SLTp])
    aff(SLTp[:, :], SLTp[:, :], [[-1, 128]], ALU.is_gt, 0.0, 0, 1, [SLTp], [SLTp])
    SLTb = CONST.alloc("SLTb", [128, 128], F32)
    K.copy('pool', SLTb[:, :], SLTp[:, :], [SLTp], [SLTb])
    K.memset('pool', SLTb[64:128, 0:64], 0.0, [SLTb])
    SLTblk = [SLTb, SLTp, SLTp]
    oblk = [CONST.alloc("oblk%d" % i, [128, 64], F32) for i in range(2)]
    for i_ in range(2):
        K.memset('pool', oblk[i_][:, :], 0.0, [oblk[i_]])
        K.memset('pool', oblk[i_][i_ * 64:(i_ + 1) * 64, :], 1.0, [oblk[i_]])
    sel65 = CONST.alloc("sel65", [65, 64], F32)
    K.memset('pool', sel65[:, :], 0.0, [sel65])
    K.memset('pool', sel65[64:65, :], 1.0, [sel65])
    nmS2 = CONST.alloc("nmS2", [16, 32], F32)
    K.copy('pool', nmS2[:, 0:16], nm_s[0:16, 0:16], [nm_s], [nmS2])
    K.copy('pool', nmS2[:, 16:32], nm_s[0:16, 128:144], [nm_s], [nmS2])

    for s in seqs:
        if STAGE != 'p0' and str(s.i) in SEQS:
            pass_A(s)
    K.barrier()

    AR.reset()
    w_ml = AR.alloc("w_ml", [128, 8, 672], BF16)
    w_uqb = AR.alloc("w_uqb", [128, 3, 768], BF16)
    w_kvb = AR.alloc("w_kvb", [128, 2, 1024], BF16)
    w_mo = AR.alloc("w_mo", [64, 8, 1024], BF16)
    w_ob = AR.alloc("w_ob", [128, 8, 1024], BF16)
    K.dma('pool', w_ml[:, :, :], w_in_v[:, :, CQ0:CQ0 + 672], R=[w_in], W=[w_ml])
    K.dma('pool', w_uqb[:, :, :], w_uq.h.rearrange("(k p) n -> p k n", p=128), R=[w_uq], W=[w_uqb])
    K.dma('pool', w_kvb[:, :, :], w_ukv.h.rearrange("(k p) n -> p k n", p=128), R=[w_ukv], W=[w_kvb])
    K.dma('pool', w_mo[:, :, :], w_mla_out.h.rearrange("(h p) n -> p h n", p=64), R=[w_mla_out], W=[w_mo])
    K.dma('pool', w_ob[:, :, :], w_o.h.rearrange("(k p) n -> p k n", p=128), R=[w_o], W=[w_ob])
    qg_bc = AR.alloc("qg_bc", [128, 384], F32)
    kvg_bc = AR.alloc("kvg_bc", [128, 256], F32)
    gk_bc = AR.alloc("gk_bc", [128, 96], F32)
    K.dma('sp', qg_bc[:, :], q_norm_g.h[0].partition_broadcast(128), R=[q_norm_g], W=[qg_bc])
    K.dma('sp', kvg_bc[:, :], kv_norm_g.h[0].partition_broadcast(128), R=[kv_norm_g], W=[kvg_bc])
    K.dma('sp', gqk[:, :], qh_g.h[0].partition_broadcast(128), R=[qh_g], W=[gqk])
    K.dma('sp', gk_bc[:, :], kh_g.h[0].partition_broadcast(128), R=[kh_g], W=[gk_bc])
    K.tt('dve', gqk[:, :], gqk[:, :], gk_bc[:, :], ALU.mult, [gqk, gk_bc], [gqk])
    K.tt('dve', gk_bc[:, :], gqk[:, :], gqk[:, :], ALU.mult, [gqk], [gk_bc])
    K.op('dve', lambda e: e.tensor_reduce(out=negM[:, 0:1], in_=gk_bc[:, :], axis=AX.X, op=ALU.max), [gk_bc], [negM])
    phalf = AR.alloc("phalf", [128, 1], F32)
    K.memset('pool', phalf[:, :], 0.5, [phalf])
    K.tt('pool', negM[:, :], negM[:, :], phalf[:, :], ALU.pow, [negM, phalf], [negM])
    K.ts('dve', negM[:, :], negM[:, :], -float(np.sqrt(96.0)), ALU.mult, [negM], [negM])
    K.ts('dve', gqk[:, :], gqk[:, :], float(96.0 ** -0.5), ALU.mult, [gqk], [gqk])
    g1h_bc = AR.alloc("g1h_bc", [128, 1024], F32)
    gmod2_bc = AR.alloc("gmod2_bc", [128, 1024], F32)
    shift2_bc = AR.alloc("shift2_bc", [128, 1024], F32)
    markB = AR.mark()

    def pass_B(s):
        if s.i > 0:
            K.barrier()
        AR.reset(markB)
        L, TT = s.L, s.TT
        TB = 256 if s.i == 0 else 16
        nT = TB // TT
        nblk = L // TB
        P = s.past
        nPT = P // 128
        NT = nPT + (L + 127) // 128
        load_mod_bc(g1h_bc, s.i, 2)
        load_mod_bc(shift2_bc, s.i, 3)
        load_mod_bc(gmod2_bc, s.i, 4)
        K.ts('dve', g1h_bc[:, :], g1h_bc[:, :], 0.5, ALU.mult, [g1h_bc], [g1h_bc])
        n2g_tmp = AR.alloc("n2g_tmp", [128, 1024], F32)
        K.dma('sp', n2g_tmp[:, :], norm2_g.h[0].partition_broadcast(128), R=[norm2_g], W=[n2g_tmp])
        K.stt('dve', gmod2_bc[:, :], gmod2_bc[:, :], 1.0, n2g_tmp[:, :], ALU.add, ALU.mult, [gmod2_bc, n2g_tmp], [gmod2_bc])
        K.barrier()
        AR.reset(markB)
        kT = AR.alloc("kT", [96, 8, P + L], BF16)
        vA = AR.alloc("vA", [128, NT, 8, 66], BF16)
        rkS = AR.alloc("rkS", [128, NT, 8], F32)
        kTs = [Tile(kT.h, "kT%d" % i) for i in range(NT)]
        vAs = [Tile(vA.h, "vA%d" % i) for i in range(NT)]
        rkSs = [Tile(rkS.h, "rkS%d" % i) for i in range(NT)]
        K.memset('pool', vA[:, :, :, 64:65], 1.0, vAs)
        hTbs = [AR.alloc("hTbB%d" % i, [128, 8, TB], BF16) for i in range(2)]
        cs = [AR.alloc("cs%d" % i, [128, 32], F32) for i in range(2)]
        st = AR.alloc("stB", [128, 16], F32)
        NW = 2
        ckvn2 = [AR.alloc("ckvn%d" % i, [128, 256], F32) for i in range(NW)]
        kro2 = [AR.alloc("kro%d" % i, [128, 32], F32) for i in range(NW)]
        krr = AR.alloc("krr", [128, 32], F32)
        rt = AR.alloc("rt", [128, 4, 8, 16], F32)
        cqn = AR.alloc("cqn", [128, 384], BF16)
        cqnTs = [AR.alloc("cqnT%d" % i, [128, 3, TB], BF16) for i in range(2)]
        qtok = AR.alloc("qtok", [128, 8, 96], F32)
        qsq = AR.alloc("qsq", [128, 8, 96], F32)
        qnb = AR.alloc("qnb", [128, 8, 96], BF16)
        qTs = [AR.alloc("qT%d" % i, [96, 8, TB], BF16) for i in range(2)]
        pTs = [AR.alloc("pT%d" % i, [128, TB], BF16) for i in range(4)]
        oas = [AR.alloc("oa%d" % i, [65, TB], F32) for i in range(2)]
        rds = [AR.alloc("rd%d" % i, [65, TB], F32) for i in range(2)]
        for rd_ in rds:
            K.memset("pool", rd_[:, :], 0.0, [rd_])
        obTs = [AR.alloc("obT%d" % i, [64, 8, TB], BF16) for i in range(2)]
        tgb = AR.alloc("tgb", [128, TB], F32)
        mbt = AR.alloc("mbt", [128, TB], F32)
        mal = [AR.alloc("mal%d" % i, [128, TB], F32) for i in range(2)]
        gbl = [AR.alloc("gbl%d" % i, [128, TB], F32) for i in range(2)]
        mergedT = AR.alloc("mergedT", [128, 8, TB], BF16)
        xt = [AR.alloc("xtB0", [TT, 1024], F32)] * 2
        x1t = AR.alloc("x1t", [TT, 1024], F32)
        h2b = AR.alloc("h2b", [TT, 1024], BF16)
        h2Tb = AR.alloc("h2Tb", [128, 8, TB], BF16)
        if s.i > 0:
            scs = [AR.alloc("scs%d" % i, [128, 8, 16], F32) for i in range(2)]
            pTq = [AR.alloc("pTq%d" % i, [128, 128], BF16) for i in range(3)]
            oaS = AR.alloc("oaS", [65, 128], F32)
            rdS = AR.alloc("rdS", [65, 128], F32)
            K.memset('pool', rdS[:, :], 0.0, [rdS])
        prb = Rot(banks[2:6])
        prb_tiles = prb
        poolQ = Rot(banks[2:4])
        poolK = Rot(banks[4:6])
        pra = Rot(banks[0:2])
        stQ = AR.alloc("stQ", [128, 16], F32)
        stK = AR.alloc("stK", [128, 4], F32)
        rtK = AR.alloc("rtK", [128, 4, 16], F32)
        po_rot = Rot(banks[6:8])
        pT_rot = Rot(pTs)

        class EB:
            pass
        ebs = []
        for i_ in range(NW):
            e_ = EB()
            e_.ckb = AR.alloc("ckb%d" % i_, [128, 256], BF16)
            e_.krb = AR.alloc("krb%d" % i_, [128, 32], BF16)
            e_.ckT = AR.alloc("ckT%d" % i_, [128, 2, 128], BF16)
            e_.kfull = AR.alloc("kfull%d" % i_, [128, 8, 96], BF16)
            e_.sqk = AR.alloc("sqk%d" % i_, [128, 4, 64], F32)
            e_.ssq = AR.alloc("ssqB%d" % i_, [128, 8], F32)
            e_.krs = AR.alloc("krs%d" % i_, [128, 32], F32)
            e_.st = AR.alloc("stE%d" % i_, [128, 2], F32)
            ebs.append(e_)
        sqk = ebs[0].sqk
        eb_rot = Rot(ebs)

        def expand_kv(kt, ck_ap, ck_t, kr_ap, kr_t, n, col0, prb=None, e_=None):
            if prb is None:
                prb = prb_tiles
            if e_ is None:
                e_ = eb_rot.next()
            ckb, krb, ckT, kfull, sqk_, ssq, krs, ste = e_.ckb, e_.krb, e_.ckT, e_.kfull, e_.sqk, e_.ssq, e_.krs, e_.st
            K.copy('dve', ckb[0:n, :], ck_ap, [ck_t], [ckb])
            K.copy('dve', krb[0:n, :], kr_ap, [kr_t], [krb])
            pt_ = prb.next()
            ptv = pt_[:, :].bitcast(BF16)
            for c in range(2):
                K.tr(ptv[:, c * 128:c * 128 + n], ckb[0:n, c * 128:(c + 1) * 128], ident_b[0:n, 0:n], [ckb, ident_b], [pt_])
            K.copy('act', ckT[:, :, 0:n], ptv[:, 0:256].rearrange("p (c t) -> p c t", c=2)[:, :, 0:n], [pt_], [ckT])
            yield
            for g in range(2):
                pv_ = prb.next()
                for c in range(2):
                    K.mm(pv_[0:n, 0:512], ckT[:, c, 0:n], w_kvb[:, c, g * 512:(g + 1) * 512], c == 0, c == 1, [ckT, w_kvb], [pv_])
                pvv = pv_[0:n, 0:512].rearrange("p (h x) -> p h x", h=4)
                K.copy('act', vA[0:n, kt, g * 4:g * 4 + 4, 0:64], pvv[:, :, 64:128], [pv_], [vAs[kt]])
                K.copy('act', kfull[0:n, g * 4:g * 4 + 4, 0:64], pvv[:, :, 0:64], [pv_], [kfull])
                K.act(sqk_[0:n, :, :], pvv[:, :, 0:64], AF.Square, [pv_], [sqk_])
                K.red('dve', ssq[0:n, g * 4:g * 4 + 4], sqk_[0:n, :, :], ALU.add, [sqk_], [ssq])
                yield
            for h in range(8):
                K.copy('dve', kfull[0:n, h, 64:96], krb[0:n, :], [krb], [kfull])
            yield
            pkt = prb.next()
            pktv = pkt[:, :].bitcast(BF16)
            for h in range(8):
                K.tr(pktv[0:96, h * 128:h * 128 + n], kfull[0:n, h, :], ident_b[0:n, 0:n], [kfull, ident_b], [pkt])
            K.copy('act', kT[:, :, col0:col0 + n], pktv[0:96, :].rearrange("p (h t) -> p h t", h=8)[:, :, 0:n], [pkt], [kTs[kt]])
            K.tt('pool', krs[0:n, :], kr_ap, kr_ap, ALU.mult, [kr_t], [krs])
            K.red('dve', ste[0:n, 0:1], krs[0:n, :], ALU.add, [krs], [ste])
            K.ts('dve', ssq[0:n, :], ssq[0:n, :], ste[0:n, 0:1], ALU.add, [ssq, ste], [ssq])
            K.ts('dve', rkS[0:n, kt, :], ssq[0:n, :], 1.0 / 96.0, ALU.mult, [ssq], [rkSs[kt]], s2=EPS, op1=ALU.add)
            rsqrt_inplace(rkS[0:n, kt, :], rkSs[kt], n)
            yield

        def lockstep(gs):
            alive = list(gs)
            while alive:
                for g in list(alive):
                    try:
                        next(g)
                    except StopIteration:
                        alive.remove(g)

        poolsN = [Rot(banks[0:3]), Rot(banks[3:6])]
        for pt_i in range(0, nPT, NW):
            gens = []
            for j_ in range(NW):
                p_ = pt_i + j_
                ckl = ckvn2[j_]
                krl = kro2[j_]
                K.dma('sp', ckl[:, :], ckv_c.h[s.i - 1, p_ * 128:(p_ + 1) * 128, :], R=[ckv_c], W=[ckl])
                K.dma('sp', krl[:, :], kr_c.h[s.i - 1, p_ * 128:(p_ + 1) * 128, :], R=[kr_c], W=[krl])
                gens.append(expand_kv(p_, ckl[:, :], ckl, krl[:, :], krl, 128, p_ * 128, prb=poolsN[j_], e_=ebs[j_]))
            lockstep(gens)

        def gen_tiles(blk):
            t0 = blk * TB
            hTb, cqnT, qT = hTbs[blk % 2], cqnTs[blk % 2], qTs[blk % 2]

            def q_stream(ti, pool):
                c0 = ti * TT
                cs_ = cs[ti % 2]
                pcq = pool.next()
                for k in range(8):
                    K.mm(pcq[0:TT, 0:384], hTb[:, k, c0:c0 + TT], w_ml[:, k, 0:384], k == 0, k == 7, [hTb, w_ml], [pcq])
                yield
                K.memset('pool', stQ[0:TT, 0:1], 0.0, [stQ])
                K.act(qsq[0:TT, 0:4, :].rearrange("p a b -> p (a b)"), pcq[0:TT, 0:384], AF.Square, [pcq], [qsq, stQ], accum=stQ[0:TT, 0:1])
                K.ts('dve', stQ[0:TT, 2:3], stQ[0:TT, 0:1], 1.0 / 384.0, ALU.mult, [stQ], [stQ], s2=EPS, op1=ALU.add)
                rsqrt_inplace(stQ[0:TT, 2:3], stQ, TT)
                K.stt('dve', qsq[0:TT, 0:4, :].rearrange("p a b -> p (a b)"), pcq[0:TT, 0:384], stQ[0:TT, 2:3], qg_bc[0:TT, :], ALU.mult, ALU.mult, [pcq, stQ, qg_bc], [qsq])
                K.copy('act', cqn[0:TT, :], qsq[0:TT, 0:4, :].rearrange("p a b -> p (a b)"), [qsq], [cqn])
                yield
                ptq = pool.next()
                ptqv = ptq[:, :].bitcast(BF16)
                for c in range(3):
                    K.tr(ptqv[:, c * TT:(c + 1) * TT], cqn[0:TT, c * 128:(c + 1) * 128], ident_b[0:TT, 0:TT], [cqn, ident_b], [ptq])
                K.copy('act', cqnT[:, :, c0:c0 + TT], ptqv[:, 0:3 * TT].rearrange("p (c t) -> p c t", c=3), [ptq], [cqnT])
                yield
                pq0 = pool.next()
                for c in range(3):
                    K.mm(pq0[0:TT, 0:480], cqnT[:, c, c0:c0 + TT], w_uqb[:, c, 0:480], c == 0, c == 2, [cqnT, w_uqb], [pq0])
                K.copy('act', qtok[0:TT, 0:5, :], pq0[0:TT, 0:480].rearrange("p (h d) -> p h d", h=5), [pq0], [qtok])
                pq1 = pool.next()
                for c in range(3):
                    K.mm(pq1[0:TT, 0:288], cqnT[:, c, c0:c0 + TT], w_uqb[:, c, 480:768], c == 0, c == 2, [cqnT, w_uqb], [pq1])
                K.copy('act', qtok[0:TT, 5:8, :], pq1[0:TT, 0:288].rearrange("p (h d) -> p h d", h=3), [pq1], [qtok])
                yield
                cos8 = cs_[0:TT, 0:16].unsqueeze(1).to_broadcast([TT, 8, 16])
                sin8 = cs_[0:TT, 16:32].unsqueeze(1).to_broadcast([TT, 8, 16])
                K.tt('dve', rt[0:TT, 0, :, :], qtok[0:TT, :, 64:80], cos8, ALU.mult, [qtok, cs_], [rt])
                K.tt('dve', rt[0:TT, 1, :, :], qtok[0:TT, :, 80:96], sin8, ALU.mult, [qtok, cs_], [rt])
                K.tt('dve', rt[0:TT, 2, :, :], qtok[0:TT, :, 80:96], cos8, ALU.mult, [qtok, cs_], [rt])
                K.tt('dve', rt[0:TT, 3, :, :], qtok[0:TT, :, 64:80], sin8, ALU.mult, [qtok, cs_], [rt])
                yield
                K.tt('dve', qtok[0:TT, :, 64:80], rt[0:TT, 0, :, :], rt[0:TT, 1, :, :], ALU.subtract, [rt], [qtok])
                K.tt('dve', qtok[0:TT, :, 80:96], rt[0:TT, 2, :, :], rt[0:TT, 3, :, :], ALU.add, [rt], [qtok])
                K.tt('dve', qsq[0:TT, :, :], qtok[0:TT, :, :], qtok[0:TT, :, :], ALU.mult, [qtok], [qsq])
                K.red('dve', stQ[0:TT, 8:16], qsq[0:TT, :, :], ALU.add, [qsq], [stQ])
                yield
                K.ts('dve', stQ[0:TT, 8:16], stQ[0:TT, 8:16], 1.0 / 96.0, ALU.mult, [stQ], [stQ], s2=EPS, op1=ALU.add)
                rsqrt_inplace(stQ[0:TT, 8:16], stQ, TT)
                K.tt('dve', qtok[0:TT, :, :], qtok[0:TT, :, :], stQ[0:TT, 8:16].unsqueeze(2).to_broadcast([TT, 8, 96]), ALU.mult, [qtok, stQ], [qtok])
                K.tt('dve', qnb[0:TT, :, :], qtok[0:TT, :, :], gqk[0:TT, :].unsqueeze(1).to_broadcast([TT, 8, 96]), ALU.mult, [qtok, gqk], [qnb])
                yield
                ptt = pool.next()
                pttv = ptt[:, :].bitcast(BF16)
                for h in range(8):
                    K.tr(pttv[0:96, h * TT:(h + 1) * TT], qnb[0:TT, h, :], ident_b[0:TT, 0:TT], [qnb, ident_b], [ptt])
                K.copy('act', qT[:, :, c0:c0 + TT], pttv[0:96, 0:8 * TT].rearrange("p (h t) -> p h t", h=8), [ptt], [qT])
                yield

            def k_stream(ti, pool):
                c0 = ti * TT
                pos = P + t0 + c0
                kt = nPT + (t0 + c0) // 128
                cs_ = cs[ti % 2]
                ckvn = ckvn2[ti % 2]
                kro = kro2[ti % 2]
                K.dma('sp', cs_[0:TT, :], rope_cs.h[pos:pos + TT, :], R=[rope_cs], W=[cs_])
                pck = pool.next()
                for k in range(8):
                    K.mm(pck[0:TT, 0:288], hTb[:, k, c0:c0 + TT], w_ml[:, k, 384:672], k == 0, k == 7, [hTb, w_ml], [pck])
                yield
                K.memset('pool', stK[0:TT, 0:1], 0.0, [stK])
                K.act(sqk[0:TT, :, :].rearrange("p a b -> p (a b)"), pck[0:TT, 0:256], AF.Square, [pck], [sqk, stK], accum=stK[0:TT, 0:1])
                K.ts('dve', stK[0:TT, 1:2], stK[0:TT, 0:1], 1.0 / 256.0, ALU.mult, [stK], [stK], s2=EPS, op1=ALU.add)
                rsqrt_inplace(stK[0:TT, 1:2], stK, TT)
                K.stt('dve', ckvn[0:TT, :], pck[0:TT, 0:256], stK[0:TT, 1:2], kvg_bc[0:TT, :], ALU.mult, ALU.mult, [pck, stK, kvg_bc], [ckvn])
                K.dma('sp', s.ckv_o[t0 + c0:t0 + c0 + TT, :], ckvn[0:TT, :], R=[ckvn], W=[s.ckv_ot], is_output=True)
                K.copy('act', krr[0:TT, :], pck[0:TT, 256:288], [pck], [krr])
                yield
                cosv, sinv = cs_[0:TT, 0:16], cs_[0:TT, 16:32]
                K.tt('dve', rtK[0:TT, 0, :], krr[0:TT, 0:16], cosv, ALU.mult, [krr, cs_], [rtK])
                K.tt('dve', rtK[0:TT, 1, :], krr[0:TT, 16:32], sinv, ALU.mult, [krr, cs_], [rtK])
                K.tt('dve', rtK[0:TT, 2, :], krr[0:TT, 16:32], cosv, ALU.mult, [krr, cs_], [rtK])
                K.tt('dve', rtK[0:TT, 3, :], krr[0:TT, 0:16], sinv, ALU.mult, [krr, cs_], [rtK])
                K.tt('dve', kro[0:TT, 0:16], rtK[0:TT, 0, :], rtK[0:TT, 1, :], ALU.subtract, [rtK], [kro])
                K.tt('dve', kro[0:TT, 16:32], rtK[0:TT, 2, :], rtK[0:TT, 3, :], ALU.add, [rtK], [kro])
                K.dma('sp', s.kr_o[t0 + c0:t0 + c0 + TT, :], kro[0:TT, :], R=[kro], W=[s.kr_ot], is_output=True)
                yield
                yield from expand_kv(kt, ckvn[0:TT, :], ckvn, kro[0:TT, :], kro, TT, P + t0 + c0, prb=pool)

            for ti in range(nT):
                alive = [q_stream(ti, poolQ), k_stream(ti, poolK)]
                while alive:
                    for g_ in list(alive):
                        try:
                            next(g_)
                        except StopIteration:
                            alive.remove(g_)
                    yield
            yield

        def gen_attn(blk):
            t0 = blk * TB
            qT, obT = qTs[blk % 2], obTs[blk % 2]
            if s.i == 0:
                vis = list(range(0, (t0 + TB) // 128))
            else:
                vis = list(range(NT))
            items = [(h, vi, kt) for h in range(8) for vi, kt in enumerate(vis)]
            DPIPE = 1
            pos_ = {}
            inflight = {}

            def geom(kt):
                if s.i == 0:
                    r_ = kt - t0 // 128
                    diag = r_ >= 0
                    return 128, diag, (128 * r_ if diag else 0)
                return (128 if kt < nPT else L), False, 0

            def emit_qk(it):
                h, vi, kt = it
                nk, diag, q0 = geom(kt)
                ps_ = pra.next()
                K.mm(ps_[0:nk, 0:TB - q0], kT[:, h, kt * 128:kt * 128 + nk], qT[:, h, q0:TB], True, True, [kTs[kt], qT], [ps_])
                inflight[it] = ps_

            def emit_pv(it):
                h, vi, kt = it
                nk, diag, q0 = geom(kt)
                nq = TB - q0
                ps_ = inflight.pop(it)
                if vi == 0:
                    pos_[h] = po_rot.next()
                po_ = pos_[h]
                pt_ = pT_rot.next()
                K.act(pt_[0:nk, 0:nq], ps_[0:nk, 0:nq], AF.Exp, [ps_, rkSs[kt], negM], [pt_], scale=rkS[0:nk, kt, h:h + 1], bias=negM[0:nk, 0:1])
                if diag:
                    K.memset('pool', pt_[64:128, 0:64], 0.0, [pt_])
                K.mm(po_[0:65, q0:TB], vA[0:nk, kt, h, 0:65], pt_[0:nk, 0:nq], vi == 0, vi == len(vis) - 1, [vAs[kt], pt_], [po_])
                if vi == len(vis) - 1:
                    oa, rd = oas[h % 2], rds[h % 2]
                    K.copy('act', oa[:, :], po_[0:65, 0:TB], [po_], [oa])
                    K.op('dve', lambda e: e.reciprocal(out=rd[64:65, :], in_=oa[64:65, :]), [oa], [rd])
                    K.mm(po_[0:64, 256:256 + TB], sel65[0:65, :], rd[0:65, :], True, True, [sel65, rd], [po_])
                    K.tt('dve', obT[:, h, :], oa[0:64, :], po_[0:64, 256:256 + TB], ALU.mult, [oa, po_], [obT])

            for i_ in range(len(items) + DPIPE):
                if i_ < len(items):
                    emit_qk(items[i_])
                if i_ >= DPIPE:
                    emit_pv(items[i_ - DPIPE])
                yield


        def gen_attn_sample(blk):
            qT, obT = qTs[0], obTs[0]
            vis = list(range(NT))
            po_ = po_rot.next()
            pq_rot = Rot(pTq)
            for vi, kt in enumerate(vis):
                nk = 128 if kt < nPT else L
                ps_ = pra.next()
                for h in range(8):
                    K.mm(ps_[0:nk, h * 16:(h + 1) * 16], kT[:, h, kt * 128:kt * 128 + nk], qT[:, h, 0:16], True, True, [kTs[kt], qT], [ps_])
                sc_ = scs[vi % 2]
                K.tt('dve', sc_[0:nk, :, :], ps_[0:nk, 0:128].rearrange("p (h q) -> p h q", h=8), rkS[0:nk, kt, :].unsqueeze(2).to_broadcast([nk, 8, 16]), ALU.mult, [ps_, rkSs[kt]], [sc_])
                pt_ = pq_rot.next()
                K.act(pt_[0:nk, :], sc_[0:nk, :, :].rearrange("p h q -> p (h q)"), AF.Exp, [sc_, negM], [pt_], bias=negM[0:nk, 0:1])
                for h in range(8):
                    K.mm(po_[0:65, h * 16:(h + 1) * 16], vA[0:nk, kt, h, 0:65], pt_[0:nk, h * 16:(h + 1) * 16], vi == 0 and h == 0, vi == len(vis) - 1, [vAs[kt], pt_], [po_])
                yield
            K.copy('act', oaS[:, :], po_[0:65, 0:128], [po_], [oaS])
            K.op('dve', lambda e: e.reciprocal(out=rdS[64:65, :], in_=oaS[64:65, :]), [oaS], [rdS])
            K.mm(po_[0:64, 256:384], sel65[0:65, :], rdS[0:65, :], True, True, [sel65, rdS], [po_])
            K.tt('dve', obT[:, :, :].rearrange("p h q -> p (h q)"), oaS[0:64, :], po_[0:64, 256:384], ALU.mult, [oaS, po_], [obT])
            yield

        def gen_merge(blk):
            t0 = blk * TB
            obT = obTs[blk % 2]
            def ld_f(f_):
                K.dma('sp', mal[f_ % 2][:, :], s.maT.h[:, f_, t0:t0 + TB], R=[s.maT], W=[mal[f_ % 2]])
                K.dma('sp', gbl[f_ % 2][:, :], s.gbT.h[:, f_, t0:t0 + TB], R=[s.gbT], W=[gbl[f_ % 2]])
            ld_f(0)
            ld_f(1)
            K.dma('sp', xt[0][:, :], s.x[t0:t0 + TT, :], R=[s.xt], W=[xt[0]])
            for f in range(8):
                ml = mal[f % 2]
                py = prb.next()
                for h in range(8):
                    K.mm(py[:, 0:TB], w_mo[:, h, f * 128:(f + 1) * 128], obT[:, h, :], h == 0, h == 7, [w_mo, obT], [py])
                gl_ = gbl[f % 2]
                K.tt('dve', mbt[:, :], gl_[:, :], py[:, 0:TB], ALU.mult, [gl_, py], [mbt])
                K.tt('pool', mergedT[:, f, :], mbt[:, :], ml[:, :], ALU.add, [mbt, ml], [mergedT])
                if f + 2 < 8:
                    ld_f(f + 2)
                yield
            for ti in range(nT):
                c0 = ti * TT
                xtile = xt[ti % 2]
                if ti > 0:
                    K.dma('sp', xtile[:, :], s.x[t0 + c0:t0 + c0 + TT, :], R=[s.xt], W=[xtile])
                for cb in range(2):
                    pw = prb.next()
                    for k in range(8):
                        K.mm(pw[0:TT, 0:512], mergedT[:, k, c0:c0 + TT], w_ob[:, k, cb * 512:(cb + 1) * 512], k == 0, k == 7, [mergedT, w_ob], [pw])
                    K.tt('dve', x1t[:, cb * 512:(cb + 1) * 512], pw[0:TT, 0:512], g1h_bc[0:TT, cb * 512:(cb + 1) * 512], ALU.mult, [pw, g1h_bc], [x1t])
                K.tt('dve', x1t[:, :], x1t[:, :], xtile[:, :], ALU.add, [x1t, xtile], [x1t])
                K.dma('sp', s.x1.h[t0 + c0:t0 + c0 + TT, :], x1t[:, :], R=[x1t], W=[s.x1])
                yield
                K.memset('pool', st[0:TT, 4:5], 0.0, [st])
                K.act(h2b[:, :], x1t[:, :], AF.Square, [x1t], [h2b, st], accum=st[0:TT, 4:5])
                K.ts('dve', st[0:TT, 5:6], st[0:TT, 4:5], 1.0 / D, ALU.mult, [st], [st], s2=EPS, op1=ALU.add)
                rsqrt_inplace(st[0:TT, 5:6], st, TT)
                K.stt('dve', xtile[:, :], x1t[:, :], st[0:TT, 5:6], gmod2_bc[0:TT, :], ALU.mult, ALU.mult, [x1t, st, gmod2_bc], [xtile])
                K.tt('pool', h2b[:, :], xtile[:, :], shift2_bc[0:TT, :], ALU.add, [xtile, shift2_bc], [h2b])
                yield
                ph = prb.next()
                phv = ph[:, :].bitcast(BF16)
                for k in range(8):
                    K.tr(phv[:, k * TT:(k + 1) * TT], h2b[:, k * 128:(k + 1) * 128], ident_b[0:TT, 0:TT], [h2b, ident_b], [ph])
                K.copy('act', h2Tb[:, :, c0:c0 + TT], phv[:, 0:8 * TT].rearrange("p (k t) -> p k t", k=8), [ph], [h2Tb])
            K.dma('sp', s.h2T.h[:, :, t0:t0 + TB], h2Tb[:, :, :], R=[h2Tb], W=[s.h2T])

            yield

        def run_all(g):
            for _ in g:
                pass

        def chain(*gs):
            for g in gs:
                if g is not None:
                    yield from g

        def interleave(ga, gb_, ra=1):
            a_alive, b_alive = True, True
            while a_alive or b_alive:
                if a_alive:
                    for _ in range(ra):
                        try:
                            next(ga)
                        except StopIteration:
                            a_alive = False
                            break
                if b_alive:
                    try:
                        next(gb_)
                    except StopIteration:
                        b_alive = False

        def load_hT(blk_):
            K.dma('sp', hTbs[blk_ % 2][:, :, :], s.hT.h[:, :, blk_ * TB:(blk_ + 1) * TB], R=[s.hT], W=[hTbs[blk_ % 2]])

        load_hT(0)
        run_all(gen_tiles(0))
        for blk in range(nblk):
            if blk + 1 < nblk:
                load_hT(blk + 1)
            others = chain(gen_merge(blk - 1) if blk > 0 else None, gen_tiles(blk + 1) if blk + 1 < nblk else None)
            n_items = 8 * ((blk * TB + TB) // 128 if s.i == 0 else NT)
            interleave(gen_attn(blk) if s.i == 0 else gen_attn_sample(blk), others, ra=max(1, n_items // 28))
        run_all(gen_merge(nblk - 1))

    for s in seqs:
        if STAGE in ('all', 'b') and str(s.i) in SEQS:
            pass_B(s)
    K.barrier()

    AR.reset()
    wf1 = AR.alloc("wf1", [128, 8, 4096], BF16)
    wf2 = AR.alloc("wf2", [128, 32, 1024], BF16)
    wf1q = [Tile(wf1.h, "wf1q%d" % i) for i in range(4)]
    wf2q = [Tile(wf2.h, "wf2q%d" % i) for i in range(4)]
    if STAGE == 'all':
        for q4 in range(4):
            K.dma('pool', wf1[:, :, q4 * 1024:(q4 + 1) * 1024], w_ff1.h[:, q4 * 1024:(q4 + 1) * 1024].rearrange("(k p) n -> p k n", p=128), R=[w_ff1], W=[wf1q[q4]])
        for q4 in range(4):
            K.dma('pool', wf2[:, q4 * 8:(q4 + 1) * 8, :], w_ff2.h[q4 * 1024:(q4 + 1) * 1024, :].rearrange("(j p) n -> p j n", p=128), R=[w_ff2], W=[wf2q[q4]])
    g2_bc = AR.alloc("g2_bc", [128, 1024], F32)
    markC = AR.mark()

    def pass_C(s):
        if s.i > 0:
            K.barrier()
        AR.reset(markC)
        L, TT = s.L, s.TT
        TB = 512 if s.i == 0 else 16
        nT = TB // TT
        nblk = L // TB
        load_mod_bc(g2_bc, s.i, 5)
        h2l = [AR.alloc("h2l%d" % i, [128, 8, TB], BF16) for i in range(2)]
        hid = AR.alloc("hid", [128, 32, TB], BF16)
        rl = [AR.alloc("rl%d" % i, [128, TB], BF16) for i in range(2)]
        x1l = [AR.alloc("x1l%d" % i, [TT, 1024], F32) for i in range(2)]
        yt = [AR.alloc("yt0", [TT, 1024], F32)] * 2
        pf_rot = Rot(banks[0:4])
        pw_rot = Rot(banks[4:8])
        K.dma('sp', h2l[0][:, :, :], s.h2T.h[:, :, 0:TB], R=[s.h2T], W=[h2l[0]])
        for blk in range(nblk):
            t0 = blk * TB
            hb = h2l[blk % 2]
            if blk + 1 < nblk:
                K.dma('sp', h2l[(blk + 1) % 2][:, :, :], s.h2T.h[:, :, t0 + TB:t0 + 2 * TB], R=[s.h2T], W=[h2l[(blk + 1) % 2]])
            for j in range(32):
                pf = pf_rot.next()
                for k in range(8):
                    K.mm(pf[:, 0:TB], wf1[:, k, j * 128:(j + 1) * 128], hb[:, k, :], k == 0, k == 7, [wf1q[j // 8], hb], [pf])
                r_ = rl[j % 2]
                K.act(r_[:, :], pf[:, 0:TB], AF.Relu, [pf], [r_])
                K.tt('pool' if j % 2 == 0 else 'dve', hid[:, j, :], r_[:, :], r_[:, :], ALU.mult, [r_], [hid])
            for ti in range(nT):
                c0 = ti * TT
                xl = x1l[ti % 2]
                yo = yt[ti % 2]
                K.dma('sp', xl[:, :], s.x1.h[t0 + c0:t0 + c0 + TT, :], R=[s.x1], W=[xl])
                for cb in range(2):
                    pw = pw_rot.next()
                    for j in range(32):
                        K.mm(pw[0:TT, 0:512], hid[:, j, c0:c0 + TT], wf2[:, j, cb * 512:(cb + 1) * 512], j == 0, j == 31, [hid, wf2q[j // 8]], [pw])
                    K.tt('dve', yo[:, cb * 512:(cb + 1) * 512], pw[0:TT, 0:512], g2_bc[0:TT, cb * 512:(cb + 1) * 512], ALU.mult, [pw, g2_bc], [yo])
                K.tt('pool', yo[:, :], yo[:, :], xl[:, :], ALU.add, [yo, xl], [yo])
                K.dma('sp', s.y[t0 + c0:t0 + c0 + TT, :], yo[:, :], R=[yo], W=[s.yt], is_output=True)

    for s in seqs:
        if STAGE == 'all' and str(s.i) in SEQS:
            pass_C(s)

    return nc, K, seqs, locals()


def _rope_table():
    inv = (10000.0 ** (-np.arange(0, 32, 2, dtype=np.float32) / np.float32(32))).astype(np.float32)
    pos = np.arange(2064, dtype=np.float32)
    ang = (pos[:, None] * inv[None, :]).astype(np.float32)
    return np.concatenate([np.cos(ang), np.sin(ang)], axis=1).astype(np.float32)


def make_in_maps(inp):
    f = lambda a: np.ascontiguousarray(np.asarray(a, dtype=np.float32))
    shared = {
        "ada_w": f(inp["ada_w"][0]), "ada_b": f(inp["ada_b"]), "norm1_g": f(inp["norm1_g"]),
        "w_in": f(inp["w_in"][0]), "conv_w": f(inp["gdn_conv_w"][0]), "a_log": f(inp["gdn_a_log"]),
        "dt_bias": f(inp["gdn_dt_bias"]), "gdn_norm_g": f(inp["gdn_norm_g"]),
        "w_gdn_out": f(inp["w_gdn_out"][0]), "q_norm_g": f(inp["mla_q_norm_g"]), "w_uq": f(inp["w_uq"][0]),
        "kv_norm_g": f(inp["mla_kv_norm_g"]), "w_ukv": f(inp["w_ukv"][0]), "qh_g": f(inp["q_head_norm_g"]),
        "kh_g": f(inp["k_head_norm_g"]), "w_mla_out": f(inp["w_mla_out"][0]), "w_o": f(inp["w_o"][0]),
        "norm2_g": f(inp["norm2_g"]), "w_ff1": f(inp["w_ff1"][0]), "w_ff2": f(inp["w_ff2"][0]),
        "rope_cs": _rope_table(),
    }
    maps = []
    for c in range(8):
        m = dict(shared)
        m["xp"] = f(inp["x_prompt"][c])
        m["xs"] = f(inp["x_sample"][2 * c:2 * c + 2])
        m["c3"] = f(np.concatenate([np.asarray(inp["c_prompt"])[c:c + 1], np.asarray(inp["c_sample"])[2 * c:2 * c + 2]], axis=0))
        m["ckv_c"] = f(inp["cache_mla_ckv"][0, 2 * c:2 * c + 2])
        m["kr_c"] = f(inp["cache_mla_krope"][0, 2 * c:2 * c + 2])
        m["st_c"] = f(inp["state_gdn"][0, 2 * c:2 * c + 2])
        m["cv_c"] = f(inp["state_gdn_conv"][0, 2 * c:2 * c + 2])
        maps.append(m)
    return maps


def kernel(**inputs):
    nc, K, seqs, _ = build_program(debug=False)
    K.finish()
    maps = make_in_maps(inputs)
    res = run_bass_kernel_spmd(nc, maps, core_ids=list(range(8)))
    R = res.results
    g = lambda n: [np.asarray(R[c][n], dtype=np.float32) for c in range(8)]
    y_prompt = np.stack(g("y_p"), axis=0)
    y_sample = np.concatenate(g("y_s"), axis=0)
    ckv_p = np.stack(g("ckv_p"), axis=0)[None]
    kr_p = np.stack(g("kr_p"), axis=0)[None]
    st_p = np.stack(g("st_p"), axis=0)[None]
    cv_p = np.stack(g("cv_p"), axis=0)[None]
    ckv_s = np.concatenate(g("ckv_s"), axis=0)[None]
    kr_s = np.concatenate(g("kr_s"), axis=0)[None]
    st_s = np.concatenate(g("st_s"), axis=0)[None]
    cv_s = np.concatenate(g("cv_s"), axis=0)[None]
    return (y_prompt, y_sample, ckv_p, kr_p, st_p, cv_p, ckv_s, kr_s, st_s, cv_s)
```
